# Optimizing a Trainium2 kernel written in Bass

```python
import math
import jax, jax.numpy as jnp
from jax import lax
import numpy as np

D_MODEL = 1024
BATCH = 16
SEQ = 4096
DEPTH = 4

D_MIX = D_MODEL
D_HYENA = D_MIX // 2
D_FNET = D_MIX - D_HYENA
FNET_GROUPS = 8
FNET_GROUP_DIM = D_FNET // FNET_GROUPS
HYENA_ORDER = 2
D_IN_PROJ = (HYENA_ORDER + 1) * D_HYENA + D_FNET
SHORT_CONV = 3
FILTER_BANDS = 16
FILTER_EMB = 2 * FILTER_BANDS + 1
FILTER_HIDDEN = 64
N_DIRS = 2
DECAY_TARGET = 1e-2
FAST_DECAY_PCT = 0.3
SLOW_DECAY_PCT = 1.5
D_FF = 2816
FFN_CONV = 3
N_MOD = 6
NORM_EPS = 1e-6
FILTER_EPS = 1e-6

kernel_name = "hyena_fnet_hybrid_encoder"


def _rmsnorm(x, g):
    xf = x.astype(jnp.float32)
    y = xf * lax.rsqrt(jnp.mean(xf * xf, axis=-1, keepdims=True) + NORM_EPS)
    return (y * g.astype(jnp.float32)).astype(x.dtype)


def _centred_dwconv(u, w, b):
    width = w.shape[0]
    half = width // 2
    L = u.shape[1]
    up = jnp.pad(u, ((0, 0), (half, half), (0, 0)))
    y = b
    for k in range(width):
        y = y + up[:, k:k + L] * w[k]
    return y


def _filter_pos_features(L):
    pos = jnp.arange(L, dtype=jnp.float32)
    t = pos / max(L - 1, 1)
    bands = jnp.linspace(1e-4, FILTER_BANDS - 1, FILTER_BANDS, dtype=jnp.float32)
    ang = (2.0 * math.pi / L) * pos[:, None] * bands[None, :]
    z = jnp.concatenate([t[:, None], jnp.cos(ang), -jnp.sin(ang)], axis=-1)
    return z, t


def _hyena_filters_freq(L, w1, b1, w2, b2, freq, w3):
    f32 = jnp.float32
    z, t = _filter_pos_features(L)
    fr = freq.astype(f32)
    h = jnp.sin(fr * (z @ w1.astype(f32) + b1.astype(f32)))
    h = jnp.sin(fr * (h @ w2.astype(f32) + b2.astype(f32)))
    h = (h @ w3.astype(f32)).reshape(L, HYENA_ORDER, N_DIRS, D_HYENA)
    min_decay = math.log(DECAY_TARGET) / SLOW_DECAY_PCT
    max_decay = math.log(DECAY_TARGET) / FAST_DECAY_PCT
    deltas = jnp.linspace(min_decay, max_decay, D_HYENA, dtype=f32)
    decay = jnp.exp(-t[:, None] * jnp.abs(deltas)[None, :])
    h = h * decay[:, None, None, :]
    fwd = h[:, :, 0]
    bwd = h[:, :, 1]
    k = jnp.concatenate([fwd, jnp.zeros((1, HYENA_ORDER, D_HYENA), f32), bwd[:0:-1]], axis=0)
    k = k / (jnp.sum(jnp.abs(k), axis=0, keepdims=True) + FILTER_EPS)
    return jnp.fft.rfft(k, axis=0)


def _bidir_long_conv(u, k_f):
    L = u.shape[1]
    u_f = jnp.fft.rfft(u.astype(jnp.float32), n=2 * L, axis=1)
    y = jnp.fft.irfft(u_f * k_f[None], n=2 * L, axis=1)[:, :L]
    return y.astype(u.dtype)


def _hyena_mixer(p, short_w, short_b, k_f, d_skip):
    u = _centred_dwconv(p, short_w, short_b)
    v = u[..., :D_HYENA]
    gates = [u[..., (n + 1) * D_HYENA:(n + 2) * D_HYENA] for n in range(HYENA_ORDER)]
    z = v
    for n in range(HYENA_ORDER):
        z = gates[n] * (_bidir_long_conv(z, k_f[:, n]) + d_skip[n] * z)
    return z


def _fnet_mixer(p, w_grp):
    B, L, _ = p.shape
    r = p.reshape(B, L, FNET_GROUPS, FNET_GROUP_DIM).astype(jnp.float32)
    r = jnp.fft.fft2(r, axes=(1, 3), norm="ortho").real.astype(p.dtype)
    y = jnp.einsum("blgc,gcd->blgd", r, w_grp)
    return y.reshape(B, L, D_FNET)


def setup_inputs(seed: int = 0) -> dict:
    key = jax.random.key(seed)
    ks = jax.random.split(key, 24)
    f32 = jnp.float32
    nrm = lambda k, shape, s: (jax.random.normal(k, shape, f32) * s)
    return {
        "x": nrm(ks[0], (BATCH, SEQ, D_MODEL), 1.0),
        "c": nrm(ks[1], (BATCH, D_MODEL), 1.0),
        "ada_w": nrm(ks[2], (DEPTH, D_MODEL, N_MOD * D_MODEL), D_MODEL ** -0.5),
        "ada_b": nrm(ks[3], (DEPTH, N_MOD * D_MODEL), 0.02),
        "g_mix_pre": 1.0 + nrm(ks[4], (DEPTH, D_MODEL), 0.05),
        "g_mix_post": 1.0 + nrm(ks[5], (DEPTH, D_MODEL), 0.05),
        "w_in": nrm(ks[6], (DEPTH, D_MODEL, D_IN_PROJ), D_MODEL ** -0.5),
        "short_w": nrm(ks[7], (DEPTH, SHORT_CONV, (HYENA_ORDER + 1) * D_HYENA), SHORT_CONV ** -0.5),
        "short_b": nrm(ks[8], (DEPTH, (HYENA_ORDER + 1) * D_HYENA), 0.02),
        "filt_w1": nrm(ks[9], (DEPTH, FILTER_EMB, FILTER_HIDDEN), FILTER_EMB ** -0.5),
        "filt_b1": nrm(ks[10], (DEPTH, FILTER_HIDDEN), 0.02),
        "filt_w2": nrm(ks[11], (DEPTH, FILTER_HIDDEN, FILTER_HIDDEN), FILTER_HIDDEN ** -0.5),
        "filt_b2": nrm(ks[12], (DEPTH, FILTER_HIDDEN), 0.02),
        "filt_freq": 1.0 + nrm(ks[13], (DEPTH, FILTER_HIDDEN), 0.05),
        "filt_w3": nrm(ks[14], (DEPTH, FILTER_HIDDEN, HYENA_ORDER * N_DIRS * D_HYENA), FILTER_HIDDEN ** -0.5),
        "hyena_d": nrm(ks[15], (DEPTH, HYENA_ORDER, D_HYENA), 0.5),
        "fnet_w": nrm(ks[16], (DEPTH, FNET_GROUPS, FNET_GROUP_DIM, FNET_GROUP_DIM), FNET_GROUP_DIM ** -0.5),
        "w_out": nrm(ks[17], (DEPTH, D_MIX, D_MODEL), D_MIX ** -0.5),
        "g_ffn_pre": 1.0 + nrm(ks[18], (DEPTH, D_MODEL), 0.05),
        "g_ffn_post": 1.0 + nrm(ks[19], (DEPTH, D_MODEL), 0.05),
        "w_up": nrm(ks[20], (DEPTH, D_MODEL, 2 * D_FF), D_MODEL ** -0.5),
        "dw_w": nrm(ks[21], (DEPTH, FFN_CONV, 2 * D_FF), FFN_CONV ** -0.5),
        "dw_b": nrm(ks[22], (DEPTH, 2 * D_FF), 0.02),
        "w_down": nrm(ks[23], (DEPTH, D_FF, D_MODEL), D_FF ** -0.5),
    }


def reference(x, c, ada_w, ada_b, g_mix_pre, g_mix_post, w_in, short_w, short_b,
              filt_w1, filt_b1, filt_w2, filt_b2, filt_freq, filt_w3, hyena_d,
              fnet_w, w_out, g_ffn_pre, g_ffn_post, w_up, dw_w, dw_b, w_down):
    L = x.shape[1]
    c_act = jax.nn.silu(c.astype(jnp.float32))
    for l in range(DEPTH):
        mod = (c_act @ ada_w[l].astype(jnp.float32) + ada_b[l].astype(jnp.float32)).astype(x.dtype)
        shift_m, scale_m, gate_m, shift_f, scale_f, gate_f = jnp.split(mod[:, None, :], N_MOD, axis=-1)

        h = _rmsnorm(x, g_mix_pre[l]) * (1.0 + scale_m) + shift_m
        p = jnp.einsum("bld,de->ble", h, w_in[l])
        k_f = _hyena_filters_freq(L, filt_w1[l], filt_b1[l], filt_w2[l], filt_b2[l],
                                  filt_freq[l], filt_w3[l])
        y_hyena = _hyena_mixer(p[..., :(HYENA_ORDER + 1) * D_HYENA], short_w[l], short_b[l],
                               k_f, hyena_d[l])
        y_fnet = _fnet_mixer(p[..., (HYENA_ORDER + 1) * D_HYENA:], fnet_w[l])
        y = jnp.einsum("ble,ed->bld", jnp.concatenate([y_hyena, y_fnet], axis=-1), w_out[l])
        x = x + gate_m * _rmsnorm(y, g_mix_post[l])

        h = _rmsnorm(x, g_ffn_pre[l]) * (1.0 + scale_f) + shift_f
        u = _centred_dwconv(jnp.einsum("bld,df->blf", h, w_up[l]), dw_w[l], dw_b[l])
        a, b = jnp.split(u, 2, axis=-1)
        y = jnp.einsum("blf,fd->bld", jax.nn.gelu(a, approximate=False) * b, w_down[l])
        x = x + gate_f * _rmsnorm(y, g_ffn_post[l])
    return x
```

```python
import numpy as np
import ml_dtypes
import concourse.bass as bass
import concourse.mybir as mybir
from concourse.bass_utils import run_bass_kernel_spmd

F32 = mybir.dt.float32
BF16 = mybir.dt.bfloat16
I32 = mybir.dt.int32
ALU = mybir.AluOpType
AF = mybir.ActivationFunctionType

D = 1024
T = 4096
DEPTH = 4
NSEQ = 2
DH = 512
DFF = 2816
NFF = 22
TT = 512
NT = T // TT
EPS = 1e-6
N2 = 8192


def _hy_consts():
    a = np.arange(32)[:, None]; flo = np.arange(64)[None, :]
    phi = 2 * np.pi * a * flo / 64.0
    W1 = np.concatenate([np.cos(phi), -np.sin(phi)], axis=1)
    W2 = np.concatenate([np.cos(phi).T, -np.sin(phi).T], axis=0)
    i = np.arange(128)[:, None]; fhi = np.arange(64)[None, :]
    M2 = np.zeros((64, 2, 2, 128, 128))
    MAB = np.zeros((64, 2, 2, 128, 128))
    G1 = np.zeros((64, 2, 128, 128))
    sgn = (-1.0) ** np.arange(128)
    for fl in range(64):
        f = fl + 64 * fhi
        th = 2 * np.pi * i * f / N2
        c, s = np.cos(th), np.sin(th)
        m_re = np.concatenate([c, -s], axis=1)
        m_im = np.concatenate([s, c], axis=1)
        ma_re = np.concatenate([c, c], axis=1); ma_im = np.concatenate([s, s], axis=1)
        mb_re = np.concatenate([s, -s], axis=1); mb_im = np.concatenate([-c, c], axis=1)
        if fl == 0:
            m_re[:, 64] = sgn; m_im[:, 64] = 0.0
            ma_re[:, 64] = sgn; ma_im[:, 64] = 0.0
            mb_re[:, 64] = 0; mb_im[:, 64] = 0; mb_re[:, 0] = 0; mb_im[:, 0] = 0
        M2[fl, 0, 0] = m_re; M2[fl, 1, 0] = m_im
        M2[fl, 0, 1] = np.roll(m_re, -64, axis=1); M2[fl, 1, 1] = np.roll(m_im, -64, axis=1)
        MAB[fl, 0, 0] = ma_re; MAB[fl, 0, 1] = ma_im; MAB[fl, 1, 0] = mb_re; MAB[fl, 1, 1] = mb_im
        wgt = np.where(f == 0, 1.0, 2.0) / N2
        g_re = np.concatenate([(wgt * c).T, (-wgt * s).T], axis=0)
        g_im = np.concatenate([(wgt * s).T, (wgt * c).T], axis=0)
        if fl == 0:
            g_re[64, :] = sgn / N2; g_im[64, :] = 0.0
        G1[fl, 0] = g_re; G1[fl, 1] = g_im
    bf = ml_dtypes.bfloat16
    M2s = M2.reshape(8, 8, 2, 2, 128, 128).transpose(0, 4, 1, 2, 3, 5).reshape(8, 128, 8 * 4 * 128)
    MABs = MAB.reshape(8, 8, 2, 2, 128, 128).transpose(0, 4, 1, 2, 3, 5).reshape(8, 128, 8 * 4 * 128)
    G1s = G1.reshape(8, 8, 2, 128, 128).transpose(0, 3, 1, 2, 4).reshape(8, 128, 8 * 2 * 128)
    return dict(W1=W1.astype(bf), W2=W2.astype(bf), M2s=np.ascontiguousarray(M2s).astype(bf),
                MABs=np.ascontiguousarray(MABs).astype(bf), G1s=np.ascontiguousarray(G1s).astype(bf))


def _fn_consts():
    a = np.arange(32)[:, None]; flo = np.arange(32)[None, :]
    phi = 2 * np.pi * a * flo / 32.0
    c, s = np.cos(phi), np.sin(phi)
    W1f = np.concatenate([np.concatenate([c, -s], axis=1), np.concatenate([-s, -c], axis=1)], axis=0)
    i = np.arange(128)[:, None]; fhi = np.arange(128)[None, :]
    M2f = np.zeros((32, 2, 128, 128))
    for fl in range(32):
        f = fl + 32 * fhi
        th = 2 * np.pi * i * f / T
        M2f[fl, 0] = np.cos(th) / 512.0; M2f[fl, 1] = np.sin(th) / 512.0
    bf = ml_dtypes.bfloat16
    M2fs = M2f.transpose(2, 0, 1, 3).reshape(128, 32 * 2 * 128)
    j = np.arange(64)
    cc = np.cos(2 * np.pi * np.outer(j, j) / 64.0); sc = np.sin(2 * np.pi * np.outer(j, j) / 64.0)
    Cb = np.zeros((128, 128)); Sb = np.zeros((128, 128))
    Cb[:64, :64] = cc; Cb[64:, 64:] = cc; Sb[:64, :64] = sc; Sb[64:, 64:] = sc
    return dict(W1f=W1f.astype(bf), M2fs=np.ascontiguousarray(M2fs).astype(bf),
                CSb=np.concatenate([Cb, Sb], axis=1).astype(np.float32))


def _filter_consts():
    pos = np.arange(T, dtype=np.float32)
    t = (pos / np.float32(T - 1)).astype(np.float32)
    bands = np.linspace(1e-4, 15, 16, dtype=np.float32)
    ang = (np.float32(2.0 * np.pi / T) * pos[:, None] * bands[None, :]).astype(np.float32)
    z = np.concatenate([t[:, None], np.cos(ang), -np.sin(ang)], axis=-1).astype(np.float32)
    min_decay = np.log(1e-2) / 1.5; max_decay = np.log(1e-2) / 0.3
    deltas = np.linspace(min_decay, max_decay, DH, dtype=np.float32)
    dsc = (-np.abs(deltas)).astype(np.float32).reshape(4, 128).T
    tb = np.broadcast_to(t[None, :], (128, T)).astype(np.float32)
    return dict(zT=np.ascontiguousarray(z.T), dsc=np.ascontiguousarray(dsc), tb=np.ascontiguousarray(tb))


class Trk:
    __slots__ = ("w", "r", "name")

    def __init__(self, name=""):
        self.w = {}
        self.r = {}
        self.name = name


class Eng:
    def __init__(self, fw, name, eng, sem):
        self.fw = fw; self.name = name; self.e = eng; self.sem = sem
        self.cnt = 0
        self.seen = {}

    def wait(self, sem, val):
        k = id(sem)
        if self.seen.get(k, 0) >= val:
            return
        if sem is self.sem and self.name == "pe":
            return
        self.e.wait_ge(sem, val)
        self.seen[k] = val


class FW:
    def __init__(self, nc, sems):
        self.nc = nc
        self._sems = list(sems)
        self.pe = Eng(self, "pe", nc.tensor, self._sems.pop())
        self.act = Eng(self, "act", nc.scalar, self._sems.pop())
        self.dve = Eng(self, "dve", nc.vector, self._sems.pop())
        self.pool = Eng(self, "pool", nc.gpsimd, self._sems.pop())
        self.sp = Eng(self, "sp", nc.sync, self._sems.pop())
        self.engs = [self.pe, self.act, self.dve, self.pool, self.sp]
        self.dma_sems = {}
        self.all_dma = []
        self.rr = 0
        self.npe = 0
        self.pending = []
        self.loads_since = False

    def _deps(self, eng, reads, writes):
        for t in reads:
            for (sem, val) in t.w.values():
                eng.wait(sem, val)
        for t in writes:
            for (sem, val) in t.w.values():
                if sem is not eng.sem:
                    eng.wait(sem, val)
            for (sem, val) in t.r.values():
                if sem is not eng.sem:
                    eng.wait(sem, val)

    def _conflict(self, reads, writes):
        for p in self.pending:
            pr, pw = p[3], p[4]
            for t in writes:
                if any(t is x for x in pr) or any(t is x for x in pw):
                    return True
            for t in reads:
                if any(t is x for x in pw):
                    return True
        return False

    def order(self, eng, trk):
        for (sem, val) in list(trk.w.values()) + list(trk.r.values()):
            if sem is not eng.sem:
                eng.wait(sem, val)

    def flush(self):
        pend, self.pending = self.pending, []
        for (q, out, in_, rd, wr, sem_on, acc, kw) in pend:
            self.dma(q, out, in_, reads=rd, writes=wr, sem_on=sem_on, accumulate_w=acc, _is_flush=True, **kw)
        self.loads_since = False

    def op(self, eng, fn, reads=(), writes=(), signal=True):
        if self.pending and (self.loads_since or self._conflict(reads, writes)):
            self.flush()
        self._deps(eng, reads, writes)
        ins = fn()
        if eng is self.pe:
            self.npe += 1
        if signal:
            eng.cnt += 1
            ins.then_inc(eng.sem, 1)
            mark = (eng.sem, eng.cnt)
        else:
            mark = (eng.sem, eng.cnt + 1)
        k = id(eng.sem)
        for t in writes:
            t.w = {k: mark}; t.r = {}
        for t in reads:
            t.r[k] = mark
        return ins

    def _dsem(self, trk):
        ent = self.dma_sems.get(id(trk))
        if ent is None:
            ent = [self._sems.pop(), 0, trk]
            self.dma_sems[id(trk)] = ent
            self.all_dma.append(ent)
        return ent

    def dma(self, q, out, in_, reads=(), writes=(), sem_on=None, accumulate_w=False, defer=False, _is_flush=False, **kw):
        rd = list(reads); wr = list(writes)
        if defer:
            self.pending.append((q, out, in_, rd, wr, sem_on, accumulate_w, kw))
            self.loads_since = False
            return None
        if not _is_flush:
            if self.pending and self._conflict(rd, wr):
                self.flush()
            self.loads_since = True
        if accumulate_w:
            for t in rd:
                for (sem, val) in t.w.values():
                    q.wait(sem, val)
            eng_ids = {id(e.sem) for e in self.engs}
            for t in wr:
                for (sem, val) in t.r.values():
                    q.wait(sem, val)
                for (sem, val) in t.w.values():
                    if id(sem) in eng_ids:
                        q.wait(sem, val)
        else:
            self._deps(q, rd, wr)
        ent = self._dsem(sem_on)
        ent[1] += 16
        ins = q.e.dma_start(out=out, in_=in_, **kw)
        ins.then_inc(ent[0], 16)
        mark = (ent[0], ent[1])
        k = id(ent[0])
        for t in wr:
            if accumulate_w:
                t.w[k] = mark
            else:
                t.w = {k: mark}; t.r = {}
        for t in rd:
            t.r[k] = mark
        return ins

    def barrier(self):
        self.flush()
        sp = self.sp
        for e in self.engs:
            if e is not sp and e.cnt > 0:
                sp.wait(e.sem, e.cnt)
        for ent in self.all_dma:
            if ent[1] > 0:
                sp.wait(ent[0], ent[1])
        sp.cnt += 1
        sp.e.sem_inc(sp.sem, 1)
        for e in self.engs:
            if e is not sp:
                e.wait(sp.sem, sp.cnt)
        for e in self.engs:
            for o in self.engs:
                e.seen[id(o.sem)] = max(e.seen.get(id(o.sem), 0), o.cnt if o is not sp else sp.cnt)
            for ent in self.all_dma:
                e.seen[id(ent[0])] = max(e.seen.get(id(ent[0]), 0), ent[1])

    def finish(self):
        self.flush()
        sp = self.sp
        for e in self.engs:
            if e is not sp and e.cnt > 0:
                sp.wait(e.sem, e.cnt)
        for ent in self.all_dma:
            if ent[1] > 0:
                sp.wait(ent[0], ent[1])

    def recycle_dma(self):
        self.free_dma = getattr(self, "free_dma", []) + [e for e in self.all_dma if e[2] is not None]
        for e in self.free_dma:
            e[2] = None
        self.dma_sems = {}


def _dsem_recycling(self, trk):
    ent = self.dma_sems.get(id(trk))
    if ent is None:
        fl = getattr(self, "free_dma", [])
        if fl:
            ent = fl.pop()
            ent[2] = trk
        else:
            ent = [self._sems.pop(), 0, trk]
            self.all_dma.append(ent)
        self.dma_sems[id(trk)] = ent
    return ent


FW._dsem = _dsem_recycling


WEIGHT_NAMES = ["ada_w", "ada_b", "g_mix_pre", "g_mix_post", "w_in", "short_w", "short_b", "filt_w1", "filt_b1",
                "filt_w2", "filt_b2", "filt_freq", "filt_w3", "hyena_d", "fnet_w", "w_out", "g_ffn_pre",
                "g_ffn_post", "w_up", "dw_w", "dw_b", "w_down"]
WEIGHT_SHAPES = {
    "ada_w": (DEPTH, D, 6 * D), "ada_b": (DEPTH, 6 * D), "g_mix_pre": (DEPTH, D), "g_mix_post": (DEPTH, D),
    "w_in": (DEPTH, D, 2048), "short_w": (DEPTH, 3, 1536), "short_b": (DEPTH, 1536),
    "filt_w1": (DEPTH, 33, 64), "filt_b1": (DEPTH, 64), "filt_w2": (DEPTH, 64, 64), "filt_b2": (DEPTH, 64),
    "filt_freq": (DEPTH, 64), "filt_w3": (DEPTH, 64, 2048), "hyena_d": (DEPTH, 2, DH),
    "fnet_w": (DEPTH, 8, 64, 64), "w_out": (DEPTH, D, D), "g_ffn_pre": (DEPTH, D), "g_ffn_post": (DEPTH, D),
    "w_up": (DEPTH, D, 2 * DFF), "dw_w": (DEPTH, 3, 2 * DFF), "dw_b": (DEPTH, 2 * DFF), "w_down": (DEPTH, DFF, D),
}
CONST_SHAPES = {
    "identf": ((128, 128), F32), "W1": ((32, 128), BF16), "W2": ((128, 32), BF16),
    "M2s": ((8, 128, 4096), BF16), "MABs": ((8, 128, 4096), BF16), "G1s": ((8, 128, 2048), BF16),
    "W1f": ((64, 64), BF16), "M2fs": ((128, 8192), BF16), "CSb": ((128, 256), F32),
    "zT": ((33, T), F32), "dsc": ((128, 4), F32), "tb": ((128, T), F32),
}


def make_consts():
    c = {}
    c.update(_hy_consts()); c.update(_fn_consts()); c.update(_filter_consts())
    c["identf"] = np.eye(128, dtype=np.float32)
    return c


class Builder:
    def __init__(self, n_layers=DEPTH, n_seq=NSEQ, phases=None, dbg=()):
        self.n_layers = n_layers; self.n_seq = n_seq
        self.phases = phases
        self.dbg = set(dbg)
        self.nc = bass.Bass("TRN2", target_bir_lowering=False)
        self.I = {}
        nc = self.nc
        self.I["x"] = nc.dram_tensor("x", [NSEQ, T, D], F32, kind="ExternalInput").ap()
        self.I["c"] = nc.dram_tensor("c", [NSEQ, D], F32, kind="ExternalInput").ap()
        for n in WEIGHT_NAMES:
            self.I[n] = nc.dram_tensor(n, list(WEIGHT_SHAPES[n]), F32, kind="ExternalInput").ap()
        for n, (shp, dt) in CONST_SHAPES.items():
            self.I[n] = nc.dram_tensor(n, list(shp), dt, kind="ExternalInput").ap()
        self.out = nc.dram_tensor("out", [NSEQ, T, D], F32, kind="ExternalOutput").ap()
        self.S = {}
        self.ST = {}

    def scratch(self, name, shape, dt):
        kind = "ExternalOutput" if name in self.dbg else "Internal"
        self.S[name] = self.nc.dram_tensor(name, list(shape), dt, kind=kind).ap()
        return self.S[name]

    def uniq(self, name):
        self._uid = getattr(self, "_uid", 0) + 1
        return f"{name}_u{self._uid}"

    def dump(self, name, ap, trk, dt=F32):
        if name not in self.dbg or name in self.S:
            return
        shp = list(ap.shape)
        d = self.nc.dram_tensor(name, shp, dt, kind="ExternalOutput").ap()
        self.S[name] = d
        self.fw.dma(self.fw.sp, d, ap, reads=[trk], sem_on=trk)

    def on(self, ph):
        return self.phases is None or ph in self.phases

    def build(self):
        import contextlib
        nc = self.nc
        with contextlib.ExitStack() as es:
            sems = [es.enter_context(nc.semaphore(f"s{i}")) for i in range(100)]
            fw = self.fw = FW(nc, sems)
            self.es = es
            self.scratch("xres", [NSEQ, D, T], F32)
            self.scratch("hT", [NSEQ, D, T], BF16)
            self.scratch("uv", [12, 128, T], BF16)
            self.scratch("ab", [8, 128, T], BF16)
            self.scratch("z2", [4, 128, T], BF16)
            self.scratch("mixT", [D, T], BF16)
            self.scratch("gT", [DFF, T], BF16)
            self.scratch("sd", [2, 4, 2, 128, T], BF16)
            self.scratch("kab", [2, 4, 2, 128, 64 * 128], BF16)
            for n in self.S:
                self.ST[n] = Trk(n)
            self.xrt = [[Trk(f"xres{s_}_{t_}") for t_ in range(NT)] for s_ in range(NSEQ)]
            self.ps = [es.enter_context(nc.psum_tensor(f"ps{b}", [128, 512], F32)) for b in range(8)]
            self.pst = [Trk(f"ps{b}") for b in range(8)]
            self.psi = 0
            sb = lambda name, shape, dt: es.enter_context(nc.sbuf_tensor(self.uniq(name), list(shape), dt))
            self.identf = sb("identf", [128, 128], F32); self.identb = sb("identb", [128, 128], BF16)
            self.onesb = sb("onesb", [128, 128], BF16)
            self.W1 = sb("W1sb", [32, 128], BF16); self.W2 = sb("W2sb", [128, 32], BF16)
            self.W1f = sb("W1fsb", [64, 64], BF16); self.M2f = sb("M2fsb", [128, 8192], BF16)
            self.CSb = sb("CSbsb", [128, 256], F32); self.dsc = sb("dscsb", [128, 4], F32)
            self.V = sb("V", [128, 1280], F32)
            self.modT = sb("modT", [128, DEPTH, 48, NSEQ], F32)
            self.DER = sb("DER", [128, DEPTH, NSEQ, 4, 8], F32)
            self.cT = sb("cT", [128, 8, NSEQ], F32)
            self.ctrk = Trk("consts")
            self.rr = 0
            self.hf = None
            self.marks = []
            self.setup()
            fw.barrier(); fw.recycle_dma()
            for s in range(self.n_seq):
                if self.on("x0"):
                    self.phase_x0(s)
                    fw.barrier(); fw.recycle_dma()
            for l in range(self.n_layers):
                if self.on("filt"):
                    self.marks.append((f"filt_l{l}", fw.npe))
                    self.phase_filter(l)
                    fw.barrier(); fw.recycle_dma()
                for s in range(self.n_seq):
                    for ph in ("p1", "conv", "fnet", "p45", "p6"):
                        if ph == "p45":
                            if self.on("p4") and self.on("p5"):
                                with nc.sbuf_tensor(self.uniq("hfull_r"), [128, 8, T], BF16) as hf:
                                    self.hf = (hf, [Trk(f"hfr{i}") for i in range(8)])
                                    self.marks.append((f"p4_l{l}s{s}", fw.npe))
                                    self.phase_p4(l, s)
                                    fw.barrier(); fw.recycle_dma()
                                    self.marks.append((f"p5_l{l}s{s}", fw.npe))
                                    self.phase_p5(l, s)
                                    fw.barrier(); fw.recycle_dma()
                                    self.hf = None
                            continue
                        if self.on(ph):
                            self.marks.append((f"{ph}_l{l}s{s}", fw.npe))
                            getattr(self, "phase_" + ph)(l, s)
                            fw.barrier(); fw.recycle_dma()
            fw.finish()
        return nc

    def next_ps(self):
        b = self.psi; self.psi = (self.psi + 1) % 8
        return self.ps[b], self.pst[b]

    def rot(self, engs):
        self.rr += 1
        return engs[self.rr % len(engs)]

    def copy(self, eng, out, in_, reads, writes):
        fw = self.fw
        if eng is fw.act:
            return fw.op(eng, lambda: eng.e.copy(out=out, in_=in_), reads, writes)
        return fw.op(eng, lambda: eng.e.tensor_copy(out=out, in_=in_), reads, writes)

    def setup(self):
        nc, fw, I = self.nc, self.fw, self.I
        sp = fw.sp
        ct = self.ctrk
        for dst, src in ((self.identf, "identf"), (self.W1, "W1"), (self.W2, "W2"), (self.W1f, "W1f"),
                         (self.M2f, "M2fs"), (self.CSb, "CSb"), (self.dsc, "dsc")):
            fw.dma(sp, dst[:], I[src], writes=[ct], sem_on=ct, accumulate_w=True)
        fw.op(fw.dve, lambda: nc.vector.tensor_copy(out=self.identb[:], in_=self.identf[:]), [ct], [ct])
        fw.op(fw.dve, lambda: nc.vector.memset(self.onesb[:], 1.0 / D), [], [ct])
        self.col = {}
        rows = []
        def reg(name, l, ap2d, n, width=128):
            self.col[(name, l)] = len(rows)
            for k in range(n):
                rows.append((ap2d, k, width))
        for l in range(self.n_layers):
            for nm in ("g_mix_pre", "g_mix_post", "g_ffn_pre", "g_ffn_post"):
                reg(nm, l, I[nm][l].rearrange("(c p) -> c p", p=128), 8)
            reg("ada_b", l, I["ada_b"][l].rearrange("(c p) -> c p", p=128), 48)
            reg("short_w", l, I["short_w"][l].rearrange("k (q p) -> (k q) p", p=128), 36)
            reg("short_b", l, I["short_b"][l].rearrange("(c p) -> c p", p=128), 12)
            reg("dw_w", l, I["dw_w"][l].rearrange("k (q p) -> (k q) p", p=128), 132)
            reg("dw_b", l, I["dw_b"][l].rearrange("(c p) -> c p", p=128), 44)
            reg("hyena_d", l, I["hyena_d"][l].rearrange("n (c p) -> (n c) p", p=128), 8)
            for nm in ("filt_b1", "filt_b2", "filt_freq"):
                reg(nm, l, I[nm][l].rearrange("(o p) -> o p", p=64), 1, 64)
        nrows = len(rows)
        ngrp = (nrows + 127) // 128
        assert ngrp * 128 <= 1280
        with nc.sbuf_tensor("stg_v", [128, ngrp, 128], F32) as stg, nc.sbuf_tensor("ctile_v", [NSEQ, D], F32) as ctile:
            st = Trk("stg")
            fw.op(fw.dve, lambda: nc.vector.memset(stg[:], 0.0), [], [st])
            r = 0
            while r < nrows:
                ap2d, k0, width = rows[r]
                n = 1
                while (r + n < nrows and rows[r + n][0] is ap2d and rows[r + n][1] == k0 + n
                       and (r + n) % 128 != 0):
                    n += 1
                fw.dma(sp, stg[r % 128:r % 128 + n, r // 128, 0:width], ap2d[k0:k0 + n, :],
                       writes=[st], sem_on=st, accumulate_w=True)
                r += n
            vt = Trk("V")
            for g in range(ngrp):
                ps, pt = self.next_ps()
                fw.op(fw.pe, lambda: nc.tensor.transpose(ps[:, 0:128], stg[:, g, :], self.identf[:]), [st, ct], [pt])
                self.copy(fw.dve, self.V[:, g * 128:(g + 1) * 128], ps[:, 0:128], [pt], [vt])
            self.vt = vt
            ctt = Trk("ctile")
            fw.dma(sp, ctile[:], I["c"], writes=[ctt], sem_on=ctt)
            ps, pt = self.next_ps()
            for kc in range(8):
                fw.op(fw.pe, lambda: nc.tensor.transpose(ps[:, kc * NSEQ:(kc + 1) * NSEQ],
                                                         ctile[0:NSEQ, kc * 128:(kc + 1) * 128],
                                                         self.identf[0:NSEQ, 0:NSEQ]), [ctt, ct], [pt], signal=(kc == 7))
            cact = Trk("cT")
            fw.op(fw.act, lambda: nc.scalar.activation(out=self.cT[:].rearrange("p k s -> p (k s)"),
                                                       in_=ps[:, 0:8 * NSEQ], func=AF.Silu), [pt], [cact])
            with nc.sbuf_tensor("aw0", [128, 8, 512], F32) as aw0, nc.sbuf_tensor("aw1", [128, 8, 512], F32) as aw1:
                aws = [(aw0, Trk("aw0")), (aw1, Trk("aw1"))]
                mt = Trk("modT")
                self.mt = mt
                k = 0
                for l in range(self.n_layers):
                    psm, ptm = self.next_ps()
                    for j in range(12):
                        aw, awt = aws[k % 2]; k += 1
                        fw.dma(sp, aw[:], I["ada_w"][l][:, j * 512:(j + 1) * 512].rearrange("(kc p) n -> p kc n", p=128),
                               writes=[awt], sem_on=awt)
                        for sub in range(4):
                            ch = j * 4 + sub
                            for kc in range(8):
                                fw.op(fw.pe, lambda: nc.tensor.matmul(psm[:, ch * NSEQ:(ch + 1) * NSEQ],
                                                                      lhsT=aw[:, kc, sub * 128:(sub + 1) * 128],
                                                                      rhs=self.cT[:, kc, :], start=(kc == 0), stop=(kc == 7)),
                                      [awt, cact], [ptm], signal=(kc == 7))
                    ab0 = self.col[("ada_b", l)]
                    for s in range(NSEQ):
                        fw.op(fw.dve, lambda: nc.vector.tensor_tensor(
                            out=self.modT[:, l, :, s], in0=psm[:, 0:48 * NSEQ].rearrange("p (c s) -> p c s", s=NSEQ)[:, :, s],
                            in1=self.V[:, ab0:ab0 + 48], op=ALU.add), [ptm, vt], [mt])
                    for s in range(NSEQ):
                        for w, (gname, sc_ch, ga_ch) in enumerate((("g_mix_pre", 8, None), ("g_mix_post", None, 16),
                                                                   ("g_ffn_pre", 32, None), ("g_ffn_post", None, 40))):
                            gc = self.col[(gname, l)]
                            if sc_ch is not None:
                                fw.op(fw.dve, lambda: nc.vector.scalar_tensor_tensor(
                                    out=self.DER[:, l, s, w, :], in0=self.modT[:, l, sc_ch:sc_ch + 8, s], scalar=1.0,
                                    in1=self.V[:, gc:gc + 8], op0=ALU.add, op1=ALU.mult), [mt, vt], [mt])
                            else:
                                fw.op(fw.dve, lambda: nc.vector.tensor_tensor(
                                    out=self.DER[:, l, s, w, :], in0=self.modT[:, l, ga_ch:ga_ch + 8, s],
                                    in1=self.V[:, gc:gc + 8], op=ALU.mult), [mt, vt], [mt])
                fw.barrier()

    def gm(self, l, s, which, kc):
        return self.DER[:, l, s, 2 * which, kc:kc + 1]

    def gg(self, l, s, which, kc):
        return self.DER[:, l, s, 2 * which + 1, kc:kc + 1]

    def sh(self, l, s, which, kc):
        return self.modT[:, l, (0 if which == 0 else 24) + kc, s:s + 1]

    def alloc_norm_tiles(self, es, with_tmp=True, with_ho=True):
        nc = self.nc
        sb = lambda name, shape, dt: es.enter_context(nc.sbuf_tensor(self.uniq(name), list(shape), dt))
        n = {}
        n["sq"] = (sb("n_sq", [128, 8, TT], BF16), Trk("sq"))
        n["rs"] = (sb("n_rs", [128, TT], F32), Trk("rs"))
        if with_tmp:
            n["tmp"] = (sb("n_tmp", [128, 8, TT], F32), [Trk(f"tmp{i}") for i in range(8)])
        if with_ho:
            n["ho"] = (sb("n_ho", [128, 8, TT], BF16), Trk("ho"))
        else:
            n["ho"] = (None, None)
        return n

    def sq_of(self, n, src, srct):
        nc, fw = self.nc, self.fw
        sq, sqt = n["sq"]
        fw.op(fw.act, lambda: nc.scalar.activation(out=sq[:].rearrange("p k t -> p (k t)"),
                                                   in_=src.rearrange("p k t -> p (k t)"), func=AF.Square),
              list(srct) if isinstance(srct, (list, tuple)) else [srct], [sqt])

    def rs_from_sq(self, n):
        nc, fw = self.nc, self.fw
        sq, sqt = n["sq"]; rs, rst = n["rs"]
        ps, pt = self.next_ps()
        for kc in range(8):
            fw.op(fw.pe, lambda: nc.tensor.matmul(ps[:, :], lhsT=self.onesb[:], rhs=sq[:, kc, :],
                                                  start=(kc == 0), stop=(kc == 7)), [sqt, self.ctrk], [pt], signal=(kc == 7))
        fw.op(fw.act, lambda: nc.scalar.activation(out=rs[:], in_=ps[:, :], func=AF.Sqrt, bias=EPS, scale=1.0),
              [pt], [rst])
        fw.op(fw.dve, lambda: nc.vector.reciprocal(out=rs[:], in_=rs[:]), [rst], [rst])

    def rstd(self, n, src, srct):
        self.sq_of(n, src, srct)
        self.rs_from_sq(n)

    def prenorm(self, n, src, srct, l, s, which, t0):
        nc, fw = self.nc, self.fw
        self.rstd(n, src, srct)
        self.prenorm_tail(n, src, srct, l, s, which, t0)

    def prenorm_tail(self, n, src, srct, l, s, which, t0):
        nc, fw = self.nc, self.fw
        rs, rst = n["rs"]; tmp, tmpt = n["tmp"]; ho, hot = n["ho"]
        for kc in range(8):
            eng = fw.dve
            fw.op(eng, lambda: eng.e.tensor_tensor(out=tmp[:, kc, :], in0=src[:, kc, :], in1=rs[:], op=ALU.mult),
                  [srct, rst], [tmpt[kc]])
        if self.hf is not None and which == 1:
            hf, hfts = self.hf
            for kc in range(8):
                fw.op(fw.act, lambda: nc.scalar.activation(out=hf[:, kc, t0:t0 + TT], in_=tmp[:, kc, :], func=AF.Identity,
                                                           bias=self.sh(l, s, which, kc), scale=self.gm(l, s, which, kc)),
                      [tmpt[kc], self.mt], [hfts[kc]])
            return
        for kc in range(8):
            fw.op(fw.act, lambda: nc.scalar.activation(out=ho[:, kc, :], in_=tmp[:, kc, :], func=AF.Identity,
                                                       bias=self.sh(l, s, which, kc), scale=self.gm(l, s, which, kc)),
                  [tmpt[kc], self.mt], [hot])
        self.dump("d_tmp", tmp[:, 0, :], tmpt[0])
        self.dump("d_mod", self.modT[:, 0, :, :].rearrange("p a b -> p (a b)"), self.mt)
        self.dump("d_der", self.DER[:, 0, 0, :, :].rearrange("p a b -> p (a b)"), self.mt)
        self.dump("d_V", self.V[:, :], self.vt)
        fw.dma(fw.sp, self.S["hT"][s].rearrange("(kc p) t -> p kc t", p=128)[:, :, t0:t0 + TT], ho[:],
               reads=[hot], writes=[self.ST["hT"]], sem_on=hot, accumulate_w=True, defer=True)

    def epi_b(self, n, yt, ytt, xt, xtt, xn, xnt, l, s, which, t0, last):
        nc, fw = self.nc, self.fw
        self.rs_from_sq(n)
        rs, rst = n["rs"]
        for oc in range(8):
            fw.op(fw.dve, lambda: nc.vector.scalar_tensor_tensor(out=xn[:, oc, :], in0=yt[:, oc, :],
                                                                 scalar=self.gg(l, s, which, oc), in1=rs[:],
                                                                 op0=ALU.mult, op1=ALU.mult), [ytt[oc], rst, self.mt], [xnt])
            fw.op(fw.dve, lambda: nc.vector.tensor_tensor(out=xn[:, oc, :], in0=xn[:, oc, :], in1=xt[:, oc, :],
                                                          op=ALU.add), [xnt, xtt], [xnt])
        if not last:
            fw.dma(fw.sp, self.S["xres"][s].rearrange("(kc p) t -> p kc t", p=128)[:, :, t0:t0 + TT], xn[:],
                   reads=[xnt], writes=[self.xrt[s][t0 // TT]], sem_on=xnt, accumulate_w=True, defer=True)
            self.sq_of(n, xn[:], xnt)

    def epi_c(self, n, xn, xnt, l, s, which, t0, last):
        nc, fw = self.nc, self.fw
        if not last:
            self.rs_from_sq(n)
            if which == 0:
                self.prenorm_tail(n, xn[:], xnt, l, s, 1, t0)
            else:
                self.prenorm_tail(n, xn[:], xnt, l + 1, s, 0, t0)
        else:
            self.final_out(n, xn, xnt, s, t0)

    def epilogue(self, n, yt, ytt, xt, xtt, xn, xnt, l, s, which, t0, last):
        nc, fw = self.nc, self.fw
        self.rstd(n, yt[:], ytt)
        rs, rst = n["rs"]
        for oc in range(8):
            fw.op(fw.dve, lambda: nc.vector.scalar_tensor_tensor(out=xn[:, oc, :], in0=yt[:, oc, :],
                                                                 scalar=self.gg(l, s, which, oc), in1=rs[:],
                                                                 op0=ALU.mult, op1=ALU.mult), [ytt[oc], rst, self.mt], [xnt])
            fw.op(fw.dve, lambda: nc.vector.tensor_tensor(out=xn[:, oc, :], in0=xn[:, oc, :], in1=xt[:, oc, :],
                                                          op=ALU.add), [xnt, xtt], [xnt])
        if not last:
            fw.dma(fw.sp, self.S["xres"][s].rearrange("(kc p) t -> p kc t", p=128)[:, :, t0:t0 + TT], xn[:],
                   reads=[xnt], writes=[self.xrt[s][t0 // TT]], sem_on=xnt, accumulate_w=True, defer=True)
            if which == 0:
                self.prenorm(n, xn[:], xnt, l, s, 1, t0)
            else:
                self.prenorm(n, xn[:], xnt, l + 1, s, 0, t0)
        else:
            self.final_out(n, xn, xnt, s, t0)

    def final_out(self, n, xn, xnt, s, t0):
        nc, fw = self.nc, self.fw
        xo, xot = n["xo"]
        for sub in range(4):
            for half in range(2):
                ps, pt = self.next_ps()
                for k in range(4):
                    kc = half * 4 + k
                    fw.op(fw.pe, lambda: nc.tensor.transpose(ps[:, k * 128:(k + 1) * 128],
                                                             xn[:, kc, sub * 128:(sub + 1) * 128], self.identf[:]),
                          [xnt, self.ctrk], [pt], signal=(k == 3))
                eng = self.rot([fw.act, fw.dve])
                self.copy(eng, xo[:, sub, half * 512:(half + 1) * 512], ps[:, :], [pt], [xot])
        fw.dma(fw.sp, self.out[s, t0:t0 + TT, :].rearrange("(sub p) d -> p sub d", p=128), xo,
               reads=[xot], writes=[self.xrt[s][t0 // TT]], sem_on=xot, accumulate_w=True, defer=True)

    def phase_x0(self, s):
        import contextlib
        nc, fw, I = self.nc, self.fw, self.I
        with contextlib.ExitStack() as es:
            sb = lambda name, shape, dt: es.enter_context(nc.sbuf_tensor(self.uniq(name), list(shape), dt))
            n = self.alloc_norm_tiles(es)
            xins = [(sb(f"xin{i}", [128, 4, D], F32), Trk(f"xin{i}")) for i in range(2)]
            xts = [(sb(f"xt{i}", [128, 8, TT], F32), Trk(f"xt{i}")) for i in range(2)]
            for ti in range(NT):
                t0 = ti * TT
                xin, xint = xins[ti % 2]; xt, xtt = xts[ti % 2]
                fw.dma(fw.sp, xin[:], I["x"][s, t0:t0 + TT, :].rearrange("(sub p) d -> p sub d", p=128),
                       writes=[xint], sem_on=xint)
                for kc in range(8):
                    ps, pt = self.next_ps()
                    for sub in range(4):
                        fw.op(fw.pe, lambda: nc.tensor.transpose(ps[:, sub * 128:(sub + 1) * 128],
                                                                 xin[:, sub, kc * 128:(kc + 1) * 128], self.identf[:]),
                              [xint, self.ctrk], [pt], signal=(sub == 3))
                    eng = self.rot([fw.act, fw.dve])
                    self.copy(eng, xt[:, kc, :], ps[:, :], [pt], [xtt])
                fw.dma(fw.sp, self.S["xres"][s].rearrange("(kc p) t -> p kc t", p=128)[:, :, t0:t0 + TT], xt[:],
                       reads=[xtt], writes=[self.xrt[s][ti]], sem_on=xtt, accumulate_w=True, defer=True)
                self.prenorm(n, xt[:], xtt, 0, s, 0, t0)

    def load_cast(self, dst, dstt, src_ap, stg_list, k):
        nc, fw = self.nc, self.fw
        stg, stgt = stg_list[k % len(stg_list)]
        shp = list(src_ap.shape)
        view = stg
        fw.dma(fw.sp, view, src_ap, writes=[stgt], sem_on=stgt)
        eng = self.rot([fw.dve, fw.pool, fw.act])
        self.copy(eng, dst, view, [stgt], [dstt])

    def phase_p1(self, l, s):
        import contextlib
        nc, fw, I = self.nc, self.fw, self.I
        with contextlib.ExitStack() as es:
            sb = lambda name, shape, dt: es.enter_context(nc.sbuf_tensor(self.uniq(name), list(shape), dt))
            hfull = sb("hfull", [128, 8, T], BF16); hfts = [Trk(f"hfull{i}") for i in range(8)]
            win = sb("win", [128, 8, 2048], BF16); wints = [Trk(f"win{i}") for i in range(8)]
            stgs = [(sb(f"wstg{i}", [128, 2048], F32), Trk(f"wstg{i}")) for i in range(2)]
            raw = sb("raw", [128, T], F32); rawt = [Trk(f"raw{i}") for i in range(NT)]
            acc = sb("acc", [128, T], F32); acct = Trk("acc")
            cbs = [(sb(f"cb{i}", [128, T], BF16), Trk(f"cb{i}")) for i in range(2)]
            rbf = sb("rbf", [128, T], BF16); rbft = [Trk(f"rbf{i}") for i in range(NT)]
            wgb = sb("wgb", [128, 128], F32); wgbt = Trk("wgb")
            mab = sb("mab", [128, 4, 256], BF16); mabt = Trk("mab")
            for kc in range(8):
                fw.dma(fw.sp, hfull[:, kc, :], self.S["hT"][s][kc * 128:(kc + 1) * 128, :], reads=[self.ST["hT"]],
                       writes=[hfts[kc]], sem_on=hfts[kc])
                self.load_cast(win[:, kc, :], wints[kc], I["w_in"][l][kc * 128:(kc + 1) * 128, :],
                               [(stgs[0][0][:], stgs[0][1]), (stgs[1][0][:], stgs[1][1])], kc)
            for cc in range(4):
                fw.op(fw.dve, lambda: nc.vector.memset(wgb[:], 0.0), [], [wgbt])
                fw.dma(fw.sp, wgb[0:64, 0:64], I["fnet_w"][l, 2 * cc], writes=[wgbt], sem_on=wgbt)
                fw.dma(fw.sp, wgb[64:128, 64:128], I["fnet_w"][l, 2 * cc + 1], writes=[wgbt], sem_on=wgbt, accumulate_w=True)
                ps, pt = self.next_ps()
                fw.op(fw.pe, lambda: nc.tensor.matmul(ps[:, 0:128], lhsT=self.CSb[:, 0:128], rhs=wgb[:], start=True, stop=True),
                      [wgbt, self.ctrk], [pt], signal=False)
                fw.op(fw.pe, lambda: nc.tensor.matmul(ps[:, 128:256], lhsT=self.CSb[:, 128:256], rhs=wgb[:], start=True, stop=True),
                      [wgbt, self.ctrk], [pt])
                self.copy(fw.dve, mab[:, cc, :], ps[:, 0:256], [pt], [mabt])
            sw0 = self.col[("short_w", l)]; sb0 = self.col[("short_b", l)]
            V = self.V
            for q in range(16):
                for ti in range(NT):
                    t0 = ti * TT
                    ps, pt = self.next_ps()
                    for kc in range(8):
                        fw.op(fw.pe, lambda: nc.tensor.matmul(ps[:, :], lhsT=win[:, kc, q * 128:(q + 1) * 128],
                                                              rhs=hfull[:, kc, t0:t0 + TT], start=(kc == 0), stop=(kc == 7)),
                              [wints[kc], hfts[kc]], [pt], signal=(kc == 7))
                    if q < 12:
                        self.copy(fw.act, raw[:, t0:t0 + TT], ps[:, :], [pt], [rawt[ti]])
                    else:
                        self.copy(fw.act, rbf[:, t0:t0 + TT], ps[:, :], [pt], [rbft[ti]])
                if q < 12:
                    cb, cbt = cbs[q % 2]
                    w0 = V[:, sw0 + q:sw0 + q + 1]; w1 = V[:, sw0 + 12 + q:sw0 + 12 + q + 1]
                    w2 = V[:, sw0 + 24 + q:sw0 + 24 + q + 1]; bb = V[:, sb0 + q:sb0 + q + 1]
                    fw.op(fw.act, lambda: nc.scalar.activation(out=acc[:], in_=raw[:], func=AF.Identity, bias=bb, scale=w1),
                          rawt + [self.vt], [acct])
                    fw.op(fw.dve, lambda: nc.vector.scalar_tensor_tensor(out=acc[:, 1:T], in0=raw[:, 0:T - 1], scalar=w0,
                                                                         in1=acc[:, 1:T], op0=ALU.mult, op1=ALU.add),
                          rawt + [acct], [acct])
                    fw.op(fw.dve, lambda: nc.vector.scalar_tensor_tensor(out=cb[:, 0:T - 1], in0=raw[:, 1:T], scalar=w2,
                                                                         in1=acc[:, 0:T - 1], op0=ALU.mult, op1=ALU.add),
                          rawt + [acct], [cbt])
                    fw.op(fw.dve, lambda: nc.vector.tensor_copy(out=cb[:, T - 1:T], in_=acc[:, T - 1:T]), [acct], [cbt])
                    self.dump("d_raw", raw[:, 0:512], rawt[0])
                    self.dump("d_acc", acc[:, 0:512], acct)
                    self.dump("d_cb", cb[:, 0:512], cbt, BF16)
                    fw.dma(fw.sp, self.S["uv"][q], cb[:], reads=[cbt], writes=[self.ST["uv"]], sem_on=cbt, accumulate_w=True, defer=True)
                else:
                    cc = q - 12
                    for half in range(2):
                        cb, cbt = cbs[half]
                        for ti in range(NT):
                            t0 = ti * TT
                            ps, pt = self.next_ps()
                            fw.op(fw.pe, lambda: nc.tensor.matmul(ps[:, :], lhsT=mab[:, cc, half * 128:(half + 1) * 128],
                                                                  rhs=rbf[:, t0:t0 + TT], start=True, stop=True),
                                  [mabt, rbft[ti]], [pt])
                            eng = self.rot([fw.act, fw.dve])
                            self.copy(eng, cb[:, t0:t0 + TT], ps[:, :], [pt], [cbt])
                        fw.dma(fw.sp, self.S["ab"][half * 4 + cc], cb[:], reads=[cbt], writes=[self.ST["ab"]],
                               sem_on=cbt, accumulate_w=True, defer=True)

    def proj_epilogue_phase(self, l, s, which, src_name, nk, w_ap, last):
        import contextlib
        nc, fw, I = self.nc, self.fw, self.I
        with contextlib.ExitStack() as es:
            sb = lambda name, shape, dt: es.enter_context(nc.sbuf_tensor(self.uniq(name), list(shape), dt))
            n = self.alloc_norm_tiles(es, with_tmp=False, with_ho=not (self.hf is not None and which == 0))
            wsb = sb("wsb", [128, nk, D], BF16); wts = [Trk(f"wsb{i}") for i in range(nk)]
            stgs = [(sb(f"wstg{i}", [128, 1024], F32), Trk(f"wstg{i}")) for i in range(2)]
            ins = [(sb(f"pin{i}", [128, nk, TT], BF16), Trk(f"pin{i}")) for i in range(2)]
            xt = sb("xt", [128, 8, TT], F32); xtt = Trk("xt")
            yts = [(sb(f"yt{b_}", [128, 8, TT], F32), [Trk(f"yt{b_}_{i}") for i in range(8)]) for b_ in range(2)]
            xn = sb("xn", [128, 8, TT], F32); xnt = Trk("xn")
            if last:
                n["xo"] = (xt[:].rearrange("p k t -> p (k t)").rearrange("p (a d) -> p a d", a=4), xtt)
            for k1 in range(nk):
                stg, stgt = stgs[k1 % 2]
                fw.dma(fw.sp, stg[:], w_ap[k1 * 128:(k1 + 1) * 128, :], writes=[stgt], sem_on=stgt)
                eng = self.rot([fw.dve, fw.pool, fw.act])
                self.copy(eng, wsb[:, k1, :], stg[:], [stgt], [wts[k1]])
            src = self.S[src_name]

            def load_pin(ti):
                pin, pint = ins[ti % 2]
                fw.dma(fw.sp, pin[:], src.rearrange("(kc p) t -> p kc t", p=128)[:, :, ti * TT:(ti + 1) * TT],
                       reads=[self.ST[src_name]], writes=[pint], sem_on=pint)

            def mm(ti, ocs):
                pin, pint = ins[ti % 2]
                yt, ytt = yts[ti % 2]
                for oc in ocs:
                    ps, pt = self.next_ps()
                    for kc in range(nk):
                        fw.op(fw.pe, lambda: nc.tensor.matmul(ps[:, :], lhsT=wsb[:, kc, oc * 128:(oc + 1) * 128],
                                                              rhs=pin[:, kc, :], start=(kc == 0), stop=(kc == nk - 1)),
                              [wts[kc], pint], [pt], signal=(kc == nk - 1))
                    eng = self.rot([fw.act, fw.dve])
                    self.copy(eng, yt[:, oc, :], ps[:, :], [pt], [ytt[oc]])

            def mm_first():
                pin, pint = ins[0]
                yt, ytt = yts[0]
                banks = [self.next_ps() for _ in range(8)]
                for kc in range(nk):
                    for oc in range(8):
                        ps, pt = banks[oc]
                        fw.op(fw.pe, lambda: nc.tensor.matmul(ps[:, :], lhsT=wsb[:, kc, oc * 128:(oc + 1) * 128],
                                                              rhs=pin[:, kc, :], start=(kc == 0), stop=(kc == nk - 1)),
                              [wts[kc], pint], [pt], signal=(kc == nk - 1))
                for oc in range(8):
                    ps, pt = banks[oc]
                    eng = self.rot([fw.act, fw.dve])
                    self.copy(eng, yt[:, oc, :], ps[:, :], [pt], [ytt[oc]])

            load_pin(0)
            mm_first()
            for ti in range(NT):
                t0 = ti * TT
                nxt = ti + 1 < NT
                yt, ytt = yts[ti % 2]
                n["tmp"] = (yt, ytt)
                if nxt:
                    load_pin(ti + 1)
                fw.dma(fw.sp, xt[:], self.S["xres"][s].rearrange("(kc p) t -> p kc t", p=128)[:, :, t0:t0 + TT],
                       reads=[self.xrt[s][ti]], writes=[xtt], sem_on=xtt)
                self.sq_of(n, yt[:], ytt)
                if nxt:
                    mm(ti + 1, [0, 1])
                self.epi_b(n, yt, ytt, xt, xtt, xn, xnt, l, s, which, t0, last)
                if nxt:
                    mm(ti + 1, [2, 3, 4, 5])
                self.epi_c(n, xn, xnt, l, s, which, t0, last)
                if nxt:
                    mm(ti + 1, [6, 7])

    def phase_p4(self, l, s):
        self.proj_epilogue_phase(l, s, 0, "mixT", 8, self.I["w_out"][l], False)

    def phase_p6(self, l, s):
        self.proj_epilogue_phase(l, s, 1, "gT", NFF, self.I["w_down"][l], l == self.n_layers - 1)

    def phase_p5(self, l, s):
        import contextlib
        nc, fw, I = self.nc, self.fw, self.I
        V = self.V
        with contextlib.ExitStack() as es:
            sb = lambda name, shape, dt: es.enter_context(nc.sbuf_tensor(self.uniq(name), list(shape), dt))
            if self.hf is not None:
                hfull, hfts = self.hf
            else:
                hfull = sb("hfull", [128, 8, T], BF16); hfts = [Trk(f"hfull{i}") for i in range(8)]
            wups = [(sb(f"wup{i}", [128, 2, 8, 128], BF16), Trk(f"wup{i}")) for i in range(2)]
            stgs = [(sb(f"ustg{i}", [128, 8, 128], F32), Trk(f"ustg{i}")) for i in range(2)]
            raws = [(sb(f"raw{h}", [128, T], F32), [Trk(f"raw{h}_{i}") for i in range(NT)]) for h in range(2)]
            accs = [(sb(f"acc{h}", [128, T], F32), Trk(f"acc{h}")) for h in range(2)]
            gos = [(sb(f"go{i}", [128, T], BF16), Trk(f"go{i}")) for i in range(2)]
            for kc in range(8):
                if self.hf is None:
                    fw.dma(fw.sp, hfull[:, kc, :], self.S["hT"][s][kc * 128:(kc + 1) * 128, :], reads=[self.ST["hT"]],
                           writes=[hfts[kc]], sem_on=hfts[kc])
            dw0 = self.col[("dw_w", l)]; db0 = self.col[("dw_b", l)]
            kst = 0
            acct = [[Trk(f"acc{h}_{i}") for i in range(NT)] for h in range(2)]

            CW = 2

            def conv_s1(j, gi):
                tl = list(range(gi * CW, (gi + 1) * CW))
                t0 = tl[0] * TT; t1 = (tl[-1] + 1) * TT
                for h in range(2):
                    raw, rawt = raws[h]; acc = accs[h][0]
                    q = h * NFF + j
                    w0 = V[:, dw0 + q:dw0 + q + 1]; w1 = V[:, dw0 + 44 + q:dw0 + 44 + q + 1]
                    w2 = V[:, dw0 + 88 + q:dw0 + 88 + q + 1]; bb = V[:, db0 + q:db0 + q + 1]
                    own = [rawt[t] for t in tl]
                    nb = own + ([rawt[tl[0] - 1]] if tl[0] > 0 else []) + ([rawt[tl[-1] + 1]] if tl[-1] < NT - 1 else [])
                    fw.op(fw.act, lambda: nc.scalar.activation(out=acc[:, t0:t1], in_=raw[:, t0:t1], func=AF.Identity,
                                                               bias=bb, scale=w1), own + [self.vt], [acct[h][gi]])
                    a0 = max(t0, 1)
                    fw.op(fw.dve, lambda: nc.vector.scalar_tensor_tensor(out=acc[:, a0:t1], in0=raw[:, a0 - 1:t1 - 1], scalar=w0,
                                                                         in1=acc[:, a0:t1], op0=ALU.mult, op1=ALU.add),
                          nb + [acct[h][gi]], [acct[h][gi]])
                    b1 = min(t1, T - 1)
                    fw.op(fw.dve, lambda: nc.vector.scalar_tensor_tensor(out=acc[:, t0:b1], in0=raw[:, t0 + 1:b1 + 1], scalar=w2,
                                                                         in1=acc[:, t0:b1], op0=ALU.mult, op1=ALU.add),
                          nb + [acct[h][gi]], [acct[h][gi]])

            def conv_s2(j, gi, go, got):
                if gi != NT // CW - 1:
                    return
                a0_, a1_ = accs[0][0], accs[1][0]
                fw.op(fw.act, lambda: nc.scalar.activation(out=a0_[:, :], in_=a0_[:, :], func=AF.Gelu),
                      acct[0], acct[0])
                fw.op(fw.dve, lambda: nc.vector.tensor_tensor(out=go[:, :], in0=a0_[:, :], in1=a1_[:, :], op=ALU.mult),
                      acct[0] + acct[1], [got])

            for j in range(NFF):
                wup, wupt = wups[j % 2]
                go, got = gos[j % 2]
                for h in range(2):
                    c0 = h * DFF + j * 128
                    stg, stgt = stgs[h]
                    fw.dma(fw.sp, stg[:], I["w_up"][l][:, c0:c0 + 128].rearrange("(kc p) n -> p kc n", p=128),
                           writes=[stgt], sem_on=stgt)
                for h in range(2):
                    stg, stgt = stgs[h]
                    self.copy(fw.act, wup[:, h, :, :], stg[:], [stgt], [wupt])
                for ti in range(NT):
                    t0 = ti * TT
                    for h in range(2):
                        raw, rawt = raws[h]
                        ps, pt = self.next_ps()
                        for kc in range(8):
                            fw.op(fw.pe, lambda: nc.tensor.matmul(ps[:, :], lhsT=wup[:, h, kc, :], rhs=hfull[:, kc, t0:t0 + TT],
                                                                  start=(kc == 0), stop=(kc == 7)), [wupt, hfts[kc]], [pt],
                                  signal=(kc == 7))
                        self.copy(fw.act, raw[:, t0:t0 + TT], ps[:, :], [pt], [rawt[ti]])
                    if ti % CW == CW - 1 and ti >= 2 * CW - 1:
                        g_ = ti // CW - 1
                        conv_s1(j, g_)
                        if g_ >= 1:
                            conv_s2(j, g_ - 1, go, got)
                NG = NT // CW
                conv_s1(j, NG - 1)
                conv_s2(j, NG - 2, go, got)
                conv_s2(j, NG - 1, go, got)
                fw.dma(fw.sp, self.S["gT"][j * 128:(j + 1) * 128, :], go[:], reads=[got], writes=[self.ST["gT"]],
                       sem_on=got, accumulate_w=True, defer=True)

    def fft_s1(self, u, ut, Ap, Apts, nk, Wsb, ncol, after=None):
        nc, fw = self.nc, self.fw
        per = 512 // ncol
        if after is not None:
            fw.order(fw.act, after); fw.order(fw.dve, after)
        for gi, c0 in enumerate(range(0, 128, per)):
            ps, pt = self.next_ps()
            for k in range(per):
                fw.op(fw.pe, lambda: nc.tensor.matmul(ps[:, k * ncol:(k + 1) * ncol], lhsT=u[0:nk, c0 + k, :],
                                                      rhs=Wsb[0:nk, :], start=True, stop=True),
                      [ut, self.ctrk], [pt], signal=(k == per - 1))
            eng = fw.act if gi % 2 == 0 else fw.dve
            self.copy(eng, Ap[:, c0:c0 + per, :].rearrange("p c k -> p (c k)"), ps[:, :], [pt], [Apts[gi]])

    def phase_conv(self, l, s):
        import contextlib
        nc, fw, I = self.nc, self.fw, self.I
        with contextlib.ExitStack() as es:
            sb = lambda name, shape, dt: es.enter_context(nc.sbuf_tensor(self.uniq(name), list(shape), dt))
            u = sb("fu", [32, 128, 128], BF16); ut = Trk("fu")
            Ap = sb("fAp", [128, 128, 128], BF16)
            Apts = [Trk(f"fAp{i}") for i in range(32)]; ApR = Trk("fApR")
            Bpts = [Trk(f"fBp{i}") for i in range(32)]; BpR = Trk("fBpR")
            P = sb("fP", [128, 64, 128], BF16); Pt = Trk("fP")
            Bt = sb("fBt", [128, 128, 128], BF16); Btts = [Trk(f"fBt{i}") for i in range(16)]
            m2s = [(sb(f"fm2_{i}", [128, 4096], BF16), Trk(f"fm2_{i}")) for i in range(2)]
            kas = [(sb(f"fka_{i}", [128, 2, 8, 128], BF16), Trk(f"fka_{i}")) for i in range(2)]
            t1 = sb("ft1", [128, 512], F32); t1t = Trk("ft1")
            t2 = sb("ft2", [128, 512], F32); t2t = Trk("ft2")
            gate = sb("fgate", [128, T], BF16); gatet = Trk("fgate")
            zo = sb("fzo", [128, T], BF16); zot = Trk("fzo")
            kst = 0
            for n in range(2):
                for cc in range(4):
                    if n == 0:
                        src = self.S["uv"][cc]; srct = self.ST["uv"]; g_ap = self.S["uv"][4 + cc]
                        dst = self.S["z2"][cc]; dstt = self.ST["z2"]
                    else:
                        src = self.S["z2"][cc]; srct = self.ST["z2"]; g_ap = self.S["uv"][8 + cc]
                        dst = self.S["mixT"][cc * 128:(cc + 1) * 128, :]; dstt = self.ST["mixT"]
                    fw.dma(fw.sp, u[:], src.rearrange("c (a i) -> a c i", a=32), reads=[srct], writes=[ut], sem_on=ut)
                    fw.dma(fw.sp, gate[:], g_ap, reads=[self.ST["uv"]], writes=[gatet], sem_on=gatet)
                    self.fft_s1(u, ut, Ap, Apts, 32, self.W1, 128, after=BpR)
                    for piece in range(8):
                        m2, m2t = m2s[kst % 2]; ka, kat = kas[kst % 2]; kst += 1
                        fw.dma(fw.sp, m2[:], I["M2s"][piece], writes=[m2t], sem_on=m2t)
                        fw.dma(fw.sp, ka[:].rearrange("p a f c -> p a (f c)"),
                               self.S["kab"][n, cc][:, :, piece * 1024:(piece + 1) * 1024].rearrange("a p n -> p a n"),
                               reads=[self.ST["kab"]], writes=[kat], sem_on=kat)
                        for g in range(2):
                            psx, ptx = self.next_ps(); pss, pts = self.next_ps()
                            for k in range(4):
                                fl = g * 4 + k; flo = piece * 8 + fl
                                for var, (ps, pt) in enumerate(((psx, ptx), (pss, pts))):
                                    for rip in range(2):
                                        o = ((fl * 2 + rip) * 2 + var) * 128
                                        fw.op(fw.pe, lambda: nc.tensor.matmul(ps[:, k * 128:(k + 1) * 128], lhsT=m2[:, o:o + 128],
                                                                              rhs=Ap[:, :, rip * 64 + flo],
                                                                              start=(rip == 0), stop=(rip == 1)),
                                              [m2t, ApR] + Apts, [pt], signal=(k == 3 and rip == 1))
                            kav = ka[:, 0, g * 4:(g + 1) * 4, :].rearrange("p f c -> p (f c)")
                            kbv = ka[:, 1, g * 4:(g + 1) * 4, :].rearrange("p f c -> p (f c)")
                            fw.op(fw.dve, lambda: nc.vector.tensor_tensor(out=t1[:], in0=psx[:, :], in1=kav, op=ALU.mult),
                                  [ptx, kat], [t1t])
                            fw.op(fw.dve, lambda: nc.vector.tensor_tensor(out=t2[:], in0=pss[:, :], in1=kbv, op=ALU.mult),
                                  [pts, kat], [t2t])
                            f0 = piece * 8 + g * 4
                            fw.op(fw.pool, lambda: nc.gpsimd.tensor_tensor(out=P[:, f0:f0 + 4, :].rearrange("p f c -> p (f c)"),
                                                                           in0=t1[:], in1=t2[:], op=ALU.add), [t1t, t2t], [Pt])
                    Bp = Ap[:].rearrange("p a b -> p (a b)").rearrange("p (k c) -> p k c", k=128)
                    fw.order(fw.act, ApR); fw.order(fw.dve, ApR)
                    gi = 0
                    for piece in range(8):
                        m2, m2t = m2s[kst % 2]; kst += 1
                        fw.dma(fw.sp, m2[:, 0:2048], I["G1s"][piece], writes=[m2t], sem_on=m2t)
                        for g in range(2):
                            for ri in range(2):
                                ps, pt = self.next_ps()
                                for k in range(4):
                                    fl = g * 4 + k; flo = piece * 8 + fl
                                    o = (fl * 2 + ri) * 128
                                    fw.op(fw.pe, lambda: nc.tensor.matmul(ps[:, k * 128:(k + 1) * 128], lhsT=m2[:, o:o + 128],
                                                                          rhs=P[:, flo, :], start=True, stop=True),
                                          [m2t, Pt], [pt], signal=(k == 3))
                                col0 = ri * 64 + piece * 8 + g * 4
                                eng = fw.act if gi % 2 == 0 else fw.dve
                                self.copy(eng, Bp[:, col0:col0 + 4, :].rearrange("p k c -> p (k c)"), ps[:, :], [pt], [Bpts[gi]])
                                gi += 1
                    for gi, c0 in enumerate(range(0, 128, 8)):
                        ps, pt = self.next_ps()
                        psb = ps[:, :].bitcast(BF16)
                        for k in range(8):
                            fw.op(fw.pe, lambda: nc.tensor.transpose(psb[:, k * 128:(k + 1) * 128], Bp[:, :, c0 + k], self.identb[:]),
                                  [BpR, self.ctrk] + Bpts, [pt], signal=(k == 7))
                        eng = fw.act if gi % 2 == 0 else fw.dve
                        self.copy(eng, Bt[:, c0:c0 + 8, :].rearrange("p c i -> p (c i)"), psb[:, 0:1024], [pt], [Btts[gi]])
                    zo3 = zo[:].rearrange("p (a i) -> p a i", a=32)
                    g3 = gate[:].rearrange("p (a i) -> p a i", a=32)
                    for i0 in range(0, 128, 16):
                        ps, pt = self.next_ps()
                        for k in range(16):
                            fw.op(fw.pe, lambda: nc.tensor.matmul(ps[:, k * 32:(k + 1) * 32], lhsT=Bt[:, :, i0 + k], rhs=self.W2[:, :],
                                                                  start=True, stop=True), Btts + [self.ctrk], [pt], signal=(k == 15))
                        fw.op(fw.dve, lambda: nc.vector.tensor_tensor(out=zo3[:, :, i0:i0 + 16],
                                                                      in0=ps[:, :].rearrange("p (k a) -> p a k", k=16),
                                                                      in1=g3[:, :, i0:i0 + 16], op=ALU.mult), [pt, gatet], [zot])
                    fw.dma(fw.sp, dst, zo[:], reads=[zot], writes=[dstt], sem_on=zot, accumulate_w=True, defer=True)

    def phase_fnet(self, l, s):
        import contextlib
        nc, fw, I = self.nc, self.fw, self.I
        with contextlib.ExitStack() as es:
            sb = lambda name, shape, dt: es.enter_context(nc.sbuf_tensor(self.uniq(name), list(shape), dt))
            us = [(sb(f"nu{i}", [64, 128, 128], BF16), Trk(f"nu{i}")) for i in range(2)]
            Ap = sb("nAp", [128, 128, 64], BF16); Apts = [Trk(f"nAp{i}") for i in range(16)]
            yfs = [(sb(f"nyf{i}", [128, T], BF16), Trk(f"nyf{i}")) for i in range(2)]
            for cc in range(4):
                u, ut = us[cc % 2]; yf, yft = yfs[cc % 2]
                fw.dma(fw.sp, u[0:32], self.S["ab"][cc].rearrange("c (a i) -> a c i", a=32), reads=[self.ST["ab"]],
                       writes=[ut], sem_on=ut)
                fw.dma(fw.sp, u[32:64], self.S["ab"][4 + cc].rearrange("c (a i) -> a c i", a=32), reads=[self.ST["ab"]],
                       writes=[ut], sem_on=ut, accumulate_w=True)
                self.fft_s1(u, ut, Ap, Apts, 64, self.W1f, 64)
                yf3 = yf[:].rearrange("p (f k) -> p f k", k=32)
                for g in range(8):
                    ps, pt = self.next_ps()
                    for k in range(4):
                        flo = g * 4 + k
                        for rip in range(2):
                            o = (flo * 2 + rip) * 128
                            fw.op(fw.pe, lambda: nc.tensor.matmul(ps[:, k * 128:(k + 1) * 128], lhsT=Ap[:, :, rip * 32 + flo],
                                                                  rhs=self.M2f[:, o:o + 128], start=(rip == 0), stop=(rip == 1)),
                                  Apts + [self.ctrk], [pt], signal=(k == 3 and rip == 1))
                    eng = self.rot([fw.act, fw.dve])
                    self.copy(eng, yf3[:, :, g * 4:(g + 1) * 4].rearrange("p f k -> p k f"),
                              ps[:, :].rearrange("p (k f) -> p k f", k=4), [pt], [yft])
                fw.dma(fw.sp, self.S["mixT"][512 + cc * 128:512 + (cc + 1) * 128, :], yf[:], reads=[yft],
                       writes=[self.ST["mixT"]], sem_on=yft, accumulate_w=True, defer=True)

    def phase_filter(self, l):
        import contextlib
        nc, fw, I = self.nc, self.fw, self.I
        V = self.V
        TWO_PI = float(2.0 * np.pi)
        with contextlib.ExitStack() as es:
            sb = lambda name, shape, dt: es.enter_context(nc.sbuf_tensor(self.uniq(name), list(shape), dt))
            w1 = sb("g_w1", [33, 64], F32); w2 = sb("g_w2", [64, 64], F32); w3 = sb("g_w3", [64, 2048], F32)
            zT = sb("g_zT", [33, T], F32); tb = sb("g_tb", [128, T], F32)
            wt = Trk("g_w")
            h1 = sb("g_h1", [64, T], F32); h1t = Trk("g_h1")
            h2 = sb("g_h2", [64, T], F32); h2t = Trk("g_h2")
            arg = sb("g_arg", [64, TT], F32); argt = Trk("g_arg")
            ki = sb("g_ki", [64, TT], I32); kit = Trk("g_ki")
            kr = sb("g_kr", [64, TT], F32); krt = Trk("g_kr")
            frb = sb("g_frb", [64, 2], F32); frbt = Trk("g_frb")
            dec = sb("g_dec", [128, T], F32); dect = Trk("g_dec")
            kf = sb("g_kf", [128, T], F32); kft = Trk("g_kf")
            kb = sb("g_kb", [128, T], F32); kbt = Trk("g_kb")
            sm = sb("g_sm", [128, 4], F32); smt = Trk("g_sm")
            sbf = sb("g_sbf", [128, T], BF16); sbft = Trk("g_sbf")
            dbf = sb("g_dbf", [128, T], BF16); dbft = Trk("g_dbf")
            for dst, src in ((w1, I["filt_w1"][l]), (w2, I["filt_w2"][l]), (w3, I["filt_w3"][l]), (zT, I["zT"]), (tb, I["tb"])):
                fw.dma(fw.sp, dst[:], src, writes=[wt], sem_on=wt, accumulate_w=True)
            cb1 = self.col[("filt_b1", l)]; cb2 = self.col[("filt_b2", l)]; cfr = self.col[("filt_freq", l)]
            hd0 = self.col[("hyena_d", l)]
            fr = V[0:64, cfr:cfr + 1]
            fw.op(fw.dve, lambda: nc.vector.tensor_tensor(out=frb[:, 0:1], in0=V[0:64, cb1:cb1 + 1], in1=fr, op=ALU.mult),
                  [self.vt], [frbt])
            fw.op(fw.dve, lambda: nc.vector.tensor_tensor(out=frb[:, 1:2], in0=V[0:64, cb2:cb2 + 1], in1=fr, op=ALU.mult),
                  [self.vt], [frbt])
            for layer_i, (wsb, kdim, src, srct, dst, dstt) in enumerate(((w1, 33, zT, wt, h1, h1t), (w2, 64, h1, h1t, h2, h2t))):
                for ti in range(NT):
                    t0 = ti * TT
                    ps, pt = self.next_ps()
                    fw.op(fw.pe, lambda: nc.tensor.matmul(ps[0:64, :], lhsT=wsb[0:kdim, :], rhs=src[0:kdim, t0:t0 + TT],
                                                          start=True, stop=True), [wt, srct], [pt])
                    fw.op(fw.act, lambda: nc.scalar.activation(out=arg[:], in_=ps[0:64, :], func=AF.Identity,
                                                               bias=frb[:, layer_i:layer_i + 1], scale=fr), [pt, frbt, self.vt], [argt])
                    fw.op(fw.dve, lambda: nc.vector.tensor_scalar(out=ki[:], in0=arg[:], scalar1=1.0 / TWO_PI, scalar2=None,
                                                                  op0=ALU.mult), [argt], [kit])
                    fw.op(fw.dve, lambda: nc.vector.tensor_copy(out=kr[:], in_=ki[:]), [kit], [krt])
                    fw.op(fw.dve, lambda: nc.vector.scalar_tensor_tensor(out=kr[:], in0=kr[:], scalar=-TWO_PI, in1=arg[:],
                                                                         op0=ALU.mult, op1=ALU.add), [krt, argt], [krt])
                    fw.op(fw.dve, lambda: nc.vector.tensor_scalar(out=kr[:], in0=kr[:], scalar1=-3.141592, scalar2=3.141592,
                                                                  op0=ALU.max, op1=ALU.min), [krt], [krt])
                    fw.op(fw.act, lambda: nc.scalar.activation(out=dst[:, t0:t0 + TT], in_=kr[:], func=AF.Sin), [krt], [dstt])
            for cc in range(4):
                fw.op(fw.act, lambda: nc.scalar.activation(out=dec[:], in_=tb[:], func=AF.Exp, scale=self.dsc[:, cc:cc + 1]),
                      [wt, self.ctrk], [dect])
                for o in range(2):
                    for dirn, (kt, ktt) in enumerate(((kf, kft), (kb, kbt))):
                        q = o * 8 + dirn * 4 + cc
                        for ti in range(NT):
                            t0 = ti * TT
                            ps, pt = self.next_ps()
                            fw.op(fw.pe, lambda: nc.tensor.matmul(ps[:, :], lhsT=w3[0:64, q * 128:(q + 1) * 128],
                                                                  rhs=h2[0:64, t0:t0 + TT], start=True, stop=True), [wt, h2t], [pt])
                            fw.op(fw.dve, lambda: nc.vector.tensor_tensor(out=kt[:, t0:t0 + TT], in0=ps[:, :],
                                                                          in1=dec[:, t0:t0 + TT], op=ALU.mult), [pt, dect], [ktt])
                    fw.op(fw.dve, lambda: nc.vector.memset(kb[:, 0:1], 0.0), [], [kbt])
                    fw.op(fw.dve, lambda: nc.vector.tensor_reduce(out=sm[:, 0:1], in_=kf[:], axis=mybir.AxisListType.X, op=ALU.add,
                                                                  apply_absolute_value=True), [kft], [smt])
                    fw.op(fw.dve, lambda: nc.vector.tensor_reduce(out=sm[:, 1:2], in_=kb[:], axis=mybir.AxisListType.X, op=ALU.add,
                                                                  apply_absolute_value=True), [kbt], [smt])
                    fw.op(fw.dve, lambda: nc.vector.scalar_tensor_tensor(out=sm[:, 2:3], in0=sm[:, 0:1], scalar=1e-6, in1=sm[:, 1:2],
                                                                         op0=ALU.add, op1=ALU.add), [smt], [smt])
                    fw.op(fw.dve, lambda: nc.vector.reciprocal(out=sm[:, 3:4], in_=sm[:, 2:3]), [smt], [smt])
                    fw.op(fw.act, lambda: nc.scalar.activation(out=kf[:], in_=kf[:], func=AF.Copy, scale=sm[:, 3:4]),
                          [kft, smt], [kft])
                    fw.op(fw.dve, lambda: nc.vector.tensor_scalar(out=kb[:], in0=kb[:], scalar1=sm[:, 3:4], scalar2=None,
                                                                  op0=ALU.mult), [kbt, smt], [kbt])
                    dcol = hd0 + o * 4 + cc
                    fw.op(fw.dve, lambda: nc.vector.tensor_tensor(out=kf[:, 0:1], in0=kf[:, 0:1], in1=V[:, dcol:dcol + 1],
                                                                  op=ALU.add), [kft, self.vt], [kft])
                    fw.op(fw.dve, lambda: nc.vector.tensor_tensor(out=sbf[:], in0=kf[:], in1=kb[:], op=ALU.add),
                          [kft, kbt], [sbft])
                    fw.op(fw.pool, lambda: nc.gpsimd.tensor_tensor(out=dbf[:], in0=kf[:], in1=kb[:], op=ALU.subtract),
                          [kft, kbt], [dbft])
                    fw.dma(fw.sp, self.S["sd"][o, cc, 0], sbf[:], reads=[sbft], writes=[self.ST["sd"]], sem_on=sbft, accumulate_w=True, defer=True)
                    fw.dma(fw.sp, self.S["sd"][o, cc, 1], dbf[:], reads=[dbft], writes=[self.ST["sd"]], sem_on=dbft, accumulate_w=True, defer=True)
        fw.barrier(); fw.recycle_dma()
        with contextlib.ExitStack() as es:
            sb = lambda name, shape, dt: es.enter_context(nc.sbuf_tensor(self.uniq(name), list(shape), dt))
            us = [(sb(f"k_u{i}", [32, 128, 128], BF16), Trk(f"k_u{i}")) for i in range(2)]
            Aps = [(sb(f"k_Ap{i}", [128, 128, 128], BF16), [Trk(f"k_Ap{i}_{j}") for j in range(32)]) for i in range(2)]
            mabs = [(sb(f"k_mab{i}", [128, 4096], BF16), Trk(f"k_mab{i}")) for i in range(2)]
            kos = [(sb(f"k_ko{i}", [128, 2, 8, 128], BF16), Trk(f"k_ko{i}")) for i in range(2)]
            kst = 0
            for o in range(2):
                for cc in range(4):
                    for sd in range(2):
                        fw.dma(fw.sp, us[sd][0][:], self.S["sd"][o, cc, sd].rearrange("c (a i) -> a c i", a=32),
                               reads=[self.ST["sd"]], writes=[us[sd][1]], sem_on=us[sd][1])
                        self.fft_s1(us[sd][0], us[sd][1], Aps[sd][0], Aps[sd][1], 32, self.W1, 128)
                    for piece in range(8):
                        mab, mabt = mabs[kst % 2]; ko, kot = kos[kst % 2]; kst += 1
                        fw.dma(fw.sp, mab[:], I["MABs"][piece], writes=[mabt], sem_on=mabt)
                        for g in range(2):
                            for ab in range(2):
                                Ap, Apt = Aps[ab]
                                ps, pt = self.next_ps()
                                for k in range(4):
                                    fl = g * 4 + k; flo = piece * 8 + fl
                                    for rip in range(2):
                                        off = ((fl * 2 + ab) * 2 + rip) * 128
                                        fw.op(fw.pe, lambda: nc.tensor.matmul(ps[:, k * 128:(k + 1) * 128], lhsT=mab[:, off:off + 128],
                                                                              rhs=Ap[:, :, rip * 64 + flo],
                                                                              start=(rip == 0), stop=(rip == 1)),
                                              [mabt] + Apt, [pt], signal=(k == 3 and rip == 1))
                                eng = self.rot([fw.act, fw.dve])
                                self.copy(eng, ko[:, ab, g * 4:(g + 1) * 4, :].rearrange("p f c -> p (f c)"), ps[:, :], [pt], [kot])
                        fw.dma(fw.sp, self.S["kab"][o, cc][:, :, piece * 1024:(piece + 1) * 1024].rearrange("a p n -> p a n"),
                               ko[:].rearrange("p a f c -> p a (f c)"), reads=[kot], writes=[self.ST["kab"]], sem_on=kot,
                               accumulate_w=True, defer=True)


_CONSTS = None


def kernel(**inputs):
    global _CONSTS
    if _CONSTS is None:
        _CONSTS = make_consts()
    x = np.ascontiguousarray(np.asarray(inputs["x"], dtype=np.float32))
    c = np.ascontiguousarray(np.asarray(inputs["c"], dtype=np.float32))
    weights = {n: np.ascontiguousarray(np.asarray(inputs[n], dtype=np.float32)) for n in WEIGHT_NAMES}
    n_cores = 8
    nc = Builder().build()
    in_maps = []
    for i in range(n_cores):
        m = {"x": x[i * NSEQ:(i + 1) * NSEQ], "c": c[i * NSEQ:(i + 1) * NSEQ]}
        m.update(weights)
        m.update(_CONSTS)
        in_maps.append(m)
    res = run_bass_kernel_spmd(nc, in_maps, core_ids=list(range(n_cores)))
    out = np.concatenate([np.asarray(r["out"], dtype=np.float32) for r in res.results], axis=0)
    return out
```

```python
import numpy as np
import ml_dtypes
import concourse.bass as bass
import concourse.mybir as mybir
from concourse.bass_utils import run_bass_kernel_spmd

F32 = mybir.dt.float32
BF16 = mybir.dt.bfloat16
I32 = mybir.dt.int32
ALU = mybir.AluOpType
AF = mybir.ActivationFunctionType

D = 1024
T = 4096
DEPTH = 4
NSEQ = 2
DH = 512
DFF = 2816
NFF = 22
TT = 512
NT = T // TT
EPS = 1e-6
N2 = 8192


def _hy_consts():
    a = np.arange(32)[:, None]; flo = np.arange(64)[None, :]
    phi = 2 * np.pi * a * flo / 64.0
    W1 = np.concatenate([np.cos(phi), -np.sin(phi)], axis=1)
    W2 = np.concatenate([np.cos(phi).T, -np.sin(phi).T], axis=0)
    i = np.arange(128)[:, None]; fhi = np.arange(64)[None, :]
    M2 = np.zeros((64, 2, 2, 128, 128))
    MAB = np.zeros((64, 2, 2, 128, 128))
    G1 = np.zeros((64, 2, 128, 128))
    sgn = (-1.0) ** np.arange(128)
    for fl in range(64):
        f = fl + 64 * fhi
        th = 2 * np.pi * i * f / N2
        c, s = np.cos(th), np.sin(th)
        m_re = np.concatenate([c, -s], axis=1)
        m_im = np.concatenate([s, c], axis=1)
        ma_re = np.concatenate([c, c], axis=1); ma_im = np.concatenate([s, s], axis=1)
        mb_re = np.concatenate([s, -s], axis=1); mb_im = np.concatenate([-c, c], axis=1)
        if fl == 0:
            m_re[:, 64] = sgn; m_im[:, 64] = 0.0
            ma_re[:, 64] = sgn; ma_im[:, 64] = 0.0
            mb_re[:, 64] = 0; mb_im[:, 64] = 0; mb_re[:, 0] = 0; mb_im[:, 0] = 0
        M2[fl, 0, 0] = m_re; M2[fl, 1, 0] = m_im
        M2[fl, 0, 1] = np.roll(m_re, -64, axis=1); M2[fl, 1, 1] = np.roll(m_im, -64, axis=1)
        MAB[fl, 0, 0] = ma_re; MAB[fl, 0, 1] = ma_im; MAB[fl, 1, 0] = mb_re; MAB[fl, 1, 1] = mb_im
        wgt = np.where(f == 0, 1.0, 2.0) / N2
        g_re = np.concatenate([(wgt * c).T, (-wgt * s).T], axis=0)
        g_im = np.concatenate([(wgt * s).T, (wgt * c).T], axis=0)
        if fl == 0:
            g_re[64, :] = sgn / N2; g_im[64, :] = 0.0
        G1[fl, 0] = g_re; G1[fl, 1] = g_im
    bf = ml_dtypes.bfloat16
    M2s = M2.reshape(8, 8, 2, 2, 128, 128).transpose(0, 4, 1, 2, 3, 5).reshape(8, 128, 8 * 4 * 128)
    MABs = MAB.reshape(8, 8, 2, 2, 128, 128).transpose(0, 4, 1, 2, 3, 5).reshape(8, 128, 8 * 4 * 128)
    G1s = G1.reshape(8, 8, 2, 128, 128).transpose(0, 3, 1, 2, 4).reshape(8, 128, 8 * 2 * 128)
    return dict(W1=W1.astype(bf), W2=W2.astype(bf), M2s=np.ascontiguousarray(M2s).astype(bf),
                MABs=np.ascontiguousarray(MABs).astype(bf), G1s=np.ascontiguousarray(G1s).astype(bf))


def _fn_consts():
    a = np.arange(32)[:, None]; flo = np.arange(32)[None, :]
    phi = 2 * np.pi * a * flo / 32.0
    c, s = np.cos(phi), np.sin(phi)
    W1f = np.concatenate([np.concatenate([c, -s], axis=1), np.concatenate([-s, -c], axis=1)], axis=0)
    i = np.arange(128)[:, None]; fhi = np.arange(128)[None, :]
    M2f = np.zeros((32, 2, 128, 128))
    for fl in range(32):
        f = fl + 32 * fhi
        th = 2 * np.pi * i * f / T
        M2f[fl, 0] = np.cos(th) / 512.0; M2f[fl, 1] = np.sin(th) / 512.0
    bf = ml_dtypes.bfloat16
    M2fs = M2f.transpose(2, 0, 1, 3).reshape(128, 32 * 2 * 128)
    j = np.arange(64)
    cc = np.cos(2 * np.pi * np.outer(j, j) / 64.0); sc = np.sin(2 * np.pi * np.outer(j, j) / 64.0)
    Cb = np.zeros((128, 128)); Sb = np.zeros((128, 128))
    Cb[:64, :64] = cc; Cb[64:, 64:] = cc; Sb[:64, :64] = sc; Sb[64:, 64:] = sc
    return dict(W1f=W1f.astype(bf), M2fs=np.ascontiguousarray(M2fs).astype(bf),
                CSb=np.concatenate([Cb, Sb], axis=1).astype(np.float32))


def _filter_consts():
    pos = np.arange(T, dtype=np.float32)
    t = (pos / np.float32(T - 1)).astype(np.float32)
    bands = np.linspace(1e-4, 15, 16, dtype=np.float32)
    ang = (np.float32(2.0 * np.pi / T) * pos[:, None] * bands[None, :]).astype(np.float32)
    z = np.concatenate([t[:, None], np.cos(ang), -np.sin(ang)], axis=-1).astype(np.float32)
    min_decay = np.log(1e-2) / 1.5; max_decay = np.log(1e-2) / 0.3
    deltas = np.linspace(min_decay, max_decay, DH, dtype=np.float32)
    dsc = (-np.abs(deltas)).astype(np.float32).reshape(4, 128).T
    tb = np.broadcast_to(t[None, :], (128, T)).astype(np.float32)
    return dict(zT=np.ascontiguousarray(z.T), dsc=np.ascontiguousarray(dsc), tb=np.ascontiguousarray(tb))


class Trk:
    __slots__ = ("w", "r", "name")

    def __init__(self, name=""):
        self.w = {}
        self.r = {}
        self.name = name


class Eng:
    def __init__(self, fw, name, eng, sem):
        self.fw = fw; self.name = name; self.e = eng; self.sem = sem
        self.cnt = 0
        self.seen = {}

    def wait(self, sem, val):
        k = id(sem)
        if self.seen.get(k, 0) >= val:
            return
        if sem is self.sem and self.name == "pe":
            return
        self.e.wait_ge(sem, val)
        self.seen[k] = val


class FW:
    def __init__(self, nc, sems):
        self.nc = nc
        self._sems = list(sems)
        self.pe = Eng(self, "pe", nc.tensor, self._sems.pop())
        self.act = Eng(self, "act", nc.scalar, self._sems.pop())
        self.dve = Eng(self, "dve", nc.vector, self._sems.pop())
        self.pool = Eng(self, "pool", nc.gpsimd, self._sems.pop())
        self.sp = Eng(self, "sp", nc.sync, self._sems.pop())
        self.engs = [self.pe, self.act, self.dve, self.pool, self.sp]
        self.dma_sems = {}
        self.all_dma = []
        self.rr = 0
        self.npe = 0
        self.pending = []
        self.loads_since = False

    def _deps(self, eng, reads, writes):
        for t in reads:
            for (sem, val) in t.w.values():
                eng.wait(sem, val)
        for t in writes:
            for (sem, val) in t.w.values():
                eng.wait(sem, val)
            for (sem, val) in t.r.values():
                eng.wait(sem, val)

    def _conflict(self, reads, writes):
        for p in self.pending:
            pr, pw = p[3], p[4]
            for t in writes:
                if any(t is x for x in pr) or any(t is x for x in pw):
                    return True
            for t in reads:
                if any(t is x for x in pw):
                    return True
        return False

    def order(self, eng, trk):
        for (sem, val) in list(trk.w.values()) + list(trk.r.values()):
            if sem is not eng.sem:
                eng.wait(sem, val)

    def flush(self):
        pend, self.pending = self.pending, []
        for (q, out, in_, rd, wr, sem_on, acc, kw) in pend:
            self.dma(q, out, in_, reads=rd, writes=wr, sem_on=sem_on, accumulate_w=acc, _is_flush=True, **kw)
        self.loads_since = False

    def op(self, eng, fn, reads=(), writes=(), signal=True):
        if self.pending and (self.loads_since or self._conflict(reads, writes)):
            self.flush()
        self._deps(eng, reads, writes)
        ins = fn()
        if eng is self.pe:
            self.npe += 1
        if signal:
            eng.cnt += 1
            ins.then_inc(eng.sem, 1)
            mark = (eng.sem, eng.cnt)
        else:
            mark = (eng.sem, eng.cnt + 1)
        k = id(eng.sem)
        for t in writes:
            t.w = {k: mark}; t.r = {}
        for t in reads:
            t.r[k] = mark
        return ins

    def _dsem(self, trk):
        ent = self.dma_sems.get(id(trk))
        if ent is None:
            ent = [self._sems.pop(), 0, trk]
            self.dma_sems[id(trk)] = ent
            self.all_dma.append(ent)
        return ent

    def dma(self, q, out, in_, reads=(), writes=(), sem_on=None, accumulate_w=False, defer=False, _is_flush=False, **kw):
        rd = list(reads); wr = list(writes)
        if defer:
            self.pending.append((q, out, in_, rd, wr, sem_on, accumulate_w, kw))
            self.loads_since = False
            return None
        if not _is_flush:
            if self.pending and self._conflict(rd, wr):
                self.flush()
            self.loads_since = True
        if accumulate_w:
            for t in rd:
                for (sem, val) in t.w.values():
                    q.wait(sem, val)
            eng_ids = {id(e.sem) for e in self.engs}
            for t in wr:
                for (sem, val) in t.r.values():
                    q.wait(sem, val)
                for (sem, val) in t.w.values():
                    if id(sem) in eng_ids:
                        q.wait(sem, val)
        else:
            self._deps(q, rd, wr)
        ent = self._dsem(sem_on)
        ent[1] += 16
        ins = q.e.dma_start(out=out, in_=in_, **kw)
        ins.then_inc(ent[0], 16)
        mark = (ent[0], ent[1])
        k = id(ent[0])
        for t in wr:
            if accumulate_w:
                t.w[k] = mark
            else:
                t.w = {k: mark}; t.r = {}
        for t in rd:
            t.r[k] = mark
        return ins

    def barrier(self):
        self.flush()
        sp = self.sp
        for e in self.engs:
            if e is not sp and e.cnt > 0:
                sp.wait(e.sem, e.cnt)
        for ent in self.all_dma:
            if ent[1] > 0:
                sp.wait(ent[0], ent[1])
        sp.cnt += 1
        sp.e.sem_inc(sp.sem, 1)
        for e in self.engs:
            if e is not sp:
                e.wait(sp.sem, sp.cnt)
        for e in self.engs:
            for o in self.engs:
                e.seen[id(o.sem)] = max(e.seen.get(id(o.sem), 0), o.cnt if o is not sp else sp.cnt)
            for ent in self.all_dma:
                e.seen[id(ent[0])] = max(e.seen.get(id(ent[0]), 0), ent[1])

    def finish(self):
        self.flush()
        sp = self.sp
        for e in self.engs:
            if e is not sp and e.cnt > 0:
                sp.wait(e.sem, e.cnt)
        for ent in self.all_dma:
            if ent[1] > 0:
                sp.wait(ent[0], ent[1])

    def recycle_dma(self):
        self.free_dma = getattr(self, "free_dma", []) + [e for e in self.all_dma if e[2] is not None]
        for e in self.free_dma:
            e[2] = None
        self.dma_sems = {}


def _dsem_recycling(self, trk):
    ent = self.dma_sems.get(id(trk))
    if ent is None:
        fl = getattr(self, "free_dma", [])
        if fl:
            ent = fl.pop()
            ent[2] = trk
        else:
            ent = [self._sems.pop(), 0, trk]
            self.all_dma.append(ent)
        self.dma_sems[id(trk)] = ent
    return ent


FW._dsem = _dsem_recycling


WEIGHT_NAMES = ["ada_w", "ada_b", "g_mix_pre", "g_mix_post", "w_in", "short_w", "short_b", "filt_w1", "filt_b1",
                "filt_w2", "filt_b2", "filt_freq", "filt_w3", "hyena_d", "fnet_w", "w_out", "g_ffn_pre",
                "g_ffn_post", "w_up", "dw_w", "dw_b", "w_down"]
WEIGHT_SHAPES = {
    "ada_w": (DEPTH, D, 6 * D), "ada_b": (DEPTH, 6 * D), "g_mix_pre": (DEPTH, D), "g_mix_post": (DEPTH, D),
    "w_in": (DEPTH, D, 2048), "short_w": (DEPTH, 3, 1536), "short_b": (DEPTH, 1536),
    "filt_w1": (DEPTH, 33, 64), "filt_b1": (DEPTH, 64), "filt_w2": (DEPTH, 64, 64), "filt_b2": (DEPTH, 64),
    "filt_freq": (DEPTH, 64), "filt_w3": (DEPTH, 64, 2048), "hyena_d": (DEPTH, 2, DH),
    "fnet_w": (DEPTH, 8, 64, 64), "w_out": (DEPTH, D, D), "g_ffn_pre": (DEPTH, D), "g_ffn_post": (DEPTH, D),
    "w_up": (DEPTH, D, 2 * DFF), "dw_w": (DEPTH, 3, 2 * DFF), "dw_b": (DEPTH, 2 * DFF), "w_down": (DEPTH, DFF, D),
}
CONST_SHAPES = {
    "identf": ((128, 128), F32), "W1": ((32, 128), BF16), "W2": ((128, 32), BF16),
    "M2s": ((8, 128, 4096), BF16), "MABs": ((8, 128, 4096), BF16), "G1s": ((8, 128, 2048), BF16),
    "W1f": ((64, 64), BF16), "M2fs": ((128, 8192), BF16), "CSb": ((128, 256), F32),
    "zT": ((33, T), F32), "dsc": ((128, 4), F32), "tb": ((128, T), F32),
}


def make_consts():
    c = {}
    c.update(_hy_consts()); c.update(_fn_consts()); c.update(_filter_consts())
    c["identf"] = np.eye(128, dtype=np.float32)
    return c


class Builder:
    def __init__(self, n_layers=DEPTH, n_seq=NSEQ, phases=None, dbg=()):
        self.n_layers = n_layers; self.n_seq = n_seq
        self.phases = phases
        self.dbg = set(dbg)
        self.nc = bass.Bass("TRN2", target_bir_lowering=False)
        self.I = {}
        nc = self.nc
        self.I["x"] = nc.dram_tensor("x", [NSEQ, T, D], F32, kind="ExternalInput").ap()
        self.I["c"] = nc.dram_tensor("c", [NSEQ, D], F32, kind="ExternalInput").ap()
        for n in WEIGHT_NAMES:
            self.I[n] = nc.dram_tensor(n, list(WEIGHT_SHAPES[n]), F32, kind="ExternalInput").ap()
        for n, (shp, dt) in CONST_SHAPES.items():
            self.I[n] = nc.dram_tensor(n, list(shp), dt, kind="ExternalInput").ap()
        self.out = nc.dram_tensor("out", [NSEQ, T, D], F32, kind="ExternalOutput").ap()
        self.S = {}
        self.ST = {}

    def scratch(self, name, shape, dt):
        kind = "ExternalOutput" if name in self.dbg else "Internal"
        self.S[name] = self.nc.dram_tensor(name, list(shape), dt, kind=kind).ap()
        return self.S[name]

    def uniq(self, name):
        self._uid = getattr(self, "_uid", 0) + 1
        return f"{name}_u{self._uid}"

    def dump(self, name, ap, trk, dt=F32):
        if name not in self.dbg or name in self.S:
            return
        shp = list(ap.shape)
        d = self.nc.dram_tensor(name, shp, dt, kind="ExternalOutput").ap()
        self.S[name] = d
        self.fw.dma(self.fw.sp, d, ap, reads=[trk], sem_on=trk)

    def on(self, ph):
        return self.phases is None or ph in self.phases

    def build(self):
        import contextlib
        nc = self.nc
        with contextlib.ExitStack() as es:
            sems = [es.enter_context(nc.semaphore(f"s{i}")) for i in range(100)]
            fw = self.fw = FW(nc, sems)
            self.es = es
            self.scratch("xres", [NSEQ, D, T], F32)
            self.scratch("hT", [NSEQ, D, T], BF16)
            self.scratch("uv", [12, 128, T], BF16)
            self.scratch("ab", [8, 128, T], BF16)
            self.scratch("z2", [4, 128, T], BF16)
            self.scratch("mixT", [D, T], BF16)
            self.scratch("gT", [DFF, T], BF16)
            self.scratch("sd", [2, 4, 2, 128, T], BF16)
            self.scratch("kab", [2, 4, 2, 128, 64 * 128], BF16)
            for n in self.S:
                self.ST[n] = Trk(n)
            self.xrt = [[Trk(f"xres{s_}_{t_}") for t_ in range(NT)] for s_ in range(NSEQ)]
            self.ps = [es.enter_context(nc.psum_tensor(f"ps{b}", [128, 512], F32)) for b in range(8)]
            self.pst = [Trk(f"ps{b}") for b in range(8)]
            self.psi = 0
            sb = lambda name, shape, dt: es.enter_context(nc.sbuf_tensor(self.uniq(name), list(shape), dt))
            self.identf = sb("identf", [128, 128], F32); self.identb = sb("identb", [128, 128], BF16)
            self.onesb = sb("onesb", [128, 128], BF16)
            self.W1 = sb("W1sb", [32, 128], BF16); self.W2 = sb("W2sb", [128, 32], BF16)
            self.W1f = sb("W1fsb", [64, 64], BF16); self.M2f = sb("M2fsb", [128, 8192], BF16)
            self.CSb = sb("CSbsb", [128, 256], F32); self.dsc = sb("dscsb", [128, 4], F32)
            self.V = sb("V", [128, 1280], F32)
            self.modT = sb("modT", [128, DEPTH, 48, NSEQ], F32)
            self.DER = sb("DER", [128, DEPTH, NSEQ, 4, 8], F32)
            self.cT = sb("cT", [128, 8, NSEQ], F32)
            self.ctrk = Trk("consts")
            self.rr = 0
            self.marks = []
            self.setup()
            fw.barrier(); fw.recycle_dma()
            for s in range(self.n_seq):
                if self.on("x0"):
                    self.phase_x0(s)
                    fw.barrier(); fw.recycle_dma()
            for l in range(self.n_layers):
                if self.on("filt"):
                    self.marks.append((f"filt_l{l}", fw.npe))
                    self.phase_filter(l)
                    fw.barrier(); fw.recycle_dma()
                for s in range(self.n_seq):
                    for ph in ("p1", "conv", "fnet", "p4", "p5", "p6"):
                        if self.on(ph):
                            self.marks.append((f"{ph}_l{l}s{s}", fw.npe))
                            getattr(self, "phase_" + ph)(l, s)
                            fw.barrier(); fw.recycle_dma()
            fw.finish()
        return nc

    def next_ps(self):
        b = self.psi; self.psi = (self.psi + 1) % 8
        return self.ps[b], self.pst[b]

    def rot(self, engs):
        self.rr += 1
        return engs[self.rr % len(engs)]

    def copy(self, eng, out, in_, reads, writes):
        fw = self.fw
        if eng is fw.act:
            return fw.op(eng, lambda: eng.e.copy(out=out, in_=in_), reads, writes)
        return fw.op(eng, lambda: eng.e.tensor_copy(out=out, in_=in_), reads, writes)

    def setup(self):
        nc, fw, I = self.nc, self.fw, self.I
        sp = fw.sp
        ct = self.ctrk
        for dst, src in ((self.identf, "identf"), (self.W1, "W1"), (self.W2, "W2"), (self.W1f, "W1f"),
                         (self.M2f, "M2fs"), (self.CSb, "CSb"), (self.dsc, "dsc")):
            fw.dma(sp, dst[:], I[src], writes=[ct], sem_on=ct, accumulate_w=True)
        fw.op(fw.dve, lambda: nc.vector.tensor_copy(out=self.identb[:], in_=self.identf[:]), [ct], [ct])
        fw.op(fw.dve, lambda: nc.vector.memset(self.onesb[:], 1.0 / D), [], [ct])
        self.col = {}
        rows = []
        def reg(name, l, ap2d, n, width=128):
            self.col[(name, l)] = len(rows)
            for k in range(n):
                rows.append((ap2d, k, width))
        for l in range(self.n_layers):
            for nm in ("g_mix_pre", "g_mix_post", "g_ffn_pre", "g_ffn_post"):
                reg(nm, l, I[nm][l].rearrange("(c p) -> c p", p=128), 8)
            reg("ada_b", l, I["ada_b"][l].rearrange("(c p) -> c p", p=128), 48)
            reg("short_w", l, I["short_w"][l].rearrange("k (q p) -> (k q) p", p=128), 36)
            reg("short_b", l, I["short_b"][l].rearrange("(c p) -> c p", p=128), 12)
            reg("dw_w", l, I["dw_w"][l].rearrange("k (q p) -> (k q) p", p=128), 132)
            reg("dw_b", l, I["dw_b"][l].rearrange("(c p) -> c p", p=128), 44)
            reg("hyena_d", l, I["hyena_d"][l].rearrange("n (c p) -> (n c) p", p=128), 8)
            for nm in ("filt_b1", "filt_b2", "filt_freq"):
                reg(nm, l, I[nm][l].rearrange("(o p) -> o p", p=64), 1, 64)
        nrows = len(rows)
        ngrp = (nrows + 127) // 128
        assert ngrp * 128 <= 1280
        with nc.sbuf_tensor("stg_v", [128, ngrp, 128], F32) as stg, nc.sbuf_tensor("ctile_v", [NSEQ, D], F32) as ctile:
            st = Trk("stg")
            fw.op(fw.dve, lambda: nc.vector.memset(stg[:], 0.0), [], [st])
            r = 0
            while r < nrows:
                ap2d, k0, width = rows[r]
                n = 1
                while (r + n < nrows and rows[r + n][0] is ap2d and rows[r + n][1] == k0 + n
                       and (r + n) % 128 != 0):
                    n += 1
                fw.dma(sp, stg[r % 128:r % 128 + n, r // 128, 0:width], ap2d[k0:k0 + n, :],
                       writes=[st], sem_on=st, accumulate_w=True)
                r += n
            vt = Trk("V")
            for g in range(ngrp):
                ps, pt = self.next_ps()
                fw.op(fw.pe, lambda: nc.tensor.transpose(ps[:, 0:128], stg[:, g, :], self.identf[:]), [st, ct], [pt])
                self.copy(fw.dve, self.V[:, g * 128:(g + 1) * 128], ps[:, 0:128], [pt], [vt])
            self.vt = vt
            ctt = Trk("ctile")
            fw.dma(sp, ctile[:], I["c"], writes=[ctt], sem_on=ctt)
            ps, pt = self.next_ps()
            for kc in range(8):
                fw.op(fw.pe, lambda: nc.tensor.transpose(ps[:, kc * NSEQ:(kc + 1) * NSEQ],
                                                         ctile[0:NSEQ, kc * 128:(kc + 1) * 128],
                                                         self.identf[0:NSEQ, 0:NSEQ]), [ctt, ct], [pt], signal=(kc == 7))
            cact = Trk("cT")
            fw.op(fw.act, lambda: nc.scalar.activation(out=self.cT[:].rearrange("p k s -> p (k s)"),
                                                       in_=ps[:, 0:8 * NSEQ], func=AF.Silu), [pt], [cact])
            with nc.sbuf_tensor("aw0", [128, 8, 512], F32) as aw0, nc.sbuf_tensor("aw1", [128, 8, 512], F32) as aw1:
                aws = [(aw0, Trk("aw0")), (aw1, Trk("aw1"))]
                mt = Trk("modT")
                self.mt = mt
                k = 0
                for l in range(self.n_layers):
                    psm, ptm = self.next_ps()
                    for j in range(12):
                        aw, awt = aws[k % 2]; k += 1
                        fw.dma(sp, aw[:], I["ada_w"][l][:, j * 512:(j + 1) * 512].rearrange("(kc p) n -> p kc n", p=128),
                               writes=[awt], sem_on=awt)
                        for sub in range(4):
                            ch = j * 4 + sub
                            for kc in range(8):
                                fw.op(fw.pe, lambda: nc.tensor.matmul(psm[:, ch * NSEQ:(ch + 1) * NSEQ],
                                                                      lhsT=aw[:, kc, sub * 128:(sub + 1) * 128],
                                                                      rhs=self.cT[:, kc, :], start=(kc == 0), stop=(kc == 7)),
                                      [awt, cact], [ptm], signal=(kc == 7))
                    ab0 = self.col[("ada_b", l)]
                    for s in range(NSEQ):
                        fw.op(fw.dve, lambda: nc.vector.tensor_tensor(
                            out=self.modT[:, l, :, s], in0=psm[:, 0:48 * NSEQ].rearrange("p (c s) -> p c s", s=NSEQ)[:, :, s],
                            in1=self.V[:, ab0:ab0 + 48], op=ALU.add), [ptm, vt], [mt])
                    for s in range(NSEQ):
                        for w, (gname, sc_ch, ga_ch) in enumerate((("g_mix_pre", 8, None), ("g_mix_post", None, 16),
                                                                   ("g_ffn_pre", 32, None), ("g_ffn_post", None, 40))):
                            gc = self.col[(gname, l)]
                            if sc_ch is not None:
                                fw.op(fw.dve, lambda: nc.vector.scalar_tensor_tensor(
                                    out=self.DER[:, l, s, w, :], in0=self.modT[:, l, sc_ch:sc_ch + 8, s], scalar=1.0,
                                    in1=self.V[:, gc:gc + 8], op0=ALU.add, op1=ALU.mult), [mt, vt], [mt])
                            else:
                                fw.op(fw.dve, lambda: nc.vector.tensor_tensor(
                                    out=self.DER[:, l, s, w, :], in0=self.modT[:, l, ga_ch:ga_ch + 8, s],
                                    in1=self.V[:, gc:gc + 8], op=ALU.mult), [mt, vt], [mt])
                fw.barrier()

    def gm(self, l, s, which, kc):
        return self.DER[:, l, s, 2 * which, kc:kc + 1]

    def gg(self, l, s, which, kc):
        return self.DER[:, l, s, 2 * which + 1, kc:kc + 1]

    def sh(self, l, s, which, kc):
        return self.modT[:, l, (0 if which == 0 else 24) + kc, s:s + 1]

    def alloc_norm_tiles(self, es, with_tmp=True):
        nc = self.nc
        sb = lambda name, shape, dt: es.enter_context(nc.sbuf_tensor(self.uniq(name), list(shape), dt))
        n = {}
        n["sq"] = (sb("n_sq", [128, 8, TT], BF16), Trk("sq"))
        n["rs"] = (sb("n_rs", [128, TT], F32), Trk("rs"))
        if with_tmp:
            n["tmp"] = (sb("n_tmp", [128, 8, TT], F32), [Trk(f"tmp{i}") for i in range(8)])
        n["ho"] = (sb("n_ho", [128, 8, TT], BF16), Trk("ho"))
        return n

    def sq_of(self, n, src, srct):
        nc, fw = self.nc, self.fw
        sq, sqt = n["sq"]
        fw.op(fw.act, lambda: nc.scalar.activation(out=sq[:].rearrange("p k t -> p (k t)"),
                                                   in_=src.rearrange("p k t -> p (k t)"), func=AF.Square),
              list(srct) if isinstance(srct, (list, tuple)) else [srct], [sqt])

    def rs_from_sq(self, n):
        nc, fw = self.nc, self.fw
        sq, sqt = n["sq"]; rs, rst = n["rs"]
        ps, pt = self.next_ps()
        for kc in range(8):
            fw.op(fw.pe, lambda: nc.tensor.matmul(ps[:, :], lhsT=self.onesb[:], rhs=sq[:, kc, :],
                                                  start=(kc == 0), stop=(kc == 7)), [sqt, self.ctrk], [pt], signal=(kc == 7))
        fw.op(fw.act, lambda: nc.scalar.activation(out=rs[:], in_=ps[:, :], func=AF.Sqrt, bias=EPS, scale=1.0),
              [pt], [rst])
        fw.op(fw.dve, lambda: nc.vector.reciprocal(out=rs[:], in_=rs[:]), [rst], [rst])

    def rstd(self, n, src, srct):
        self.sq_of(n, src, srct)
        self.rs_from_sq(n)

    def prenorm(self, n, src, srct, l, s, which, t0):
        nc, fw = self.nc, self.fw
        self.rstd(n, src, srct)
        self.prenorm_tail(n, src, srct, l, s, which, t0)

    def prenorm_tail(self, n, src, srct, l, s, which, t0):
        nc, fw = self.nc, self.fw
        rs, rst = n["rs"]; tmp, tmpt = n["tmp"]; ho, hot = n["ho"]
        for kc in range(8):
            eng = fw.dve
            fw.op(eng, lambda: eng.e.tensor_tensor(out=tmp[:, kc, :], in0=src[:, kc, :], in1=rs[:], op=ALU.mult),
                  [srct, rst], [tmpt[kc]])
        for kc in range(8):
            fw.op(fw.act, lambda: nc.scalar.activation(out=ho[:, kc, :], in_=tmp[:, kc, :], func=AF.Identity,
                                                       bias=self.sh(l, s, which, kc), scale=self.gm(l, s, which, kc)),
                  [tmpt[kc], self.mt], [hot])
        self.dump("d_tmp", tmp[:, 0, :], tmpt[0])
        self.dump("d_mod", self.modT[:, 0, :, :].rearrange("p a b -> p (a b)"), self.mt)
        self.dump("d_der", self.DER[:, 0, 0, :, :].rearrange("p a b -> p (a b)"), self.mt)
        self.dump("d_V", self.V[:, :], self.vt)
        fw.dma(fw.sp, self.S["hT"][s].rearrange("(kc p) t -> p kc t", p=128)[:, :, t0:t0 + TT], ho[:],
               reads=[hot], writes=[self.ST["hT"]], sem_on=hot, accumulate_w=True, defer=True)

    def epi_b(self, n, yt, ytt, xt, xtt, xn, xnt, l, s, which, t0, last):
        nc, fw = self.nc, self.fw
        self.rs_from_sq(n)
        rs, rst = n["rs"]
        for oc in range(8):
            fw.op(fw.dve, lambda: nc.vector.scalar_tensor_tensor(out=xn[:, oc, :], in0=yt[:, oc, :],
                                                                 scalar=self.gg(l, s, which, oc), in1=rs[:],
                                                                 op0=ALU.mult, op1=ALU.mult), [ytt[oc], rst, self.mt], [xnt])
            fw.op(fw.dve, lambda: nc.vector.tensor_tensor(out=xn[:, oc, :], in0=xn[:, oc, :], in1=xt[:, oc, :],
                                                          op=ALU.add), [xnt, xtt], [xnt])
        if not last:
            fw.dma(fw.sp, self.S["xres"][s].rearrange("(kc p) t -> p kc t", p=128)[:, :, t0:t0 + TT], xn[:],
                   reads=[xnt], writes=[self.xrt[s][t0 // TT]], sem_on=xnt, accumulate_w=True, defer=True)
            self.sq_of(n, xn[:], xnt)

    def epi_c(self, n, xn, xnt, l, s, which, t0, last):
        nc, fw = self.nc, self.fw
        if not last:
            self.rs_from_sq(n)
            if which == 0:
                self.prenorm_tail(n, xn[:], xnt, l, s, 1, t0)
            else:
                self.prenorm_tail(n, xn[:], xnt, l + 1, s, 0, t0)
        else:
            self.final_out(n, xn, xnt, s, t0)

    def epilogue(self, n, yt, ytt, xt, xtt, xn, xnt, l, s, which, t0, last):
        nc, fw = self.nc, self.fw
        self.rstd(n, yt[:], ytt)
        rs, rst = n["rs"]
        for oc in range(8):
            fw.op(fw.dve, lambda: nc.vector.scalar_tensor_tensor(out=xn[:, oc, :], in0=yt[:, oc, :],
                                                                 scalar=self.gg(l, s, which, oc), in1=rs[:],
                                                                 op0=ALU.mult, op1=ALU.mult), [ytt[oc], rst, self.mt], [xnt])
            fw.op(fw.dve, lambda: nc.vector.tensor_tensor(out=xn[:, oc, :], in0=xn[:, oc, :], in1=xt[:, oc, :],
                                                          op=ALU.add), [xnt, xtt], [xnt])
        if not last:
            fw.dma(fw.sp, self.S["xres"][s].rearrange("(kc p) t -> p kc t", p=128)[:, :, t0:t0 + TT], xn[:],
                   reads=[xnt], writes=[self.xrt[s][t0 // TT]], sem_on=xnt, accumulate_w=True, defer=True)
            if which == 0:
                self.prenorm(n, xn[:], xnt, l, s, 1, t0)
            else:
                self.prenorm(n, xn[:], xnt, l + 1, s, 0, t0)
        else:
            self.final_out(n, xn, xnt, s, t0)

    def final_out(self, n, xn, xnt, s, t0):
        nc, fw = self.nc, self.fw
        xo, xot = n["xo"]
        for sub in range(4):
            for half in range(2):
                ps, pt = self.next_ps()
                for k in range(4):
                    kc = half * 4 + k
                    fw.op(fw.pe, lambda: nc.tensor.transpose(ps[:, k * 128:(k + 1) * 128],
                                                             xn[:, kc, sub * 128:(sub + 1) * 128], self.identf[:]),
                          [xnt, self.ctrk], [pt], signal=(k == 3))
                eng = self.rot([fw.act, fw.dve])
                self.copy(eng, xo[:, sub, half * 512:(half + 1) * 512], ps[:, :], [pt], [xot])
        fw.dma(fw.sp, self.out[s, t0:t0 + TT, :].rearrange("(sub p) d -> p sub d", p=128), xo,
               reads=[xot], writes=[self.xrt[s][t0 // TT]], sem_on=xot, accumulate_w=True, defer=True)

    def phase_x0(self, s):
        import contextlib
        nc, fw, I = self.nc, self.fw, self.I
        with contextlib.ExitStack() as es:
            sb = lambda name, shape, dt: es.enter_context(nc.sbuf_tensor(self.uniq(name), list(shape), dt))
            n = self.alloc_norm_tiles(es)
            xins = [(sb(f"xin{i}", [128, 4, D], F32), Trk(f"xin{i}")) for i in range(2)]
            xts = [(sb(f"xt{i}", [128, 8, TT], F32), Trk(f"xt{i}")) for i in range(2)]
            for ti in range(NT):
                t0 = ti * TT
                xin, xint = xins[ti % 2]; xt, xtt = xts[ti % 2]
                fw.dma(fw.sp, xin[:], I["x"][s, t0:t0 + TT, :].rearrange("(sub p) d -> p sub d", p=128),
                       writes=[xint], sem_on=xint)
                for kc in range(8):
                    ps, pt = self.next_ps()
                    for sub in range(4):
                        fw.op(fw.pe, lambda: nc.tensor.transpose(ps[:, sub * 128:(sub + 1) * 128],
                                                                 xin[:, sub, kc * 128:(kc + 1) * 128], self.identf[:]),
                              [xint, self.ctrk], [pt], signal=(sub == 3))
                    eng = self.rot([fw.act, fw.dve])
                    self.copy(eng, xt[:, kc, :], ps[:, :], [pt], [xtt])
                fw.dma(fw.sp, self.S["xres"][s].rearrange("(kc p) t -> p kc t", p=128)[:, :, t0:t0 + TT], xt[:],
                       reads=[xtt], writes=[self.xrt[s][ti]], sem_on=xtt, accumulate_w=True, defer=True)
                self.prenorm(n, xt[:], xtt, 0, s, 0, t0)

    def load_cast(self, dst, dstt, src_ap, stg_list, k):
        nc, fw = self.nc, self.fw
        stg, stgt = stg_list[k % len(stg_list)]
        shp = list(src_ap.shape)
        view = stg
        fw.dma(fw.sp, view, src_ap, writes=[stgt], sem_on=stgt)
        eng = self.rot([fw.dve, fw.pool, fw.act])
        self.copy(eng, dst, view, [stgt], [dstt])

    def phase_p1(self, l, s):
        import contextlib
        nc, fw, I = self.nc, self.fw, self.I
        with contextlib.ExitStack() as es:
            sb = lambda name, shape, dt: es.enter_context(nc.sbuf_tensor(self.uniq(name), list(shape), dt))
            hfull = sb("hfull", [128, 8, T], BF16); hfts = [Trk(f"hfull{i}") for i in range(8)]
            win = sb("win", [128, 8, 2048], BF16); wints = [Trk(f"win{i}") for i in range(8)]
            stgs = [(sb(f"wstg{i}", [128, 2048], F32), Trk(f"wstg{i}")) for i in range(2)]
            raw = sb("raw", [128, T], F32); rawt = [Trk(f"raw{i}") for i in range(NT)]
            acc = sb("acc", [128, T], F32); acct = Trk("acc")
            cbs = [(sb(f"cb{i}", [128, T], BF16), Trk(f"cb{i}")) for i in range(2)]
            rbf = sb("rbf", [128, T], BF16); rbft = [Trk(f"rbf{i}") for i in range(NT)]
            wgb = sb("wgb", [128, 128], F32); wgbt = Trk("wgb")
            mab = sb("mab", [128, 4, 256], BF16); mabt = Trk("mab")
            for kc in range(8):
                fw.dma(fw.sp, hfull[:, kc, :], self.S["hT"][s][kc * 128:(kc + 1) * 128, :], reads=[self.ST["hT"]],
                       writes=[hfts[kc]], sem_on=hfts[kc])
                self.load_cast(win[:, kc, :], wints[kc], I["w_in"][l][kc * 128:(kc + 1) * 128, :],
                               [(stgs[0][0][:], stgs[0][1]), (stgs[1][0][:], stgs[1][1])], kc)
            for cc in range(4):
                fw.op(fw.dve, lambda: nc.vector.memset(wgb[:], 0.0), [], [wgbt])
                fw.dma(fw.sp, wgb[0:64, 0:64], I["fnet_w"][l, 2 * cc], writes=[wgbt], sem_on=wgbt)
                fw.dma(fw.sp, wgb[64:128, 64:128], I["fnet_w"][l, 2 * cc + 1], writes=[wgbt], sem_on=wgbt, accumulate_w=True)
                ps, pt = self.next_ps()
                fw.op(fw.pe, lambda: nc.tensor.matmul(ps[:, 0:128], lhsT=self.CSb[:, 0:128], rhs=wgb[:], start=True, stop=True),
                      [wgbt, self.ctrk], [pt], signal=False)
                fw.op(fw.pe, lambda: nc.tensor.matmul(ps[:, 128:256], lhsT=self.CSb[:, 128:256], rhs=wgb[:], start=True, stop=True),
                      [wgbt, self.ctrk], [pt])
                self.copy(fw.dve, mab[:, cc, :], ps[:, 0:256], [pt], [mabt])
            sw0 = self.col[("short_w", l)]; sb0 = self.col[("short_b", l)]
            V = self.V
            for q in range(16):
                for ti in range(NT):
                    t0 = ti * TT
                    ps, pt = self.next_ps()
                    for kc in range(8):
                        fw.op(fw.pe, lambda: nc.tensor.matmul(ps[:, :], lhsT=win[:, kc, q * 128:(q + 1) * 128],
                                                              rhs=hfull[:, kc, t0:t0 + TT], start=(kc == 0), stop=(kc == 7)),
                              [wints[kc], hfts[kc]], [pt], signal=(kc == 7))
                    if q < 12:
                        self.copy(fw.act, raw[:, t0:t0 + TT], ps[:, :], [pt], [rawt[ti]])
                    else:
                        self.copy(fw.act, rbf[:, t0:t0 + TT], ps[:, :], [pt], [rbft[ti]])
                if q < 12:
                    cb, cbt = cbs[q % 2]
                    w0 = V[:, sw0 + q:sw0 + q + 1]; w1 = V[:, sw0 + 12 + q:sw0 + 12 + q + 1]
                    w2 = V[:, sw0 + 24 + q:sw0 + 24 + q + 1]; bb = V[:, sb0 + q:sb0 + q + 1]
                    fw.op(fw.act, lambda: nc.scalar.activation(out=acc[:], in_=raw[:], func=AF.Identity, bias=bb, scale=w1),
                          rawt + [self.vt], [acct])
                    fw.op(fw.dve, lambda: nc.vector.scalar_tensor_tensor(out=acc[:, 1:T], in0=raw[:, 0:T - 1], scalar=w0,
                                                                         in1=acc[:, 1:T], op0=ALU.mult, op1=ALU.add),
                          rawt + [acct], [acct])
                    fw.op(fw.dve, lambda: nc.vector.scalar_tensor_tensor(out=cb[:, 0:T - 1], in0=raw[:, 1:T], scalar=w2,
                                                                         in1=acc[:, 0:T - 1], op0=ALU.mult, op1=ALU.add),
                          rawt + [acct], [cbt])
                    fw.op(fw.dve, lambda: nc.vector.tensor_copy(out=cb[:, T - 1:T], in_=acc[:, T - 1:T]), [acct], [cbt])
                    self.dump("d_raw", raw[:, 0:512], rawt[0])
                    self.dump("d_acc", acc[:, 0:512], acct)
                    self.dump("d_cb", cb[:, 0:512], cbt, BF16)
                    fw.dma(fw.sp, self.S["uv"][q], cb[:], reads=[cbt], writes=[self.ST["uv"]], sem_on=cbt, accumulate_w=True, defer=True)
                else:
                    cc = q - 12
                    for half in range(2):
                        cb, cbt = cbs[half]
                        for ti in range(NT):
                            t0 = ti * TT
                            ps, pt = self.next_ps()
                            fw.op(fw.pe, lambda: nc.tensor.matmul(ps[:, :], lhsT=mab[:, cc, half * 128:(half + 1) * 128],
                                                                  rhs=rbf[:, t0:t0 + TT], start=True, stop=True),
                                  [mabt, rbft[ti]], [pt])
                            eng = self.rot([fw.act, fw.dve])
                            self.copy(eng, cb[:, t0:t0 + TT], ps[:, :], [pt], [cbt])
                        fw.dma(fw.sp, self.S["ab"][half * 4 + cc], cb[:], reads=[cbt], writes=[self.ST["ab"]],
                               sem_on=cbt, accumulate_w=True, defer=True)

    def proj_epilogue_phase(self, l, s, which, src_name, nk, w_ap, last):
        import contextlib
        nc, fw, I = self.nc, self.fw, self.I
        with contextlib.ExitStack() as es:
            sb = lambda name, shape, dt: es.enter_context(nc.sbuf_tensor(self.uniq(name), list(shape), dt))
            n = self.alloc_norm_tiles(es, with_tmp=False)
            wsb = sb("wsb", [128, nk, D], BF16); wts = [Trk(f"wsb{i}") for i in range(nk)]
            stgs = [(sb(f"wstg{i}", [128, 1024], F32), Trk(f"wstg{i}")) for i in range(2)]
            ins = [(sb(f"pin{i}", [128, nk, TT], BF16), Trk(f"pin{i}")) for i in range(2)]
            xt = sb("xt", [128, 8, TT], F32); xtt = Trk("xt")
            yts = [(sb(f"yt{b_}", [128, 8, TT], F32), [Trk(f"yt{b_}_{i}") for i in range(8)]) for b_ in range(2)]
            xn = sb("xn", [128, 8, TT], F32); xnt = Trk("xn")
            if last:
                n["xo"] = (xt[:].rearrange("p k t -> p (k t)").rearrange("p (a d) -> p a d", a=4), xtt)
            for k1 in range(nk):
                stg, stgt = stgs[k1 % 2]
                fw.dma(fw.sp, stg[:], w_ap[k1 * 128:(k1 + 1) * 128, :], writes=[stgt], sem_on=stgt)
                eng = self.rot([fw.dve, fw.pool, fw.act])
                self.copy(eng, wsb[:, k1, :], stg[:], [stgt], [wts[k1]])
            src = self.S[src_name]

            def load_pin(ti):
                pin, pint = ins[ti % 2]
                fw.dma(fw.sp, pin[:], src.rearrange("(kc p) t -> p kc t", p=128)[:, :, ti * TT:(ti + 1) * TT],
                       reads=[self.ST[src_name]], writes=[pint], sem_on=pint)

            def mm(ti, ocs):
                pin, pint = ins[ti % 2]
                yt, ytt = yts[ti % 2]
                for oc in ocs:
                    ps, pt = self.next_ps()
                    for kc in range(nk):
                        fw.op(fw.pe, lambda: nc.tensor.matmul(ps[:, :], lhsT=wsb[:, kc, oc * 128:(oc + 1) * 128],
                                                              rhs=pin[:, kc, :], start=(kc == 0), stop=(kc == nk - 1)),
                              [wts[kc], pint], [pt], signal=(kc == nk - 1))
                    eng = self.rot([fw.act, fw.dve])
                    self.copy(eng, yt[:, oc, :], ps[:, :], [pt], [ytt[oc]])

            def mm_first():
                pin, pint = ins[0]
                yt, ytt = yts[0]
                banks = [self.next_ps() for _ in range(8)]
                for kc in range(nk):
                    for oc in range(8):
                        ps, pt = banks[oc]
                        fw.op(fw.pe, lambda: nc.tensor.matmul(ps[:, :], lhsT=wsb[:, kc, oc * 128:(oc + 1) * 128],
                                                              rhs=pin[:, kc, :], start=(kc == 0), stop=(kc == nk - 1)),
                              [wts[kc], pint], [pt], signal=(kc == nk - 1))
                for oc in range(8):
                    ps, pt = banks[oc]
                    eng = self.rot([fw.act, fw.dve])
                    self.copy(eng, yt[:, oc, :], ps[:, :], [pt], [ytt[oc]])

            load_pin(0)
            mm_first()
            for ti in range(NT):
                t0 = ti * TT
                nxt = ti + 1 < NT
                yt, ytt = yts[ti % 2]
                n["tmp"] = (yt, ytt)
                if nxt:
                    load_pin(ti + 1)
                fw.dma(fw.sp, xt[:], self.S["xres"][s].rearrange("(kc p) t -> p kc t", p=128)[:, :, t0:t0 + TT],
                       reads=[self.xrt[s][ti]], writes=[xtt], sem_on=xtt)
                self.sq_of(n, yt[:], ytt)
                if nxt:
                    mm(ti + 1, [0, 1])
                self.epi_b(n, yt, ytt, xt, xtt, xn, xnt, l, s, which, t0, last)
                if nxt:
                    mm(ti + 1, [2, 3, 4, 5])
                self.epi_c(n, xn, xnt, l, s, which, t0, last)
                if nxt:
                    mm(ti + 1, [6, 7])

    def phase_p4(self, l, s):
        self.proj_epilogue_phase(l, s, 0, "mixT", 8, self.I["w_out"][l], False)

    def phase_p6(self, l, s):
        self.proj_epilogue_phase(l, s, 1, "gT", NFF, self.I["w_down"][l], l == self.n_layers - 1)

    def phase_p5(self, l, s):
        import contextlib
        nc, fw, I = self.nc, self.fw, self.I
        V = self.V
        with contextlib.ExitStack() as es:
            sb = lambda name, shape, dt: es.enter_context(nc.sbuf_tensor(self.uniq(name), list(shape), dt))
            hfull = sb("hfull", [128, 8, T], BF16); hfts = [Trk(f"hfull{i}") for i in range(8)]
            wups = [(sb(f"wup{i}", [128, 2, 8, 128], BF16), Trk(f"wup{i}")) for i in range(2)]
            stgs = [(sb(f"ustg{i}", [128, 8, 128], F32), Trk(f"ustg{i}")) for i in range(2)]
            raws = [(sb(f"raw{h}", [128, T], F32), [Trk(f"raw{h}_{i}") for i in range(NT)]) for h in range(2)]
            accs = [(sb(f"acc{h}", [128, T], F32), Trk(f"acc{h}")) for h in range(2)]
            gos = [(sb(f"go{i}", [128, T], BF16), Trk(f"go{i}")) for i in range(2)]
            for kc in range(8):
                fw.dma(fw.sp, hfull[:, kc, :], self.S["hT"][s][kc * 128:(kc + 1) * 128, :], reads=[self.ST["hT"]],
                       writes=[hfts[kc]], sem_on=hfts[kc])
            dw0 = self.col[("dw_w", l)]; db0 = self.col[("dw_b", l)]
            kst = 0
            acct = [[Trk(f"acc{h}_{i}") for i in range(NT)] for h in range(2)]

            CW = 2

            def conv_s1(j, gi):
                tl = list(range(gi * CW, (gi + 1) * CW))
                t0 = tl[0] * TT; t1 = (tl[-1] + 1) * TT
                for h in range(2):
                    raw, rawt = raws[h]; acc = accs[h][0]
                    q = h * NFF + j
                    w0 = V[:, dw0 + q:dw0 + q + 1]; w1 = V[:, dw0 + 44 + q:dw0 + 44 + q + 1]
                    w2 = V[:, dw0 + 88 + q:dw0 + 88 + q + 1]; bb = V[:, db0 + q:db0 + q + 1]
                    own = [rawt[t] for t in tl]
                    nb = own + ([rawt[tl[0] - 1]] if tl[0] > 0 else []) + ([rawt[tl[-1] + 1]] if tl[-1] < NT - 1 else [])
                    fw.op(fw.act, lambda: nc.scalar.activation(out=acc[:, t0:t1], in_=raw[:, t0:t1], func=AF.Identity,
                                                               bias=bb, scale=w1), own + [self.vt], [acct[h][gi]])
                    a0 = max(t0, 1)
                    fw.op(fw.dve, lambda: nc.vector.scalar_tensor_tensor(out=acc[:, a0:t1], in0=raw[:, a0 - 1:t1 - 1], scalar=w0,
                                                                         in1=acc[:, a0:t1], op0=ALU.mult, op1=ALU.add),
                          nb + [acct[h][gi]], [acct[h][gi]])
                    b1 = min(t1, T - 1)
                    fw.op(fw.dve, lambda: nc.vector.scalar_tensor_tensor(out=acc[:, t0:b1], in0=raw[:, t0 + 1:b1 + 1], scalar=w2,
                                                                         in1=acc[:, t0:b1], op0=ALU.mult, op1=ALU.add),
                          nb + [acct[h][gi]], [acct[h][gi]])

            def conv_s2(j, gi, go, got):
                if gi != NT // CW - 1:
                    return
                a0_, a1_ = accs[0][0], accs[1][0]
                fw.op(fw.act, lambda: nc.scalar.activation(out=a0_[:, :], in_=a0_[:, :], func=AF.Gelu),
                      acct[0], acct[0])
                fw.op(fw.dve, lambda: nc.vector.tensor_tensor(out=go[:, :], in0=a0_[:, :], in1=a1_[:, :], op=ALU.mult),
                      acct[0] + acct[1], [got])

            for j in range(NFF):
                wup, wupt = wups[j % 2]
                go, got = gos[j % 2]
                for h in range(2):
                    c0 = h * DFF + j * 128
                    stg, stgt = stgs[h]
                    fw.dma(fw.sp, stg[:], I["w_up"][l][:, c0:c0 + 128].rearrange("(kc p) n -> p kc n", p=128),
                           writes=[stgt], sem_on=stgt)
                for h in range(2):
                    stg, stgt = stgs[h]
                    self.copy(fw.pool, wup[:, h, :, :], stg[:], [stgt], [wupt])
                for ti in range(NT):
                    t0 = ti * TT
                    for h in range(2):
                        raw, rawt = raws[h]
                        ps, pt = self.next_ps()
                        for kc in range(8):
                            fw.op(fw.pe, lambda: nc.tensor.matmul(ps[:, :], lhsT=wup[:, h, kc, :], rhs=hfull[:, kc, t0:t0 + TT],
                                                                  start=(kc == 0), stop=(kc == 7)), [wupt, hfts[kc]], [pt],
                                  signal=(kc == 7))
                        self.copy(fw.act, raw[:, t0:t0 + TT], ps[:, :], [pt], [rawt[ti]])
                    if ti % CW == CW - 1 and ti >= 2 * CW - 1:
                        g_ = ti // CW - 1
                        conv_s1(j, g_)
                        if g_ >= 1:
                            conv_s2(j, g_ - 1, go, got)
                NG = NT // CW
                conv_s1(j, NG - 1)
                conv_s2(j, NG - 2, go, got)
                conv_s2(j, NG - 1, go, got)
                fw.dma(fw.sp, self.S["gT"][j * 128:(j + 1) * 128, :], go[:], reads=[got], writes=[self.ST["gT"]],
                       sem_on=got, accumulate_w=True, defer=True)

    def fft_s1(self, u, ut, Ap, Apts, nk, Wsb, ncol, after=None):
        nc, fw = self.nc, self.fw
        per = 512 // ncol
        if after is not None:
            fw.order(fw.act, after); fw.order(fw.dve, after)
        for gi, c0 in enumerate(range(0, 128, per)):
            ps, pt = self.next_ps()
            for k in range(per):
                fw.op(fw.pe, lambda: nc.tensor.matmul(ps[:, k * ncol:(k + 1) * ncol], lhsT=u[0:nk, c0 + k, :],
                                                      rhs=Wsb[0:nk, :], start=True, stop=True),
                      [ut, self.ctrk], [pt], signal=(k == per - 1))
            eng = fw.act if gi % 2 == 0 else fw.dve
            self.copy(eng, Ap[:, c0:c0 + per, :].rearrange("p c k -> p (c k)"), ps[:, :], [pt], [Apts[gi]])

    def phase_conv(self, l, s):
        import contextlib
        nc, fw, I = self.nc, self.fw, self.I
        with contextlib.ExitStack() as es:
            sb = lambda name, shape, dt: es.enter_context(nc.sbuf_tensor(self.uniq(name), list(shape), dt))
            u = sb("fu", [32, 128, 128], BF16); ut = Trk("fu")
            Ap = sb("fAp", [128, 128, 128], BF16)
            Apts = [Trk(f"fAp{i}") for i in range(32)]; ApR = Trk("fApR")
            Bpts = [Trk(f"fBp{i}") for i in range(32)]; BpR = Trk("fBpR")
            P = sb("fP", [128, 64, 128], BF16); Pt = Trk("fP")
            Bt = sb("fBt", [128, 128, 128], BF16); Btts = [Trk(f"fBt{i}") for i in range(16)]
            m2s = [(sb(f"fm2_{i}", [128, 4096], BF16), Trk(f"fm2_{i}")) for i in range(2)]
            kas = [(sb(f"fka_{i}", [128, 2, 8, 128], BF16), Trk(f"fka_{i}")) for i in range(2)]
            t1 = sb("ft1", [128, 512], F32); t1t = Trk("ft1")
            t2 = sb("ft2", [128, 512], F32); t2t = Trk("ft2")
            gate = sb("fgate", [128, T], BF16); gatet = Trk("fgate")
            zo = sb("fzo", [128, T], BF16); zot = Trk("fzo")
            kst = 0
            for n in range(2):
                for cc in range(4):
                    if n == 0:
                        src = self.S["uv"][cc]; srct = self.ST["uv"]; g_ap = self.S["uv"][4 + cc]
                        dst = self.S["z2"][cc]; dstt = self.ST["z2"]
                    else:
                        src = self.S["z2"][cc]; srct = self.ST["z2"]; g_ap = self.S["uv"][8 + cc]
                        dst = self.S["mixT"][cc * 128:(cc + 1) * 128, :]; dstt = self.ST["mixT"]
                    fw.dma(fw.sp, u[:], src.rearrange("c (a i) -> a c i", a=32), reads=[srct], writes=[ut], sem_on=ut)
                    fw.dma(fw.sp, gate[:], g_ap, reads=[self.ST["uv"]], writes=[gatet], sem_on=gatet)
                    self.fft_s1(u, ut, Ap, Apts, 32, self.W1, 128, after=BpR)
                    for piece in range(8):
                        m2, m2t = m2s[kst % 2]; ka, kat = kas[kst % 2]; kst += 1
                        fw.dma(fw.sp, m2[:], I["M2s"][piece], writes=[m2t], sem_on=m2t)
                        fw.dma(fw.sp, ka[:].rearrange("p a f c -> p a (f c)"),
                               self.S["kab"][n, cc][:, :, piece * 1024:(piece + 1) * 1024].rearrange("a p n -> p a n"),
                               reads=[self.ST["kab"]], writes=[kat], sem_on=kat)
                        for g in range(2):
                            psx, ptx = self.next_ps(); pss, pts = self.next_ps()
                            for k in range(4):
                                fl = g * 4 + k; flo = piece * 8 + fl
                                for var, (ps, pt) in enumerate(((psx, ptx), (pss, pts))):
                                    for rip in range(2):
                                        o = ((fl * 2 + rip) * 2 + var) * 128
                                        fw.op(fw.pe, lambda: nc.tensor.matmul(ps[:, k * 128:(k + 1) * 128], lhsT=m2[:, o:o + 128],
                                                                              rhs=Ap[:, :, rip * 64 + flo],
                                                                              start=(rip == 0), stop=(rip == 1)),
                                              [m2t, ApR] + Apts, [pt], signal=(k == 3 and rip == 1))
                            kav = ka[:, 0, g * 4:(g + 1) * 4, :].rearrange("p f c -> p (f c)")
                            kbv = ka[:, 1, g * 4:(g + 1) * 4, :].rearrange("p f c -> p (f c)")
                            fw.op(fw.dve, lambda: nc.vector.tensor_tensor(out=t1[:], in0=psx[:, :], in1=kav, op=ALU.mult),
                                  [ptx, kat], [t1t])
                            fw.op(fw.dve, lambda: nc.vector.tensor_tensor(out=t2[:], in0=pss[:, :], in1=kbv, op=ALU.mult),
                                  [pts, kat], [t2t])
                            f0 = piece * 8 + g * 4
                            fw.op(fw.pool, lambda: nc.gpsimd.tensor_tensor(out=P[:, f0:f0 + 4, :].rearrange("p f c -> p (f c)"),
                                                                           in0=t1[:], in1=t2[:], op=ALU.add), [t1t, t2t], [Pt])
                    Bp = Ap[:].rearrange("p a b -> p (a b)").rearrange("p (k c) -> p k c", k=128)
                    fw.order(fw.act, ApR); fw.order(fw.dve, ApR)
                    gi = 0
                    for piece in range(8):
                        m2, m2t = m2s[kst % 2]; kst += 1
                        fw.dma(fw.sp, m2[:, 0:2048], I["G1s"][piece], writes=[m2t], sem_on=m2t)
                        for g in range(2):
                            for ri in range(2):
                                ps, pt = self.next_ps()
                                for k in range(4):
                                    fl = g * 4 + k; flo = piece * 8 + fl
                                    o = (fl * 2 + ri) * 128
                                    fw.op(fw.pe, lambda: nc.tensor.matmul(ps[:, k * 128:(k + 1) * 128], lhsT=m2[:, o:o + 128],
                                                                          rhs=P[:, flo, :], start=True, stop=True),
                                          [m2t, Pt], [pt], signal=(k == 3))
                                col0 = ri * 64 + piece * 8 + g * 4
                                eng = fw.act if gi % 2 == 0 else fw.dve
                                self.copy(eng, Bp[:, col0:col0 + 4, :].rearrange("p k c -> p (k c)"), ps[:, :], [pt], [Bpts[gi]])
                                gi += 1
                    for gi, c0 in enumerate(range(0, 128, 8)):
                        ps, pt = self.next_ps()
                        psb = ps[:, :].bitcast(BF16)
                        for k in range(8):
                            fw.op(fw.pe, lambda: nc.tensor.transpose(psb[:, k * 128:(k + 1) * 128], Bp[:, :, c0 + k], self.identb[:]),
                                  [BpR, self.ctrk] + Bpts, [pt], signal=(k == 7))
                        eng = fw.act if gi % 2 == 0 else fw.dve
                        self.copy(eng, Bt[:, c0:c0 + 8, :].rearrange("p c i -> p (c i)"), psb[:, 0:1024], [pt], [Btts[gi]])
                    zo3 = zo[:].rearrange("p (a i) -> p a i", a=32)
                    g3 = gate[:].rearrange("p (a i) -> p a i", a=32)
                    for i0 in range(0, 128, 16):
                        ps, pt = self.next_ps()
                        for k in range(16):
                            fw.op(fw.pe, lambda: nc.tensor.matmul(ps[:, k * 32:(k + 1) * 32], lhsT=Bt[:, :, i0 + k], rhs=self.W2[:, :],
                                                                  start=True, stop=True), Btts + [self.ctrk], [pt], signal=(k == 15))
                        fw.op(fw.dve, lambda: nc.vector.tensor_tensor(out=zo3[:, :, i0:i0 + 16],
                                                                      in0=ps[:, :].rearrange("p (k a) -> p a k", k=16),
                                                                      in1=g3[:, :, i0:i0 + 16], op=ALU.mult), [pt, gatet], [zot])
                    fw.dma(fw.sp, dst, zo[:], reads=[zot], writes=[dstt], sem_on=zot, accumulate_w=True, defer=True)

    def phase_fnet(self, l, s):
        import contextlib
        nc, fw, I = self.nc, self.fw, self.I
        with contextlib.ExitStack() as es:
            sb = lambda name, shape, dt: es.enter_context(nc.sbuf_tensor(self.uniq(name), list(shape), dt))
            us = [(sb(f"nu{i}", [64, 128, 128], BF16), Trk(f"nu{i}")) for i in range(2)]
            Ap = sb("nAp", [128, 128, 64], BF16); Apts = [Trk(f"nAp{i}") for i in range(16)]
            yfs = [(sb(f"nyf{i}", [128, T], BF16), Trk(f"nyf{i}")) for i in range(2)]
            for cc in range(4):
                u, ut = us[cc % 2]; yf, yft = yfs[cc % 2]
                fw.dma(fw.sp, u[0:32], self.S["ab"][cc].rearrange("c (a i) -> a c i", a=32), reads=[self.ST["ab"]],
                       writes=[ut], sem_on=ut)
                fw.dma(fw.sp, u[32:64], self.S["ab"][4 + cc].rearrange("c (a i) -> a c i", a=32), reads=[self.ST["ab"]],
                       writes=[ut], sem_on=ut, accumulate_w=True)
                self.fft_s1(u, ut, Ap, Apts, 64, self.W1f, 64)
                yf3 = yf[:].rearrange("p (f k) -> p f k", k=32)
                for g in range(8):
                    ps, pt = self.next_ps()
                    for k in range(4):
                        flo = g * 4 + k
                        for rip in range(2):
                            o = (flo * 2 + rip) * 128
                            fw.op(fw.pe, lambda: nc.tensor.matmul(ps[:, k * 128:(k + 1) * 128], lhsT=Ap[:, :, rip * 32 + flo],
                                                                  rhs=self.M2f[:, o:o + 128], start=(rip == 0), stop=(rip == 1)),
                                  Apts + [self.ctrk], [pt], signal=(k == 3 and rip == 1))
                    eng = self.rot([fw.act, fw.dve])
                    self.copy(eng, yf3[:, :, g * 4:(g + 1) * 4].rearrange("p f k -> p k f"),
                              ps[:, :].rearrange("p (k f) -> p k f", k=4), [pt], [yft])
                fw.dma(fw.sp, self.S["mixT"][512 + cc * 128:512 + (cc + 1) * 128, :], yf[:], reads=[yft],
                       writes=[self.ST["mixT"]], sem_on=yft, accumulate_w=True, defer=True)

    def phase_filter(self, l):
        import contextlib
        nc, fw, I = self.nc, self.fw, self.I
        V = self.V
        TWO_PI = float(2.0 * np.pi)
        with contextlib.ExitStack() as es:
            sb = lambda name, shape, dt: es.enter_context(nc.sbuf_tensor(self.uniq(name), list(shape), dt))
            w1 = sb("g_w1", [33, 64], F32); w2 = sb("g_w2", [64, 64], F32); w3 = sb("g_w3", [64, 2048], F32)
            zT = sb("g_zT", [33, T], F32); tb = sb("g_tb", [128, T], F32)
            wt = Trk("g_w")
            h1 = sb("g_h1", [64, T], F32); h1t = Trk("g_h1")
            h2 = sb("g_h2", [64, T], F32); h2t = Trk("g_h2")
            arg = sb("g_arg", [64, TT], F32); argt = Trk("g_arg")
            ki = sb("g_ki", [64, TT], I32); kit = Trk("g_ki")
            kr = sb("g_kr", [64, TT], F32); krt = Trk("g_kr")
            frb = sb("g_frb", [64, 2], F32); frbt = Trk("g_frb")
            dec = sb("g_dec", [128, T], F32); dect = Trk("g_dec")
            kf = sb("g_kf", [128, T], F32); kft = Trk("g_kf")
            kb = sb("g_kb", [128, T], F32); kbt = Trk("g_kb")
            sm = sb("g_sm", [128, 4], F32); smt = Trk("g_sm")
            sbf = sb("g_sbf", [128, T], BF16); sbft = Trk("g_sbf")
            dbf = sb("g_dbf", [128, T], BF16); dbft = Trk("g_dbf")
            for dst, src in ((w1, I["filt_w1"][l]), (w2, I["filt_w2"][l]), (w3, I["filt_w3"][l]), (zT, I["zT"]), (tb, I["tb"])):
                fw.dma(fw.sp, dst[:], src, writes=[wt], sem_on=wt, accumulate_w=True)
            cb1 = self.col[("filt_b1", l)]; cb2 = self.col[("filt_b2", l)]; cfr = self.col[("filt_freq", l)]
            hd0 = self.col[("hyena_d", l)]
            fr = V[0:64, cfr:cfr + 1]
            fw.op(fw.dve, lambda: nc.vector.tensor_tensor(out=frb[:, 0:1], in0=V[0:64, cb1:cb1 + 1], in1=fr, op=ALU.mult),
                  [self.vt], [frbt])
            fw.op(fw.dve, lambda: nc.vector.tensor_tensor(out=frb[:, 1:2], in0=V[0:64, cb2:cb2 + 1], in1=fr, op=ALU.mult),
                  [self.vt], [frbt])
            for layer_i, (wsb, kdim, src, srct, dst, dstt) in enumerate(((w1, 33, zT, wt, h1, h1t), (w2, 64, h1, h1t, h2, h2t))):
                for ti in range(NT):
                    t0 = ti * TT
                    ps, pt = self.next_ps()
                    fw.op(fw.pe, lambda: nc.tensor.matmul(ps[0:64, :], lhsT=wsb[0:kdim, :], rhs=src[0:kdim, t0:t0 + TT],
                                                          start=True, stop=True), [wt, srct], [pt])
                    fw.op(fw.act, lambda: nc.scalar.activation(out=arg[:], in_=ps[0:64, :], func=AF.Identity,
                                                               bias=frb[:, layer_i:layer_i + 1], scale=fr), [pt, frbt, self.vt], [argt])
                    fw.op(fw.dve, lambda: nc.vector.tensor_scalar(out=ki[:], in0=arg[:], scalar1=1.0 / TWO_PI, scalar2=None,
                                                                  op0=ALU.mult), [argt], [kit])
                    fw.op(fw.dve, lambda: nc.vector.tensor_copy(out=kr[:], in_=ki[:]), [kit], [krt])
                    fw.op(fw.dve, lambda: nc.vector.scalar_tensor_tensor(out=kr[:], in0=kr[:], scalar=-TWO_PI, in1=arg[:],
                                                                         op0=ALU.mult, op1=ALU.add), [krt, argt], [krt])
                    fw.op(fw.dve, lambda: nc.vector.tensor_scalar(out=kr[:], in0=kr[:], scalar1=-3.141592, scalar2=3.141592,
                                                                  op0=ALU.max, op1=ALU.min), [krt], [krt])
                    fw.op(fw.act, lambda: nc.scalar.activation(out=dst[:, t0:t0 + TT], in_=kr[:], func=AF.Sin), [krt], [dstt])
            for cc in range(4):
                fw.op(fw.act, lambda: nc.scalar.activation(out=dec[:], in_=tb[:], func=AF.Exp, scale=self.dsc[:, cc:cc + 1]),
                      [wt, self.ctrk], [dect])
                for o in range(2):
                    for dirn, (kt, ktt) in enumerate(((kf, kft), (kb, kbt))):
                        q = o * 8 + dirn * 4 + cc
                        for ti in range(NT):
                            t0 = ti * TT
                            ps, pt = self.next_ps()
                            fw.op(fw.pe, lambda: nc.tensor.matmul(ps[:, :], lhsT=w3[0:64, q * 128:(q + 1) * 128],
                                                                  rhs=h2[0:64, t0:t0 + TT], start=True, stop=True), [wt, h2t], [pt])
                            fw.op(fw.dve, lambda: nc.vector.tensor_tensor(out=kt[:, t0:t0 + TT], in0=ps[:, :],
                                                                          in1=dec[:, t0:t0 + TT], op=ALU.mult), [pt, dect], [ktt])
                    fw.op(fw.dve, lambda: nc.vector.memset(kb[:, 0:1], 0.0), [], [kbt])
                    fw.op(fw.dve, lambda: nc.vector.tensor_reduce(out=sm[:, 0:1], in_=kf[:], axis=mybir.AxisListType.X, op=ALU.add,
                                                                  apply_absolute_value=True), [kft], [smt])
                    fw.op(fw.dve, lambda: nc.vector.tensor_reduce(out=sm[:, 1:2], in_=kb[:], axis=mybir.AxisListType.X, op=ALU.add,
                                                                  apply_absolute_value=True), [kbt], [smt])
                    fw.op(fw.dve, lambda: nc.vector.scalar_tensor_tensor(out=sm[:, 2:3], in0=sm[:, 0:1], scalar=1e-6, in1=sm[:, 1:2],
                                                                         op0=ALU.add, op1=ALU.add), [smt], [smt])
                    fw.op(fw.dve, lambda: nc.vector.reciprocal(out=sm[:, 3:4], in_=sm[:, 2:3]), [smt], [smt])
                    fw.op(fw.act, lambda: nc.scalar.activation(out=kf[:], in_=kf[:], func=AF.Copy, scale=sm[:, 3:4]),
                          [kft, smt], [kft])
                    fw.op(fw.dve, lambda: nc.vector.tensor_scalar(out=kb[:], in0=kb[:], scalar1=sm[:, 3:4], scalar2=None,
                                                                  op0=ALU.mult), [kbt, smt], [kbt])
                    dcol = hd0 + o * 4 + cc
                    fw.op(fw.dve, lambda: nc.vector.tensor_tensor(out=kf[:, 0:1], in0=kf[:, 0:1], in1=V[:, dcol:dcol + 1],
                                                                  op=ALU.add), [kft, self.vt], [kft])
                    fw.op(fw.dve, lambda: nc.vector.tensor_tensor(out=sbf[:], in0=kf[:], in1=kb[:], op=ALU.add),
                          [kft, kbt], [sbft])
                    fw.op(fw.pool, lambda: nc.gpsimd.tensor_tensor(out=dbf[:], in0=kf[:], in1=kb[:], op=ALU.subtract),
                          [kft, kbt], [dbft])
                    fw.dma(fw.sp, self.S["sd"][o, cc, 0], sbf[:], reads=[sbft], writes=[self.ST["sd"]], sem_on=sbft, accumulate_w=True, defer=True)
                    fw.dma(fw.sp, self.S["sd"][o, cc, 1], dbf[:], reads=[dbft], writes=[self.ST["sd"]], sem_on=dbft, accumulate_w=True, defer=True)
        fw.barrier(); fw.recycle_dma()
        with contextlib.ExitStack() as es:
            sb = lambda name, shape, dt: es.enter_context(nc.sbuf_tensor(self.uniq(name), list(shape), dt))
            us = [(sb(f"k_u{i}", [32, 128, 128], BF16), Trk(f"k_u{i}")) for i in range(2)]
            Aps = [(sb(f"k_Ap{i}", [128, 128, 128], BF16), [Trk(f"k_Ap{i}_{j}") for j in range(32)]) for i in range(2)]
            mabs = [(sb(f"k_mab{i}", [128, 4096], BF16), Trk(f"k_mab{i}")) for i in range(2)]
            kos = [(sb(f"k_ko{i}", [128, 2, 8, 128], BF16), Trk(f"k_ko{i}")) for i in range(2)]
            kst = 0
            for o in range(2):
                for cc in range(4):
                    for sd in range(2):
                        fw.dma(fw.sp, us[sd][0][:], self.S["sd"][o, cc, sd].rearrange("c (a i) -> a c i", a=32),
                               reads=[self.ST["sd"]], writes=[us[sd][1]], sem_on=us[sd][1])
                        self.fft_s1(us[sd][0], us[sd][1], Aps[sd][0], Aps[sd][1], 32, self.W1, 128)
                    for piece in range(8):
                        mab, mabt = mabs[kst % 2]; ko, kot = kos[kst % 2]; kst += 1
                        fw.dma(fw.sp, mab[:], I["MABs"][piece], writes=[mabt], sem_on=mabt)
                        for g in range(2):
                            for ab in range(2):
                                Ap, Apt = Aps[ab]
                                ps, pt = self.next_ps()
                                for k in range(4):
                                    fl = g * 4 + k; flo = piece * 8 + fl
                                    for rip in range(2):
                                        off = ((fl * 2 + ab) * 2 + rip) * 128
                                        fw.op(fw.pe, lambda: nc.tensor.matmul(ps[:, k * 128:(k + 1) * 128], lhsT=mab[:, off:off + 128],
                                                                              rhs=Ap[:, :, rip * 64 + flo],
                                                                              start=(rip == 0), stop=(rip == 1)),
                                              [mabt] + Apt, [pt], signal=(k == 3 and rip == 1))
                                eng = self.rot([fw.act, fw.dve])
                                self.copy(eng, ko[:, ab, g * 4:(g + 1) * 4, :].rearrange("p f c -> p (f c)"), ps[:, :], [pt], [kot])
                        fw.dma(fw.sp, self.S["kab"][o, cc][:, :, piece * 1024:(piece + 1) * 1024].rearrange("a p n -> p a n"),
                               ko[:].rearrange("p a f c -> p a (f c)"), reads=[kot], writes=[self.ST["kab"]], sem_on=kot,
                               accumulate_w=True, defer=True)


_CONSTS = None


def kernel(**inputs):
    global _CONSTS
    if _CONSTS is None:
        _CONSTS = make_consts()
    x = np.ascontiguousarray(np.asarray(inputs["x"], dtype=np.float32))
    c = np.ascontiguousarray(np.asarray(inputs["c"], dtype=np.float32))
    weights = {n: np.ascontiguousarray(np.asarray(inputs[n], dtype=np.float32)) for n in WEIGHT_NAMES}
    n_cores = 8
    nc = Builder().build()
    in_maps = []
    for i in range(n_cores):
        m = {"x": x[i * NSEQ:(i + 1) * NSEQ], "c": c[i * NSEQ:(i + 1) * NSEQ]}
        m.update(weights)
        m.update(_CONSTS)
        in_maps.append(m)
    res = run_bass_kernel_spmd(nc, in_maps, core_ids=list(range(n_cores)))
    out = np.concatenate([np.asarray(r["out"], dtype=np.float32) for r in res.results], axis=0)
    return out
```

```python
import numpy as np
import ml_dtypes
import concourse.bass as bass
import concourse.mybir as mybir
from concourse.bass_utils import run_bass_kernel_spmd

F32 = mybir.dt.float32
BF16 = mybir.dt.bfloat16
I32 = mybir.dt.int32
ALU = mybir.AluOpType
AF = mybir.ActivationFunctionType

D = 1024
T = 4096
DEPTH = 4
NSEQ = 2
DH = 512
DFF = 2816
NFF = 22
TT = 512
NT = T // TT
EPS = 1e-6
N2 = 8192


def _hy_consts():
    a = np.arange(32)[:, None]; flo = np.arange(64)[None, :]
    phi = 2 * np.pi * a * flo / 64.0
    W1 = np.concatenate([np.cos(phi), -np.sin(phi)], axis=1)
    W2 = np.concatenate([np.cos(phi).T, -np.sin(phi).T], axis=0)
    i = np.arange(128)[:, None]; fhi = np.arange(64)[None, :]
    M2 = np.zeros((64, 2, 2, 128, 128))
    MAB = np.zeros((64, 2, 2, 128, 128))
    G1 = np.zeros((64, 2, 128, 128))
    sgn = (-1.0) ** np.arange(128)
    for fl in range(64):
        f = fl + 64 * fhi
        th = 2 * np.pi * i * f / N2
        c, s = np.cos(th), np.sin(th)
        m_re = np.concatenate([c, -s], axis=1)
        m_im = np.concatenate([s, c], axis=1)
        ma_re = np.concatenate([c, c], axis=1); ma_im = np.concatenate([s, s], axis=1)
        mb_re = np.concatenate([s, -s], axis=1); mb_im = np.concatenate([-c, c], axis=1)
        if fl == 0:
            m_re[:, 64] = sgn; m_im[:, 64] = 0.0
            ma_re[:, 64] = sgn; ma_im[:, 64] = 0.0
            mb_re[:, 64] = 0; mb_im[:, 64] = 0; mb_re[:, 0] = 0; mb_im[:, 0] = 0
        M2[fl, 0, 0] = m_re; M2[fl, 1, 0] = m_im
        M2[fl, 0, 1] = np.roll(m_re, -64, axis=1); M2[fl, 1, 1] = np.roll(m_im, -64, axis=1)
        MAB[fl, 0, 0] = ma_re; MAB[fl, 0, 1] = ma_im; MAB[fl, 1, 0] = mb_re; MAB[fl, 1, 1] = mb_im
        wgt = np.where(f == 0, 1.0, 2.0) / N2
        g_re = np.concatenate([(wgt * c).T, (-wgt * s).T], axis=0)
        g_im = np.concatenate([(wgt * s).T, (wgt * c).T], axis=0)
        if fl == 0:
            g_re[64, :] = sgn / N2; g_im[64, :] = 0.0
        G1[fl, 0] = g_re; G1[fl, 1] = g_im
    bf = ml_dtypes.bfloat16
    M2s = M2.reshape(8, 8, 2, 2, 128, 128).transpose(0, 4, 1, 2, 3, 5).reshape(8, 128, 8 * 4 * 128)
    MABs = MAB.reshape(8, 8, 2, 2, 128, 128).transpose(0, 4, 1, 2, 3, 5).reshape(8, 128, 8 * 4 * 128)
    G1s = G1.reshape(8, 8, 2, 128, 128).transpose(0, 3, 1, 2, 4).reshape(8, 128, 8 * 2 * 128)
    return dict(W1=W1.astype(bf), W2=W2.astype(bf), M2s=np.ascontiguousarray(M2s).astype(bf),
                MABs=np.ascontiguousarray(MABs).astype(bf), G1s=np.ascontiguousarray(G1s).astype(bf))


def _fn_consts():
    a = np.arange(32)[:, None]; flo = np.arange(32)[None, :]
    phi = 2 * np.pi * a * flo / 32.0
    c, s = np.cos(phi), np.sin(phi)
    W1f = np.concatenate([np.concatenate([c, -s], axis=1), np.concatenate([-s, -c], axis=1)], axis=0)
    i = np.arange(128)[:, None]; fhi = np.arange(128)[None, :]
    M2f = np.zeros((32, 2, 128, 128))
    for fl in range(32):
        f = fl + 32 * fhi
        th = 2 * np.pi * i * f / T
        M2f[fl, 0] = np.cos(th) / 512.0; M2f[fl, 1] = np.sin(th) / 512.0
    bf = ml_dtypes.bfloat16
    M2fs = M2f.transpose(2, 0, 1, 3).reshape(128, 32 * 2 * 128)
    j = np.arange(64)
    cc = np.cos(2 * np.pi * np.outer(j, j) / 64.0); sc = np.sin(2 * np.pi * np.outer(j, j) / 64.0)
    Cb = np.zeros((128, 128)); Sb = np.zeros((128, 128))
    Cb[:64, :64] = cc; Cb[64:, 64:] = cc; Sb[:64, :64] = sc; Sb[64:, 64:] = sc
    return dict(W1f=W1f.astype(bf), M2fs=np.ascontiguousarray(M2fs).astype(bf),
                CSb=np.concatenate([Cb, Sb], axis=1).astype(np.float32))


def _filter_consts():
    pos = np.arange(T, dtype=np.float32)
    t = (pos / np.float32(T - 1)).astype(np.float32)
    bands = np.linspace(1e-4, 15, 16, dtype=np.float32)
    ang = (np.float32(2.0 * np.pi / T) * pos[:, None] * bands[None, :]).astype(np.float32)
    z = np.concatenate([t[:, None], np.cos(ang), -np.sin(ang)], axis=-1).astype(np.float32)
    min_decay = np.log(1e-2) / 1.5; max_decay = np.log(1e-2) / 0.3
    deltas = np.linspace(min_decay, max_decay, DH, dtype=np.float32)
    dsc = (-np.abs(deltas)).astype(np.float32).reshape(4, 128).T
    tb = np.broadcast_to(t[None, :], (128, T)).astype(np.float32)
    return dict(zT=np.ascontiguousarray(z.T), dsc=np.ascontiguousarray(dsc), tb=np.ascontiguousarray(tb))


class Trk:
    __slots__ = ("w", "r", "name")

    def __init__(self, name=""):
        self.w = {}
        self.r = {}
        self.name = name


class Eng:
    def __init__(self, fw, name, eng, sem):
        self.fw = fw; self.name = name; self.e = eng; self.sem = sem
        self.cnt = 0
        self.seen = {}

    def wait(self, sem, val):
        k = id(sem)
        if self.seen.get(k, 0) >= val:
            return
        if sem is self.sem and self.name == "pe":
            return
        self.e.wait_ge(sem, val)
        self.seen[k] = val


class FW:
    def __init__(self, nc, sems):
        self.nc = nc
        self._sems = list(sems)
        self.pe = Eng(self, "pe", nc.tensor, self._sems.pop())
        self.act = Eng(self, "act", nc.scalar, self._sems.pop())
        self.dve = Eng(self, "dve", nc.vector, self._sems.pop())
        self.pool = Eng(self, "pool", nc.gpsimd, self._sems.pop())
        self.sp = Eng(self, "sp", nc.sync, self._sems.pop())
        self.engs = [self.pe, self.act, self.dve, self.pool, self.sp]
        self.dma_sems = {}
        self.all_dma = []
        self.rr = 0
        self.npe = 0
        self.pending = []
        self.loads_since = False

    def _deps(self, eng, reads, writes):
        for t in reads:
            for (sem, val) in t.w.values():
                eng.wait(sem, val)
        for t in writes:
            for (sem, val) in t.w.values():
                eng.wait(sem, val)
            for (sem, val) in t.r.values():
                eng.wait(sem, val)

    def _conflict(self, reads, writes):
        for p in self.pending:
            pr, pw = p[3], p[4]
            for t in writes:
                if any(t is x for x in pr) or any(t is x for x in pw):
                    return True
            for t in reads:
                if any(t is x for x in pw):
                    return True
        return False

    def order(self, eng, trk):
        for (sem, val) in list(trk.w.values()) + list(trk.r.values()):
            if sem is not eng.sem:
                eng.wait(sem, val)

    def flush(self):
        pend, self.pending = self.pending, []
        for (q, out, in_, rd, wr, sem_on, acc, kw) in pend:
            self.dma(q, out, in_, reads=rd, writes=wr, sem_on=sem_on, accumulate_w=acc, _is_flush=True, **kw)
        self.loads_since = False

    def op(self, eng, fn, reads=(), writes=(), signal=True):
        if self.pending and (self.loads_since or self._conflict(reads, writes)):
            self.flush()
        self._deps(eng, reads, writes)
        ins = fn()
        if eng is self.pe:
            self.npe += 1
        if signal:
            eng.cnt += 1
            ins.then_inc(eng.sem, 1)
            mark = (eng.sem, eng.cnt)
        else:
            mark = (eng.sem, eng.cnt + 1)
        k = id(eng.sem)
        for t in writes:
            t.w = {k: mark}; t.r = {}
        for t in reads:
            t.r[k] = mark
        return ins

    def _dsem(self, trk):
        ent = self.dma_sems.get(id(trk))
        if ent is None:
            ent = [self._sems.pop(), 0, trk]
            self.dma_sems[id(trk)] = ent
            self.all_dma.append(ent)
        return ent

    def dma(self, q, out, in_, reads=(), writes=(), sem_on=None, accumulate_w=False, defer=False, _is_flush=False, **kw):
        rd = list(reads); wr = list(writes)
        if defer:
            self.pending.append((q, out, in_, rd, wr, sem_on, accumulate_w, kw))
            self.loads_since = False
            return None
        if not _is_flush:
            if self.pending and self._conflict(rd, wr):
                self.flush()
            self.loads_since = True
        if accumulate_w:
            for t in rd:
                for (sem, val) in t.w.values():
                    q.wait(sem, val)
            eng_ids = {id(e.sem) for e in self.engs}
            for t in wr:
                for (sem, val) in t.r.values():
                    q.wait(sem, val)
                for (sem, val) in t.w.values():
                    if id(sem) in eng_ids:
                        q.wait(sem, val)
        else:
            self._deps(q, rd, wr)
        ent = self._dsem(sem_on)
        ent[1] += 16
        ins = q.e.dma_start(out=out, in_=in_, **kw)
        ins.then_inc(ent[0], 16)
        mark = (ent[0], ent[1])
        k = id(ent[0])
        for t in wr:
            if accumulate_w:
                t.w[k] = mark
            else:
                t.w = {k: mark}; t.r = {}
        for t in rd:
            t.r[k] = mark
        return ins

    def barrier(self):
        self.flush()
        sp = self.sp
        for e in self.engs:
            if e is not sp and e.cnt > 0:
                sp.wait(e.sem, e.cnt)
        for ent in self.all_dma:
            if ent[1] > 0:
                sp.wait(ent[0], ent[1])
        sp.cnt += 1
        sp.e.sem_inc(sp.sem, 1)
        for e in self.engs:
            if e is not sp:
                e.wait(sp.sem, sp.cnt)
        for e in self.engs:
            for o in self.engs:
                e.seen[id(o.sem)] = max(e.seen.get(id(o.sem), 0), o.cnt if o is not sp else sp.cnt)
            for ent in self.all_dma:
                e.seen[id(ent[0])] = max(e.seen.get(id(ent[0]), 0), ent[1])

    def finish(self):
        self.flush()
        sp = self.sp
        for e in self.engs:
            if e is not sp and e.cnt > 0:
                sp.wait(e.sem, e.cnt)
        for ent in self.all_dma:
            if ent[1] > 0:
                sp.wait(ent[0], ent[1])

    def recycle_dma(self):
        self.free_dma = getattr(self, "free_dma", []) + [e for e in self.all_dma if e[2] is not None]
        for e in self.free_dma:
            e[2] = None
        self.dma_sems = {}


def _dsem_recycling(self, trk):
    ent = self.dma_sems.get(id(trk))
    if ent is None:
        fl = getattr(self, "free_dma", [])
        if fl:
            ent = fl.pop()
            ent[2] = trk
        else:
            ent = [self._sems.pop(), 0, trk]
            self.all_dma.append(ent)
        self.dma_sems[id(trk)] = ent
    return ent


FW._dsem = _dsem_recycling


WEIGHT_NAMES = ["ada_w", "ada_b", "g_mix_pre", "g_mix_post", "w_in", "short_w", "short_b", "filt_w1", "filt_b1",
                "filt_w2", "filt_b2", "filt_freq", "filt_w3", "hyena_d", "fnet_w", "w_out", "g_ffn_pre",
                "g_ffn_post", "w_up", "dw_w", "dw_b", "w_down"]
WEIGHT_SHAPES = {
    "ada_w": (DEPTH, D, 6 * D), "ada_b": (DEPTH, 6 * D), "g_mix_pre": (DEPTH, D), "g_mix_post": (DEPTH, D),
    "w_in": (DEPTH, D, 2048), "short_w": (DEPTH, 3, 1536), "short_b": (DEPTH, 1536),
    "filt_w1": (DEPTH, 33, 64), "filt_b1": (DEPTH, 64), "filt_w2": (DEPTH, 64, 64), "filt_b2": (DEPTH, 64),
    "filt_freq": (DEPTH, 64), "filt_w3": (DEPTH, 64, 2048), "hyena_d": (DEPTH, 2, DH),
    "fnet_w": (DEPTH, 8, 64, 64), "w_out": (DEPTH, D, D), "g_ffn_pre": (DEPTH, D), "g_ffn_post": (DEPTH, D),
    "w_up": (DEPTH, D, 2 * DFF), "dw_w": (DEPTH, 3, 2 * DFF), "dw_b": (DEPTH, 2 * DFF), "w_down": (DEPTH, DFF, D),
}
CONST_SHAPES = {
    "identf": ((128, 128), F32), "W1": ((32, 128), BF16), "W2": ((128, 32), BF16),
    "M2s": ((8, 128, 4096), BF16), "MABs": ((8, 128, 4096), BF16), "G1s": ((8, 128, 2048), BF16),
    "W1f": ((64, 64), BF16), "M2fs": ((128, 8192), BF16), "CSb": ((128, 256), F32),
    "zT": ((33, T), F32), "dsc": ((128, 4), F32), "tb": ((128, T), F32),
}


def make_consts():
    c = {}
    c.update(_hy_consts()); c.update(_fn_consts()); c.update(_filter_consts())
    c["identf"] = np.eye(128, dtype=np.float32)
    return c


class Builder:
    def __init__(self, n_layers=DEPTH, n_seq=NSEQ, phases=None, dbg=()):
        self.n_layers = n_layers; self.n_seq = n_seq
        self.phases = phases
        self.dbg = set(dbg)
        self.nc = bass.Bass("TRN2", target_bir_lowering=False)
        self.I = {}
        nc = self.nc
        self.I["x"] = nc.dram_tensor("x", [NSEQ, T, D], F32, kind="ExternalInput").ap()
        self.I["c"] = nc.dram_tensor("c", [NSEQ, D], F32, kind="ExternalInput").ap()
        for n in WEIGHT_NAMES:
            self.I[n] = nc.dram_tensor(n, list(WEIGHT_SHAPES[n]), F32, kind="ExternalInput").ap()
        for n, (shp, dt) in CONST_SHAPES.items():
            self.I[n] = nc.dram_tensor(n, list(shp), dt, kind="ExternalInput").ap()
        self.out = nc.dram_tensor("out", [NSEQ, T, D], F32, kind="ExternalOutput").ap()
        self.S = {}
        self.ST = {}

    def scratch(self, name, shape, dt):
        kind = "ExternalOutput" if name in self.dbg else "Internal"
        self.S[name] = self.nc.dram_tensor(name, list(shape), dt, kind=kind).ap()
        return self.S[name]

    def uniq(self, name):
        self._uid = getattr(self, "_uid", 0) + 1
        return f"{name}_u{self._uid}"

    def dump(self, name, ap, trk, dt=F32):
        if name not in self.dbg or name in self.S:
            return
        shp = list(ap.shape)
        d = self.nc.dram_tensor(name, shp, dt, kind="ExternalOutput").ap()
        self.S[name] = d
        self.fw.dma(self.fw.sp, d, ap, reads=[trk], sem_on=trk)

    def on(self, ph):
        return self.phases is None or ph in self.phases

    def build(self):
        import contextlib
        nc = self.nc
        with contextlib.ExitStack() as es:
            sems = [es.enter_context(nc.semaphore(f"s{i}")) for i in range(100)]
            fw = self.fw = FW(nc, sems)
            self.es = es
            self.scratch("xres", [NSEQ, D, T], F32)
            self.scratch("hT", [NSEQ, D, T], BF16)
            self.scratch("uv", [12, 128, T], BF16)
            self.scratch("ab", [8, 128, T], BF16)
            self.scratch("z2", [4, 128, T], BF16)
            self.scratch("mixT", [D, T], BF16)
            self.scratch("gT", [DFF, T], BF16)
            self.scratch("sd", [2, 4, 2, 128, T], BF16)
            self.scratch("kab", [2, 4, 2, 128, 64 * 128], BF16)
            for n in self.S:
                self.ST[n] = Trk(n)
            self.xrt = [[Trk(f"xres{s_}_{t_}") for t_ in range(NT)] for s_ in range(NSEQ)]
            self.ps = [es.enter_context(nc.psum_tensor(f"ps{b}", [128, 512], F32)) for b in range(8)]
            self.pst = [Trk(f"ps{b}") for b in range(8)]
            self.psi = 0
            sb = lambda name, shape, dt: es.enter_context(nc.sbuf_tensor(self.uniq(name), list(shape), dt))
            self.identf = sb("identf", [128, 128], F32); self.identb = sb("identb", [128, 128], BF16)
            self.onesb = sb("onesb", [128, 128], BF16)
            self.W1 = sb("W1sb", [32, 128], BF16); self.W2 = sb("W2sb", [128, 32], BF16)
            self.W1f = sb("W1fsb", [64, 64], BF16); self.M2f = sb("M2fsb", [128, 8192], BF16)
            self.CSb = sb("CSbsb", [128, 256], F32); self.dsc = sb("dscsb", [128, 4], F32)
            self.V = sb("V", [128, 1280], F32)
            self.modT = sb("modT", [128, DEPTH, 48, NSEQ], F32)
            self.DER = sb("DER", [128, DEPTH, NSEQ, 4, 8], F32)
            self.cT = sb("cT", [128, 8, NSEQ], F32)
            self.ctrk = Trk("consts")
            self.rr = 0
            self.marks = []
            self.setup()
            fw.barrier(); fw.recycle_dma()
            for s in range(self.n_seq):
                if self.on("x0"):
                    self.phase_x0(s)
                    fw.barrier(); fw.recycle_dma()
            for l in range(self.n_layers):
                if self.on("filt"):
                    self.marks.append((f"filt_l{l}", fw.npe))
                    self.phase_filter(l)
                    fw.barrier(); fw.recycle_dma()
                for s in range(self.n_seq):
                    for ph in ("p1", "conv", "fnet", "p4", "p5", "p6"):
                        if self.on(ph):
                            self.marks.append((f"{ph}_l{l}s{s}", fw.npe))
                            getattr(self, "phase_" + ph)(l, s)
                            fw.barrier(); fw.recycle_dma()
            fw.finish()
        return nc

    def next_ps(self):
        b = self.psi; self.psi = (self.psi + 1) % 8
        return self.ps[b], self.pst[b]

    def rot(self, engs):
        self.rr += 1
        return engs[self.rr % len(engs)]

    def copy(self, eng, out, in_, reads, writes):
        fw = self.fw
        if eng is fw.act:
            return fw.op(eng, lambda: eng.e.copy(out=out, in_=in_), reads, writes)
        return fw.op(eng, lambda: eng.e.tensor_copy(out=out, in_=in_), reads, writes)

    def setup(self):
        nc, fw, I = self.nc, self.fw, self.I
        sp = fw.sp
        ct = self.ctrk
        for dst, src in ((self.identf, "identf"), (self.W1, "W1"), (self.W2, "W2"), (self.W1f, "W1f"),
                         (self.M2f, "M2fs"), (self.CSb, "CSb"), (self.dsc, "dsc")):
            fw.dma(sp, dst[:], I[src], writes=[ct], sem_on=ct, accumulate_w=True)
        fw.op(fw.dve, lambda: nc.vector.tensor_copy(out=self.identb[:], in_=self.identf[:]), [ct], [ct])
        fw.op(fw.dve, lambda: nc.vector.memset(self.onesb[:], 1.0 / D), [], [ct])
        self.col = {}
        rows = []
        def reg(name, l, ap2d, n, width=128):
            self.col[(name, l)] = len(rows)
            for k in range(n):
                rows.append((ap2d, k, width))
        for l in range(self.n_layers):
            for nm in ("g_mix_pre", "g_mix_post", "g_ffn_pre", "g_ffn_post"):
                reg(nm, l, I[nm][l].rearrange("(c p) -> c p", p=128), 8)
            reg("ada_b", l, I["ada_b"][l].rearrange("(c p) -> c p", p=128), 48)
            reg("short_w", l, I["short_w"][l].rearrange("k (q p) -> (k q) p", p=128), 36)
            reg("short_b", l, I["short_b"][l].rearrange("(c p) -> c p", p=128), 12)
            reg("dw_w", l, I["dw_w"][l].rearrange("k (q p) -> (k q) p", p=128), 132)
            reg("dw_b", l, I["dw_b"][l].rearrange("(c p) -> c p", p=128), 44)
            reg("hyena_d", l, I["hyena_d"][l].rearrange("n (c p) -> (n c) p", p=128), 8)
            for nm in ("filt_b1", "filt_b2", "filt_freq"):
                reg(nm, l, I[nm][l].rearrange("(o p) -> o p", p=64), 1, 64)
        nrows = len(rows)
        ngrp = (nrows + 127) // 128
        assert ngrp * 128 <= 1280
        with nc.sbuf_tensor("stg_v", [128, ngrp, 128], F32) as stg, nc.sbuf_tensor("ctile_v", [NSEQ, D], F32) as ctile:
            st = Trk("stg")
            fw.op(fw.dve, lambda: nc.vector.memset(stg[:], 0.0), [], [st])
            r = 0
            while r < nrows:
                ap2d, k0, width = rows[r]
                n = 1
                while (r + n < nrows and rows[r + n][0] is ap2d and rows[r + n][1] == k0 + n
                       and (r + n) % 128 != 0):
                    n += 1
                fw.dma(sp, stg[r % 128:r % 128 + n, r // 128, 0:width], ap2d[k0:k0 + n, :],
                       writes=[st], sem_on=st, accumulate_w=True)
                r += n
            vt = Trk("V")
            for g in range(ngrp):
                ps, pt = self.next_ps()
                fw.op(fw.pe, lambda: nc.tensor.transpose(ps[:, 0:128], stg[:, g, :], self.identf[:]), [st, ct], [pt])
                self.copy(fw.dve, self.V[:, g * 128:(g + 1) * 128], ps[:, 0:128], [pt], [vt])
            self.vt = vt
            ctt = Trk("ctile")
            fw.dma(sp, ctile[:], I["c"], writes=[ctt], sem_on=ctt)
            ps, pt = self.next_ps()
            for kc in range(8):
                fw.op(fw.pe, lambda: nc.tensor.transpose(ps[:, kc * NSEQ:(kc + 1) * NSEQ],
                                                         ctile[0:NSEQ, kc * 128:(kc + 1) * 128],
                                                         self.identf[0:NSEQ, 0:NSEQ]), [ctt, ct], [pt], signal=(kc == 7))
            cact = Trk("cT")
            fw.op(fw.act, lambda: nc.scalar.activation(out=self.cT[:].rearrange("p k s -> p (k s)"),
                                                       in_=ps[:, 0:8 * NSEQ], func=AF.Silu), [pt], [cact])
            with nc.sbuf_tensor("aw0", [128, 8, 512], F32) as aw0, nc.sbuf_tensor("aw1", [128, 8, 512], F32) as aw1:
                aws = [(aw0, Trk("aw0")), (aw1, Trk("aw1"))]
                mt = Trk("modT")
                self.mt = mt
                k = 0
                for l in range(self.n_layers):
                    psm, ptm = self.next_ps()
                    for j in range(12):
                        aw, awt = aws[k % 2]; k += 1
                        fw.dma(sp, aw[:], I["ada_w"][l][:, j * 512:(j + 1) * 512].rearrange("(kc p) n -> p kc n", p=128),
                               writes=[awt], sem_on=awt)
                        for sub in range(4):
                            ch = j * 4 + sub
                            for kc in range(8):
                                fw.op(fw.pe, lambda: nc.tensor.matmul(psm[:, ch * NSEQ:(ch + 1) * NSEQ],
                                                                      lhsT=aw[:, kc, sub * 128:(sub + 1) * 128],
                                                                      rhs=self.cT[:, kc, :], start=(kc == 0), stop=(kc == 7)),
                                      [awt, cact], [ptm], signal=(kc == 7))
                    ab0 = self.col[("ada_b", l)]
                    for s in range(NSEQ):
                        fw.op(fw.dve, lambda: nc.vector.tensor_tensor(
                            out=self.modT[:, l, :, s], in0=psm[:, 0:48 * NSEQ].rearrange("p (c s) -> p c s", s=NSEQ)[:, :, s],
                            in1=self.V[:, ab0:ab0 + 48], op=ALU.add), [ptm, vt], [mt])
                    for s in range(NSEQ):
                        for w, (gname, sc_ch, ga_ch) in enumerate((("g_mix_pre", 8, None), ("g_mix_post", None, 16),
                                                                   ("g_ffn_pre", 32, None), ("g_ffn_post", None, 40))):
                            gc = self.col[(gname, l)]
                            if sc_ch is not None:
                                fw.op(fw.dve, lambda: nc.vector.scalar_tensor_tensor(
                                    out=self.DER[:, l, s, w, :], in0=self.modT[:, l, sc_ch:sc_ch + 8, s], scalar=1.0,
                                    in1=self.V[:, gc:gc + 8], op0=ALU.add, op1=ALU.mult), [mt, vt], [mt])
                            else:
                                fw.op(fw.dve, lambda: nc.vector.tensor_tensor(
                                    out=self.DER[:, l, s, w, :], in0=self.modT[:, l, ga_ch:ga_ch + 8, s],
                                    in1=self.V[:, gc:gc + 8], op=ALU.mult), [mt, vt], [mt])
                fw.barrier()

    def gm(self, l, s, which, kc):
        return self.DER[:, l, s, 2 * which, kc:kc + 1]

    def gg(self, l, s, which, kc):
        return self.DER[:, l, s, 2 * which + 1, kc:kc + 1]

    def sh(self, l, s, which, kc):
        return self.modT[:, l, (0 if which == 0 else 24) + kc, s:s + 1]

    def alloc_norm_tiles(self, es, with_tmp=True):
        nc = self.nc
        sb = lambda name, shape, dt: es.enter_context(nc.sbuf_tensor(self.uniq(name), list(shape), dt))
        n = {}
        n["sq"] = (sb("n_sq", [128, 8, TT], BF16), Trk("sq"))
        n["rs"] = (sb("n_rs", [128, TT], F32), Trk("rs"))
        if with_tmp:
            n["tmp"] = (sb("n_tmp", [128, 8, TT], F32), [Trk(f"tmp{i}") for i in range(8)])
        n["ho"] = (sb("n_ho", [128, 8, TT], BF16), Trk("ho"))
        return n

    def sq_of(self, n, src, srct):
        nc, fw = self.nc, self.fw
        sq, sqt = n["sq"]
        fw.op(fw.act, lambda: nc.scalar.activation(out=sq[:].rearrange("p k t -> p (k t)"),
                                                   in_=src.rearrange("p k t -> p (k t)"), func=AF.Square),
              list(srct) if isinstance(srct, (list, tuple)) else [srct], [sqt])

    def rs_from_sq(self, n):
        nc, fw = self.nc, self.fw
        sq, sqt = n["sq"]; rs, rst = n["rs"]
        ps, pt = self.next_ps()
        for kc in range(8):
            fw.op(fw.pe, lambda: nc.tensor.matmul(ps[:, :], lhsT=self.onesb[:], rhs=sq[:, kc, :],
                                                  start=(kc == 0), stop=(kc == 7)), [sqt, self.ctrk], [pt], signal=(kc == 7))
        fw.op(fw.act, lambda: nc.scalar.activation(out=rs[:], in_=ps[:, :], func=AF.Sqrt, bias=EPS, scale=1.0),
              [pt], [rst])
        fw.op(fw.dve, lambda: nc.vector.reciprocal(out=rs[:], in_=rs[:]), [rst], [rst])

    def rstd(self, n, src, srct):
        self.sq_of(n, src, srct)
        self.rs_from_sq(n)

    def prenorm(self, n, src, srct, l, s, which, t0):
        nc, fw = self.nc, self.fw
        self.rstd(n, src, srct)
        self.prenorm_tail(n, src, srct, l, s, which, t0)

    def prenorm_tail(self, n, src, srct, l, s, which, t0):
        nc, fw = self.nc, self.fw
        rs, rst = n["rs"]; tmp, tmpt = n["tmp"]; ho, hot = n["ho"]
        for kc in range(8):
            eng = fw.dve
            fw.op(eng, lambda: eng.e.tensor_tensor(out=tmp[:, kc, :], in0=src[:, kc, :], in1=rs[:], op=ALU.mult),
                  [srct, rst], [tmpt[kc]])
        for kc in range(8):
            fw.op(fw.act, lambda: nc.scalar.activation(out=ho[:, kc, :], in_=tmp[:, kc, :], func=AF.Identity,
                                                       bias=self.sh(l, s, which, kc), scale=self.gm(l, s, which, kc)),
                  [tmpt[kc], self.mt], [hot])
        self.dump("d_tmp", tmp[:, 0, :], tmpt[0])
        self.dump("d_mod", self.modT[:, 0, :, :].rearrange("p a b -> p (a b)"), self.mt)
        self.dump("d_der", self.DER[:, 0, 0, :, :].rearrange("p a b -> p (a b)"), self.mt)
        self.dump("d_V", self.V[:, :], self.vt)
        fw.dma(fw.sp, self.S["hT"][s].rearrange("(kc p) t -> p kc t", p=128)[:, :, t0:t0 + TT], ho[:],
               reads=[hot], writes=[self.ST["hT"]], sem_on=hot, accumulate_w=True, defer=True)

    def epi_b(self, n, yt, ytt, xt, xtt, xn, xnt, l, s, which, t0, last):
        nc, fw = self.nc, self.fw
        self.rs_from_sq(n)
        rs, rst = n["rs"]
        for oc in range(8):
            fw.op(fw.dve, lambda: nc.vector.scalar_tensor_tensor(out=xn[:, oc, :], in0=yt[:, oc, :],
                                                                 scalar=self.gg(l, s, which, oc), in1=rs[:],
                                                                 op0=ALU.mult, op1=ALU.mult), [ytt[oc], rst, self.mt], [xnt])
            fw.op(fw.dve, lambda: nc.vector.tensor_tensor(out=xn[:, oc, :], in0=xn[:, oc, :], in1=xt[:, oc, :],
                                                          op=ALU.add), [xnt, xtt], [xnt])
        if not last:
            fw.dma(fw.sp, self.S["xres"][s].rearrange("(kc p) t -> p kc t", p=128)[:, :, t0:t0 + TT], xn[:],
                   reads=[xnt], writes=[self.xrt[s][t0 // TT]], sem_on=xnt, accumulate_w=True, defer=True)
            self.sq_of(n, xn[:], xnt)

    def epi_c(self, n, xn, xnt, l, s, which, t0, last):
        nc, fw = self.nc, self.fw
        if not last:
            self.rs_from_sq(n)
            if which == 0:
                self.prenorm_tail(n, xn[:], xnt, l, s, 1, t0)
            else:
                self.prenorm_tail(n, xn[:], xnt, l + 1, s, 0, t0)
        else:
            self.final_out(n, xn, xnt, s, t0)

    def epilogue(self, n, yt, ytt, xt, xtt, xn, xnt, l, s, which, t0, last):
        nc, fw = self.nc, self.fw
        self.rstd(n, yt[:], ytt)
        rs, rst = n["rs"]
        for oc in range(8):
            fw.op(fw.dve, lambda: nc.vector.scalar_tensor_tensor(out=xn[:, oc, :], in0=yt[:, oc, :],
                                                                 scalar=self.gg(l, s, which, oc), in1=rs[:],
                                                                 op0=ALU.mult, op1=ALU.mult), [ytt[oc], rst, self.mt], [xnt])
            fw.op(fw.dve, lambda: nc.vector.tensor_tensor(out=xn[:, oc, :], in0=xn[:, oc, :], in1=xt[:, oc, :],
                                                          op=ALU.add), [xnt, xtt], [xnt])
        if not last:
            fw.dma(fw.sp, self.S["xres"][s].rearrange("(kc p) t -> p kc t", p=128)[:, :, t0:t0 + TT], xn[:],
                   reads=[xnt], writes=[self.xrt[s][t0 // TT]], sem_on=xnt, accumulate_w=True, defer=True)
            if which == 0:
                self.prenorm(n, xn[:], xnt, l, s, 1, t0)
            else:
                self.prenorm(n, xn[:], xnt, l + 1, s, 0, t0)
        else:
            self.final_out(n, xn, xnt, s, t0)

    def final_out(self, n, xn, xnt, s, t0):
        nc, fw = self.nc, self.fw
        xo, xot = n["xo"]
        for sub in range(4):
            for half in range(2):
                ps, pt = self.next_ps()
                for k in range(4):
                    kc = half * 4 + k
                    fw.op(fw.pe, lambda: nc.tensor.transpose(ps[:, k * 128:(k + 1) * 128],
                                                             xn[:, kc, sub * 128:(sub + 1) * 128], self.identf[:]),
                          [xnt, self.ctrk], [pt], signal=(k == 3))
                eng = self.rot([fw.act, fw.dve])
                self.copy(eng, xo[:, sub, half * 512:(half + 1) * 512], ps[:, :], [pt], [xot])
        fw.dma(fw.sp, self.out[s, t0:t0 + TT, :].rearrange("(sub p) d -> p sub d", p=128), xo,
               reads=[xot], writes=[self.xrt[s][t0 // TT]], sem_on=xot, accumulate_w=True, defer=True)

    def phase_x0(self, s):
        import contextlib
        nc, fw, I = self.nc, self.fw, self.I
        with contextlib.ExitStack() as es:
            sb = lambda name, shape, dt: es.enter_context(nc.sbuf_tensor(self.uniq(name), list(shape), dt))
            n = self.alloc_norm_tiles(es)
            xins = [(sb(f"xin{i}", [128, 4, D], F32), Trk(f"xin{i}")) for i in range(2)]
            xts = [(sb(f"xt{i}", [128, 8, TT], F32), Trk(f"xt{i}")) for i in range(2)]
            for ti in range(NT):
                t0 = ti * TT
                xin, xint = xins[ti % 2]; xt, xtt = xts[ti % 2]
                fw.dma(fw.sp, xin[:], I["x"][s, t0:t0 + TT, :].rearrange("(sub p) d -> p sub d", p=128),
                       writes=[xint], sem_on=xint)
                for kc in range(8):
                    ps, pt = self.next_ps()
                    for sub in range(4):
                        fw.op(fw.pe, lambda: nc.tensor.transpose(ps[:, sub * 128:(sub + 1) * 128],
                                                                 xin[:, sub, kc * 128:(kc + 1) * 128], self.identf[:]),
                              [xint, self.ctrk], [pt], signal=(sub == 3))
                    eng = self.rot([fw.act, fw.dve])
                    self.copy(eng, xt[:, kc, :], ps[:, :], [pt], [xtt])
                fw.dma(fw.sp, self.S["xres"][s].rearrange("(kc p) t -> p kc t", p=128)[:, :, t0:t0 + TT], xt[:],
                       reads=[xtt], writes=[self.xrt[s][ti]], sem_on=xtt, accumulate_w=True, defer=True)
                self.prenorm(n, xt[:], xtt, 0, s, 0, t0)

    def load_cast(self, dst, dstt, src_ap, stg_list, k):
        nc, fw = self.nc, self.fw
        stg, stgt = stg_list[k % len(stg_list)]
        shp = list(src_ap.shape)
        view = stg
        fw.dma(fw.sp, view, src_ap, writes=[stgt], sem_on=stgt)
        eng = self.rot([fw.dve, fw.pool, fw.act])
        self.copy(eng, dst, view, [stgt], [dstt])

    def phase_p1(self, l, s):
        import contextlib
        nc, fw, I = self.nc, self.fw, self.I
        with contextlib.ExitStack() as es:
            sb = lambda name, shape, dt: es.enter_context(nc.sbuf_tensor(self.uniq(name), list(shape), dt))
            hfull = sb("hfull", [128, 8, T], BF16); hfts = [Trk(f"hfull{i}") for i in range(8)]
            win = sb("win", [128, 8, 2048], BF16); wints = [Trk(f"win{i}") for i in range(8)]
            stgs = [(sb(f"wstg{i}", [128, 2048], F32), Trk(f"wstg{i}")) for i in range(2)]
            raw = sb("raw", [128, T], F32); rawt = [Trk(f"raw{i}") for i in range(NT)]
            acc = sb("acc", [128, T], F32); acct = Trk("acc")
            cbs = [(sb(f"cb{i}", [128, T], BF16), Trk(f"cb{i}")) for i in range(2)]
            rbf = sb("rbf", [128, T], BF16); rbft = [Trk(f"rbf{i}") for i in range(NT)]
            wgb = sb("wgb", [128, 128], F32); wgbt = Trk("wgb")
            mab = sb("mab", [128, 4, 256], BF16); mabt = Trk("mab")
            for kc in range(8):
                fw.dma(fw.sp, hfull[:, kc, :], self.S["hT"][s][kc * 128:(kc + 1) * 128, :], reads=[self.ST["hT"]],
                       writes=[hfts[kc]], sem_on=hfts[kc])
                self.load_cast(win[:, kc, :], wints[kc], I["w_in"][l][kc * 128:(kc + 1) * 128, :],
                               [(stgs[0][0][:], stgs[0][1]), (stgs[1][0][:], stgs[1][1])], kc)
            for cc in range(4):
                fw.op(fw.dve, lambda: nc.vector.memset(wgb[:], 0.0), [], [wgbt])
                fw.dma(fw.sp, wgb[0:64, 0:64], I["fnet_w"][l, 2 * cc], writes=[wgbt], sem_on=wgbt)
                fw.dma(fw.sp, wgb[64:128, 64:128], I["fnet_w"][l, 2 * cc + 1], writes=[wgbt], sem_on=wgbt, accumulate_w=True)
                ps, pt = self.next_ps()
                fw.op(fw.pe, lambda: nc.tensor.matmul(ps[:, 0:128], lhsT=self.CSb[:, 0:128], rhs=wgb[:], start=True, stop=True),
                      [wgbt, self.ctrk], [pt], signal=False)
                fw.op(fw.pe, lambda: nc.tensor.matmul(ps[:, 128:256], lhsT=self.CSb[:, 128:256], rhs=wgb[:], start=True, stop=True),
                      [wgbt, self.ctrk], [pt])
                self.copy(fw.dve, mab[:, cc, :], ps[:, 0:256], [pt], [mabt])
            sw0 = self.col[("short_w", l)]; sb0 = self.col[("short_b", l)]
            V = self.V
            for q in range(16):
                for ti in range(NT):
                    t0 = ti * TT
                    ps, pt = self.next_ps()
                    for kc in range(8):
                        fw.op(fw.pe, lambda: nc.tensor.matmul(ps[:, :], lhsT=win[:, kc, q * 128:(q + 1) * 128],
                                                              rhs=hfull[:, kc, t0:t0 + TT], start=(kc == 0), stop=(kc == 7)),
                              [wints[kc], hfts[kc]], [pt], signal=(kc == 7))
                    if q < 12:
                        self.copy(fw.act, raw[:, t0:t0 + TT], ps[:, :], [pt], [rawt[ti]])
                    else:
                        self.copy(fw.act, rbf[:, t0:t0 + TT], ps[:, :], [pt], [rbft[ti]])
                if q < 12:
                    cb, cbt = cbs[q % 2]
                    w0 = V[:, sw0 + q:sw0 + q + 1]; w1 = V[:, sw0 + 12 + q:sw0 + 12 + q + 1]
                    w2 = V[:, sw0 + 24 + q:sw0 + 24 + q + 1]; bb = V[:, sb0 + q:sb0 + q + 1]
                    fw.op(fw.act, lambda: nc.scalar.activation(out=acc[:], in_=raw[:], func=AF.Identity, bias=bb, scale=w1),
                          rawt + [self.vt], [acct])
                    fw.op(fw.dve, lambda: nc.vector.scalar_tensor_tensor(out=acc[:, 1:T], in0=raw[:, 0:T - 1], scalar=w0,
                                                                         in1=acc[:, 1:T], op0=ALU.mult, op1=ALU.add),
                          rawt + [acct], [acct])
                    fw.op(fw.dve, lambda: nc.vector.scalar_tensor_tensor(out=cb[:, 0:T - 1], in0=raw[:, 1:T], scalar=w2,
                                                                         in1=acc[:, 0:T - 1], op0=ALU.mult, op1=ALU.add),
                          rawt + [acct], [cbt])
                    fw.op(fw.dve, lambda: nc.vector.tensor_copy(out=cb[:, T - 1:T], in_=acc[:, T - 1:T]), [acct], [cbt])
                    self.dump("d_raw", raw[:, 0:512], rawt[0])
                    self.dump("d_acc", acc[:, 0:512], acct)
                    self.dump("d_cb", cb[:, 0:512], cbt, BF16)
                    fw.dma(fw.sp, self.S["uv"][q], cb[:], reads=[cbt], writes=[self.ST["uv"]], sem_on=cbt, accumulate_w=True, defer=True)
                else:
                    cc = q - 12
                    for half in range(2):
                        cb, cbt = cbs[half]
                        for ti in range(NT):
                            t0 = ti * TT
                            ps, pt = self.next_ps()
                            fw.op(fw.pe, lambda: nc.tensor.matmul(ps[:, :], lhsT=mab[:, cc, half * 128:(half + 1) * 128],
                                                                  rhs=rbf[:, t0:t0 + TT], start=True, stop=True),
                                  [mabt, rbft[ti]], [pt])
                            eng = self.rot([fw.act, fw.dve])
                            self.copy(eng, cb[:, t0:t0 + TT], ps[:, :], [pt], [cbt])
                        fw.dma(fw.sp, self.S["ab"][half * 4 + cc], cb[:], reads=[cbt], writes=[self.ST["ab"]],
                               sem_on=cbt, accumulate_w=True, defer=True)

    def proj_epilogue_phase(self, l, s, which, src_name, nk, w_ap, last):
        import contextlib
        nc, fw, I = self.nc, self.fw, self.I
        with contextlib.ExitStack() as es:
            sb = lambda name, shape, dt: es.enter_context(nc.sbuf_tensor(self.uniq(name), list(shape), dt))
            n = self.alloc_norm_tiles(es, with_tmp=False)
            wsb = sb("wsb", [128, nk, D], BF16); wts = [Trk(f"wsb{i}") for i in range(nk)]
            stgs = [(sb(f"wstg{i}", [128, 1024], F32), Trk(f"wstg{i}")) for i in range(2)]
            ins = [(sb(f"pin{i}", [128, nk, TT], BF16), Trk(f"pin{i}")) for i in range(2)]
            xt = sb("xt", [128, 8, TT], F32); xtt = Trk("xt")
            yts = [(sb(f"yt{b_}", [128, 8, TT], F32), [Trk(f"yt{b_}_{i}") for i in range(8)]) for b_ in range(2)]
            xn = sb("xn", [128, 8, TT], F32); xnt = Trk("xn")
            if last:
                n["xo"] = (xt[:].rearrange("p k t -> p (k t)").rearrange("p (a d) -> p a d", a=4), xtt)
            for k1 in range(nk):
                stg, stgt = stgs[k1 % 2]
                fw.dma(fw.sp, stg[:], w_ap[k1 * 128:(k1 + 1) * 128, :], writes=[stgt], sem_on=stgt)
                eng = self.rot([fw.dve, fw.pool, fw.act])
                self.copy(eng, wsb[:, k1, :], stg[:], [stgt], [wts[k1]])
            src = self.S[src_name]

            def load_pin(ti):
                pin, pint = ins[ti % 2]
                fw.dma(fw.sp, pin[:], src.rearrange("(kc p) t -> p kc t", p=128)[:, :, ti * TT:(ti + 1) * TT],
                       reads=[self.ST[src_name]], writes=[pint], sem_on=pint)

            def mm(ti, ocs):
                pin, pint = ins[ti % 2]
                yt, ytt = yts[ti % 2]
                for oc in ocs:
                    ps, pt = self.next_ps()
                    for kc in range(nk):
                        fw.op(fw.pe, lambda: nc.tensor.matmul(ps[:, :], lhsT=wsb[:, kc, oc * 128:(oc + 1) * 128],
                                                              rhs=pin[:, kc, :], start=(kc == 0), stop=(kc == nk - 1)),
                              [wts[kc], pint], [pt], signal=(kc == nk - 1))
                    eng = self.rot([fw.act, fw.dve])
                    self.copy(eng, yt[:, oc, :], ps[:, :], [pt], [ytt[oc]])

            def mm_first():
                pin, pint = ins[0]
                yt, ytt = yts[0]
                banks = [self.next_ps() for _ in range(8)]
                for kc in range(nk):
                    for oc in range(8):
                        ps, pt = banks[oc]
                        fw.op(fw.pe, lambda: nc.tensor.matmul(ps[:, :], lhsT=wsb[:, kc, oc * 128:(oc + 1) * 128],
                                                              rhs=pin[:, kc, :], start=(kc == 0), stop=(kc == nk - 1)),
                              [wts[kc], pint], [pt], signal=(kc == nk - 1))
                for oc in range(8):
                    ps, pt = banks[oc]
                    eng = self.rot([fw.act, fw.dve])
                    self.copy(eng, yt[:, oc, :], ps[:, :], [pt], [ytt[oc]])

            load_pin(0)
            mm_first()
            for ti in range(NT):
                t0 = ti * TT
                nxt = ti + 1 < NT
                yt, ytt = yts[ti % 2]
                n["tmp"] = (yt, ytt)
                if nxt:
                    load_pin(ti + 1)
                fw.dma(fw.sp, xt[:], self.S["xres"][s].rearrange("(kc p) t -> p kc t", p=128)[:, :, t0:t0 + TT],
                       reads=[self.xrt[s][ti]], writes=[xtt], sem_on=xtt)
                self.sq_of(n, yt[:], ytt)
                if nxt:
                    mm(ti + 1, [0, 1])
                self.epi_b(n, yt, ytt, xt, xtt, xn, xnt, l, s, which, t0, last)
                if nxt:
                    mm(ti + 1, [2, 3, 4, 5])
                self.epi_c(n, xn, xnt, l, s, which, t0, last)
                if nxt:
                    mm(ti + 1, [6, 7])

    def phase_p4(self, l, s):
        self.proj_epilogue_phase(l, s, 0, "mixT", 8, self.I["w_out"][l], False)

    def phase_p6(self, l, s):
        self.proj_epilogue_phase(l, s, 1, "gT", NFF, self.I["w_down"][l], l == self.n_layers - 1)

    def phase_p5(self, l, s):
        import contextlib
        nc, fw, I = self.nc, self.fw, self.I
        V = self.V
        with contextlib.ExitStack() as es:
            sb = lambda name, shape, dt: es.enter_context(nc.sbuf_tensor(self.uniq(name), list(shape), dt))
            hfull = sb("hfull", [128, 8, T], BF16); hfts = [Trk(f"hfull{i}") for i in range(8)]
            wups = [(sb(f"wup{i}", [128, 2, 8, 128], BF16), Trk(f"wup{i}")) for i in range(2)]
            stgs = [(sb(f"ustg{i}", [128, 8, 128], F32), Trk(f"ustg{i}")) for i in range(2)]
            raws = [(sb(f"raw{h}", [128, T], F32), [Trk(f"raw{h}_{i}") for i in range(NT)]) for h in range(2)]
            accs = [(sb(f"acc{h}", [128, T], F32), Trk(f"acc{h}")) for h in range(2)]
            gos = [(sb(f"go{i}", [128, T], BF16), Trk(f"go{i}")) for i in range(2)]
            for kc in range(8):
                fw.dma(fw.sp, hfull[:, kc, :], self.S["hT"][s][kc * 128:(kc + 1) * 128, :], reads=[self.ST["hT"]],
                       writes=[hfts[kc]], sem_on=hfts[kc])
            dw0 = self.col[("dw_w", l)]; db0 = self.col[("dw_b", l)]
            kst = 0
            acct = [[Trk(f"acc{h}_{i}") for i in range(NT)] for h in range(2)]

            CW = 2

            def conv_s1(j, gi):
                tl = list(range(gi * CW, (gi + 1) * CW))
                t0 = tl[0] * TT; t1 = (tl[-1] + 1) * TT
                for h in range(2):
                    raw, rawt = raws[h]; acc = accs[h][0]
                    q = h * NFF + j
                    w0 = V[:, dw0 + q:dw0 + q + 1]; w1 = V[:, dw0 + 44 + q:dw0 + 44 + q + 1]
                    w2 = V[:, dw0 + 88 + q:dw0 + 88 + q + 1]; bb = V[:, db0 + q:db0 + q + 1]
                    own = [rawt[t] for t in tl]
                    nb = own + ([rawt[tl[0] - 1]] if tl[0] > 0 else []) + ([rawt[tl[-1] + 1]] if tl[-1] < NT - 1 else [])
                    fw.op(fw.act, lambda: nc.scalar.activation(out=acc[:, t0:t1], in_=raw[:, t0:t1], func=AF.Identity,
                                                               bias=bb, scale=w1), own + [self.vt], [acct[h][gi]])
                    a0 = max(t0, 1)
                    fw.op(fw.dve, lambda: nc.vector.scalar_tensor_tensor(out=acc[:, a0:t1], in0=raw[:, a0 - 1:t1 - 1], scalar=w0,
                                                                         in1=acc[:, a0:t1], op0=ALU.mult, op1=ALU.add),
                          nb + [acct[h][gi]], [acct[h][gi]])
                    b1 = min(t1, T - 1)
                    fw.op(fw.dve, lambda: nc.vector.scalar_tensor_tensor(out=acc[:, t0:b1], in0=raw[:, t0 + 1:b1 + 1], scalar=w2,
                                                                         in1=acc[:, t0:b1], op0=ALU.mult, op1=ALU.add),
                          nb + [acct[h][gi]], [acct[h][gi]])

            def conv_s2(j, gi, go, got):
                if gi != NT // CW - 1:
                    return
                a0_, a1_ = accs[0][0], accs[1][0]
                fw.op(fw.act, lambda: nc.scalar.activation(out=a0_[:, :], in_=a0_[:, :], func=AF.Gelu),
                      acct[0], acct[0])
                fw.op(fw.dve, lambda: nc.vector.tensor_tensor(out=go[:, :], in0=a0_[:, :], in1=a1_[:, :], op=ALU.mult),
                      acct[0] + acct[1], [got])

            for j in range(NFF):
                wup, wupt = wups[j % 2]
                go, got = gos[j % 2]
                for h in range(2):
                    c0 = h * DFF + j * 128
                    fw.dma(fw.pool, wup[:, h, :, :], I["w_up"][l][:, c0:c0 + 128].rearrange("(kc p) n -> p kc n", p=128),
                           writes=[wupt], sem_on=wupt, accumulate_w=(h == 1))
                for ti in range(NT):
                    t0 = ti * TT
                    for h in range(2):
                        raw, rawt = raws[h]
                        ps, pt = self.next_ps()
                        for kc in range(8):
                            fw.op(fw.pe, lambda: nc.tensor.matmul(ps[:, :], lhsT=wup[:, h, kc, :], rhs=hfull[:, kc, t0:t0 + TT],
                                                                  start=(kc == 0), stop=(kc == 7)), [wupt, hfts[kc]], [pt],
                                  signal=(kc == 7))
                        self.copy(fw.act, raw[:, t0:t0 + TT], ps[:, :], [pt], [rawt[ti]])
                    if ti % CW == CW - 1 and ti >= 2 * CW - 1:
                        g_ = ti // CW - 1
                        conv_s1(j, g_)
                        if g_ >= 1:
                            conv_s2(j, g_ - 1, go, got)
                NG = NT // CW
                conv_s1(j, NG - 1)
                conv_s2(j, NG - 2, go, got)
                conv_s2(j, NG - 1, go, got)
                fw.dma(fw.sp, self.S["gT"][j * 128:(j + 1) * 128, :], go[:], reads=[got], writes=[self.ST["gT"]],
                       sem_on=got, accumulate_w=True, defer=True)

    def fft_s1(self, u, ut, Ap, Apts, nk, Wsb, ncol, after=None):
        nc, fw = self.nc, self.fw
        per = 512 // ncol
        if after is not None:
            fw.order(fw.act, after); fw.order(fw.dve, after)
        for gi, c0 in enumerate(range(0, 128, per)):
            ps, pt = self.next_ps()
            for k in range(per):
                fw.op(fw.pe, lambda: nc.tensor.matmul(ps[:, k * ncol:(k + 1) * ncol], lhsT=u[0:nk, c0 + k, :],
                                                      rhs=Wsb[0:nk, :], start=True, stop=True),
                      [ut, self.ctrk], [pt], signal=(k == per - 1))
            eng = fw.act if gi % 2 == 0 else fw.dve
            self.copy(eng, Ap[:, c0:c0 + per, :].rearrange("p c k -> p (c k)"), ps[:, :], [pt], [Apts[gi]])

    def phase_conv(self, l, s):
        import contextlib
        nc, fw, I = self.nc, self.fw, self.I
        with contextlib.ExitStack() as es:
            sb = lambda name, shape, dt: es.enter_context(nc.sbuf_tensor(self.uniq(name), list(shape), dt))
            u = sb("fu", [32, 128, 128], BF16); ut = Trk("fu")
            Ap = sb("fAp", [128, 128, 128], BF16)
            Apts = [Trk(f"fAp{i}") for i in range(32)]; ApR = Trk("fApR")
            Bpts = [Trk(f"fBp{i}") for i in range(32)]; BpR = Trk("fBpR")
            P = sb("fP", [128, 64, 128], BF16); Pt = Trk("fP")
            Bt = sb("fBt", [128, 128, 128], BF16); Btts = [Trk(f"fBt{i}") for i in range(16)]
            m2s = [(sb(f"fm2_{i}", [128, 4096], BF16), Trk(f"fm2_{i}")) for i in range(2)]
            kas = [(sb(f"fka_{i}", [128, 2, 8, 128], BF16), Trk(f"fka_{i}")) for i in range(2)]
            t1 = sb("ft1", [128, 512], F32); t1t = Trk("ft1")
            t2 = sb("ft2", [128, 512], F32); t2t = Trk("ft2")
            gate = sb("fgate", [128, T], BF16); gatet = Trk("fgate")
            zo = sb("fzo", [128, T], BF16); zot = Trk("fzo")
            kst = 0
            for n in range(2):
                for cc in range(4):
                    if n == 0:
                        src = self.S["uv"][cc]; srct = self.ST["uv"]; g_ap = self.S["uv"][4 + cc]
                        dst = self.S["z2"][cc]; dstt = self.ST["z2"]
                    else:
                        src = self.S["z2"][cc]; srct = self.ST["z2"]; g_ap = self.S["uv"][8 + cc]
                        dst = self.S["mixT"][cc * 128:(cc + 1) * 128, :]; dstt = self.ST["mixT"]
                    fw.dma(fw.sp, u[:], src.rearrange("c (a i) -> a c i", a=32), reads=[srct], writes=[ut], sem_on=ut)
                    fw.dma(fw.sp, gate[:], g_ap, reads=[self.ST["uv"]], writes=[gatet], sem_on=gatet)
                    self.fft_s1(u, ut, Ap, Apts, 32, self.W1, 128, after=BpR)
                    for piece in range(8):
                        m2, m2t = m2s[kst % 2]; ka, kat = kas[kst % 2]; kst += 1
                        fw.dma(fw.sp, m2[:], I["M2s"][piece], writes=[m2t], sem_on=m2t)
                        fw.dma(fw.sp, ka[:].rearrange("p a f c -> p a (f c)"),
                               self.S["kab"][n, cc][:, :, piece * 1024:(piece + 1) * 1024].rearrange("a p n -> p a n"),
                               reads=[self.ST["kab"]], writes=[kat], sem_on=kat)
                        for g in range(2):
                            psx, ptx = self.next_ps(); pss, pts = self.next_ps()
                            for k in range(4):
                                fl = g * 4 + k; flo = piece * 8 + fl
                                for var, (ps, pt) in enumerate(((psx, ptx), (pss, pts))):
                                    for rip in range(2):
                                        o = ((fl * 2 + rip) * 2 + var) * 128
                                        fw.op(fw.pe, lambda: nc.tensor.matmul(ps[:, k * 128:(k + 1) * 128], lhsT=m2[:, o:o + 128],
                                                                              rhs=Ap[:, :, rip * 64 + flo],
                                                                              start=(rip == 0), stop=(rip == 1)),
                                              [m2t, ApR] + Apts, [pt], signal=(k == 3 and rip == 1))
                            kav = ka[:, 0, g * 4:(g + 1) * 4, :].rearrange("p f c -> p (f c)")
                            kbv = ka[:, 1, g * 4:(g + 1) * 4, :].rearrange("p f c -> p (f c)")
                            fw.op(fw.dve, lambda: nc.vector.tensor_tensor(out=t1[:], in0=psx[:, :], in1=kav, op=ALU.mult),
                                  [ptx, kat], [t1t])
                            fw.op(fw.dve, lambda: nc.vector.tensor_tensor(out=t2[:], in0=pss[:, :], in1=kbv, op=ALU.mult),
                                  [pts, kat], [t2t])
                            f0 = piece * 8 + g * 4
                            fw.op(fw.pool, lambda: nc.gpsimd.tensor_tensor(out=P[:, f0:f0 + 4, :].rearrange("p f c -> p (f c)"),
                                                                           in0=t1[:], in1=t2[:], op=ALU.add), [t1t, t2t], [Pt])
                    Bp = Ap[:].rearrange("p a b -> p (a b)").rearrange("p (k c) -> p k c", k=128)
                    fw.order(fw.act, ApR); fw.order(fw.dve, ApR)
                    gi = 0
                    for piece in range(8):
                        m2, m2t = m2s[kst % 2]; kst += 1
                        fw.dma(fw.sp, m2[:, 0:2048], I["G1s"][piece], writes=[m2t], sem_on=m2t)
                        for g in range(2):
                            for ri in range(2):
                                ps, pt = self.next_ps()
                                for k in range(4):
                                    fl = g * 4 + k; flo = piece * 8 + fl
                                    o = (fl * 2 + ri) * 128
                                    fw.op(fw.pe, lambda: nc.tensor.matmul(ps[:, k * 128:(k + 1) * 128], lhsT=m2[:, o:o + 128],
                                                                          rhs=P[:, flo, :], start=True, stop=True),
                                          [m2t, Pt], [pt], signal=(k == 3))
                                col0 = ri * 64 + piece * 8 + g * 4
                                eng = fw.act if gi % 2 == 0 else fw.dve
                                self.copy(eng, Bp[:, col0:col0 + 4, :].rearrange("p k c -> p (k c)"), ps[:, :], [pt], [Bpts[gi]])
                                gi += 1
                    for gi, c0 in enumerate(range(0, 128, 8)):
                        ps, pt = self.next_ps()
                        psb = ps[:, :].bitcast(BF16)
                        for k in range(8):
                            fw.op(fw.pe, lambda: nc.tensor.transpose(psb[:, k * 128:(k + 1) * 128], Bp[:, :, c0 + k], self.identb[:]),
                                  [BpR, self.ctrk] + Bpts, [pt], signal=(k == 7))
                        eng = fw.act if gi % 2 == 0 else fw.dve
                        self.copy(eng, Bt[:, c0:c0 + 8, :].rearrange("p c i -> p (c i)"), psb[:, 0:1024], [pt], [Btts[gi]])
                    zo3 = zo[:].rearrange("p (a i) -> p a i", a=32)
                    g3 = gate[:].rearrange("p (a i) -> p a i", a=32)
                    for i0 in range(0, 128, 16):
                        ps, pt = self.next_ps()
                        for k in range(16):
                            fw.op(fw.pe, lambda: nc.tensor.matmul(ps[:, k * 32:(k + 1) * 32], lhsT=Bt[:, :, i0 + k], rhs=self.W2[:, :],
                                                                  start=True, stop=True), Btts + [self.ctrk], [pt], signal=(k == 15))
                        fw.op(fw.dve, lambda: nc.vector.tensor_tensor(out=zo3[:, :, i0:i0 + 16],
                                                                      in0=ps[:, :].rearrange("p (k a) -> p a k", k=16),
                                                                      in1=g3[:, :, i0:i0 + 16], op=ALU.mult), [pt, gatet], [zot])
                    fw.dma(fw.sp, dst, zo[:], reads=[zot], writes=[dstt], sem_on=zot, accumulate_w=True, defer=True)

    def phase_fnet(self, l, s):
        import contextlib
        nc, fw, I = self.nc, self.fw, self.I
        with contextlib.ExitStack() as es:
            sb = lambda name, shape, dt: es.enter_context(nc.sbuf_tensor(self.uniq(name), list(shape), dt))
            us = [(sb(f"nu{i}", [64, 128, 128], BF16), Trk(f"nu{i}")) for i in range(2)]
            Ap = sb("nAp", [128, 128, 64], BF16); Apts = [Trk(f"nAp{i}") for i in range(16)]
            yfs = [(sb(f"nyf{i}", [128, T], BF16), Trk(f"nyf{i}")) for i in range(2)]
            for cc in range(4):
                u, ut = us[cc % 2]; yf, yft = yfs[cc % 2]
                fw.dma(fw.sp, u[0:32], self.S["ab"][cc].rearrange("c (a i) -> a c i", a=32), reads=[self.ST["ab"]],
                       writes=[ut], sem_on=ut)
                fw.dma(fw.sp, u[32:64], self.S["ab"][4 + cc].rearrange("c (a i) -> a c i", a=32), reads=[self.ST["ab"]],
                       writes=[ut], sem_on=ut, accumulate_w=True)
                self.fft_s1(u, ut, Ap, Apts, 64, self.W1f, 64)
                yf3 = yf[:].rearrange("p (f k) -> p f k", k=32)
                for g in range(8):
                    ps, pt = self.next_ps()
                    for k in range(4):
                        flo = g * 4 + k
                        for rip in range(2):
                            o = (flo * 2 + rip) * 128
                            fw.op(fw.pe, lambda: nc.tensor.matmul(ps[:, k * 128:(k + 1) * 128], lhsT=Ap[:, :, rip * 32 + flo],
                                                                  rhs=self.M2f[:, o:o + 128], start=(rip == 0), stop=(rip == 1)),
                                  Apts + [self.ctrk], [pt], signal=(k == 3 and rip == 1))
                    eng = self.rot([fw.act, fw.dve])
                    self.copy(eng, yf3[:, :, g * 4:(g + 1) * 4].rearrange("p f k -> p k f"),
                              ps[:, :].rearrange("p (k f) -> p k f", k=4), [pt], [yft])
                fw.dma(fw.sp, self.S["mixT"][512 + cc * 128:512 + (cc + 1) * 128, :], yf[:], reads=[yft],
                       writes=[self.ST["mixT"]], sem_on=yft, accumulate_w=True, defer=True)

    def phase_filter(self, l):
        import contextlib
        nc, fw, I = self.nc, self.fw, self.I
        V = self.V
        TWO_PI = float(2.0 * np.pi)
        with contextlib.ExitStack() as es:
            sb = lambda name, shape, dt: es.enter_context(nc.sbuf_tensor(self.uniq(name), list(shape), dt))
            w1 = sb("g_w1", [33, 64], F32); w2 = sb("g_w2", [64, 64], F32); w3 = sb("g_w3", [64, 2048], F32)
            zT = sb("g_zT", [33, T], F32); tb = sb("g_tb", [128, T], F32)
            wt = Trk("g_w")
            h1 = sb("g_h1", [64, T], F32); h1t = Trk("g_h1")
            h2 = sb("g_h2", [64, T], F32); h2t = Trk("g_h2")
            arg = sb("g_arg", [64, TT], F32); argt = Trk("g_arg")
            ki = sb("g_ki", [64, TT], I32); kit = Trk("g_ki")
            kr = sb("g_kr", [64, TT], F32); krt = Trk("g_kr")
            frb = sb("g_frb", [64, 2], F32); frbt = Trk("g_frb")
            dec = sb("g_dec", [128, T], F32); dect = Trk("g_dec")
            kf = sb("g_kf", [128, T], F32); kft = Trk("g_kf")
            kb = sb("g_kb", [128, T], F32); kbt = Trk("g_kb")
            sm = sb("g_sm", [128, 4], F32); smt = Trk("g_sm")
            sbf = sb("g_sbf", [128, T], BF16); sbft = Trk("g_sbf")
            dbf = sb("g_dbf", [128, T], BF16); dbft = Trk("g_dbf")
            for dst, src in ((w1, I["filt_w1"][l]), (w2, I["filt_w2"][l]), (w3, I["filt_w3"][l]), (zT, I["zT"]), (tb, I["tb"])):
                fw.dma(fw.sp, dst[:], src, writes=[wt], sem_on=wt, accumulate_w=True)
            cb1 = self.col[("filt_b1", l)]; cb2 = self.col[("filt_b2", l)]; cfr = self.col[("filt_freq", l)]
            hd0 = self.col[("hyena_d", l)]
            fr = V[0:64, cfr:cfr + 1]
            fw.op(fw.dve, lambda: nc.vector.tensor_tensor(out=frb[:, 0:1], in0=V[0:64, cb1:cb1 + 1], in1=fr, op=ALU.mult),
                  [self.vt], [frbt])
            fw.op(fw.dve, lambda: nc.vector.tensor_tensor(out=frb[:, 1:2], in0=V[0:64, cb2:cb2 + 1], in1=fr, op=ALU.mult),
                  [self.vt], [frbt])
            for layer_i, (wsb, kdim, src, srct, dst, dstt) in enumerate(((w1, 33, zT, wt, h1, h1t), (w2, 64, h1, h1t, h2, h2t))):
                for ti in range(NT):
                    t0 = ti * TT
                    ps, pt = self.next_ps()
                    fw.op(fw.pe, lambda: nc.tensor.matmul(ps[0:64, :], lhsT=wsb[0:kdim, :], rhs=src[0:kdim, t0:t0 + TT],
                                                          start=True, stop=True), [wt, srct], [pt])
                    fw.op(fw.act, lambda: nc.scalar.activation(out=arg[:], in_=ps[0:64, :], func=AF.Identity,
                                                               bias=frb[:, layer_i:layer_i + 1], scale=fr), [pt, frbt, self.vt], [argt])
                    fw.op(fw.dve, lambda: nc.vector.tensor_scalar(out=ki[:], in0=arg[:], scalar1=1.0 / TWO_PI, scalar2=None,
                                                                  op0=ALU.mult), [argt], [kit])
                    fw.op(fw.dve, lambda: nc.vector.tensor_copy(out=kr[:], in_=ki[:]), [kit], [krt])
                    fw.op(fw.dve, lambda: nc.vector.scalar_tensor_tensor(out=kr[:], in0=kr[:], scalar=-TWO_PI, in1=arg[:],
                                                                         op0=ALU.mult, op1=ALU.add), [krt, argt], [krt])
                    fw.op(fw.dve, lambda: nc.vector.tensor_scalar(out=kr[:], in0=kr[:], scalar1=-3.141592, scalar2=3.141592,
                                                                  op0=ALU.max, op1=ALU.min), [krt], [krt])
                    fw.op(fw.act, lambda: nc.scalar.activation(out=dst[:, t0:t0 + TT], in_=kr[:], func=AF.Sin), [krt], [dstt])
            for cc in range(4):
                fw.op(fw.act, lambda: nc.scalar.activation(out=dec[:], in_=tb[:], func=AF.Exp, scale=self.dsc[:, cc:cc + 1]),
                      [wt, self.ctrk], [dect])
                for o in range(2):
                    for dirn, (kt, ktt) in enumerate(((kf, kft), (kb, kbt))):
                        q = o * 8 + dirn * 4 + cc
                        for ti in range(NT):
                            t0 = ti * TT
                            ps, pt = self.next_ps()
                            fw.op(fw.pe, lambda: nc.tensor.matmul(ps[:, :], lhsT=w3[0:64, q * 128:(q + 1) * 128],
                                                                  rhs=h2[0:64, t0:t0 + TT], start=True, stop=True), [wt, h2t], [pt])
                            fw.op(fw.dve, lambda: nc.vector.tensor_tensor(out=kt[:, t0:t0 + TT], in0=ps[:, :],
                                                                          in1=dec[:, t0:t0 + TT], op=ALU.mult), [pt, dect], [ktt])
                    fw.op(fw.dve, lambda: nc.vector.memset(kb[:, 0:1], 0.0), [], [kbt])
                    fw.op(fw.dve, lambda: nc.vector.tensor_reduce(out=sm[:, 0:1], in_=kf[:], axis=mybir.AxisListType.X, op=ALU.add,
                                                                  apply_absolute_value=True), [kft], [smt])
                    fw.op(fw.dve, lambda: nc.vector.tensor_reduce(out=sm[:, 1:2], in_=kb[:], axis=mybir.AxisListType.X, op=ALU.add,
                                                                  apply_absolute_value=True), [kbt], [smt])
                    fw.op(fw.dve, lambda: nc.vector.scalar_tensor_tensor(out=sm[:, 2:3], in0=sm[:, 0:1], scalar=1e-6, in1=sm[:, 1:2],
                                                                         op0=ALU.add, op1=ALU.add), [smt], [smt])
                    fw.op(fw.dve, lambda: nc.vector.reciprocal(out=sm[:, 3:4], in_=sm[:, 2:3]), [smt], [smt])
                    fw.op(fw.act, lambda: nc.scalar.activation(out=kf[:], in_=kf[:], func=AF.Copy, scale=sm[:, 3:4]),
                          [kft, smt], [kft])
                    fw.op(fw.dve, lambda: nc.vector.tensor_scalar(out=kb[:], in0=kb[:], scalar1=sm[:, 3:4], scalar2=None,
                                                                  op0=ALU.mult), [kbt, smt], [kbt])
                    dcol = hd0 + o * 4 + cc
                    fw.op(fw.dve, lambda: nc.vector.tensor_tensor(out=kf[:, 0:1], in0=kf[:, 0:1], in1=V[:, dcol:dcol + 1],
                                                                  op=ALU.add), [kft, self.vt], [kft])
                    fw.op(fw.dve, lambda: nc.vector.tensor_tensor(out=sbf[:], in0=kf[:], in1=kb[:], op=ALU.add),
                          [kft, kbt], [sbft])
                    fw.op(fw.pool, lambda: nc.gpsimd.tensor_tensor(out=dbf[:], in0=kf[:], in1=kb[:], op=ALU.subtract),
                          [kft, kbt], [dbft])
                    fw.dma(fw.sp, self.S["sd"][o, cc, 0], sbf[:], reads=[sbft], writes=[self.ST["sd"]], sem_on=sbft, accumulate_w=True, defer=True)
                    fw.dma(fw.sp, self.S["sd"][o, cc, 1], dbf[:], reads=[dbft], writes=[self.ST["sd"]], sem_on=dbft, accumulate_w=True, defer=True)
        fw.barrier(); fw.recycle_dma()
        with contextlib.ExitStack() as es:
            sb = lambda name, shape, dt: es.enter_context(nc.sbuf_tensor(self.uniq(name), list(shape), dt))
            us = [(sb(f"k_u{i}", [32, 128, 128], BF16), Trk(f"k_u{i}")) for i in range(2)]
            Aps = [(sb(f"k_Ap{i}", [128, 128, 128], BF16), [Trk(f"k_Ap{i}_{j}") for j in range(32)]) for i in range(2)]
            mabs = [(sb(f"k_mab{i}", [128, 4096], BF16), Trk(f"k_mab{i}")) for i in range(2)]
            kos = [(sb(f"k_ko{i}", [128, 2, 8, 128], BF16), Trk(f"k_ko{i}")) for i in range(2)]
            kst = 0
            for o in range(2):
                for cc in range(4):
                    for sd in range(2):
                        fw.dma(fw.sp, us[sd][0][:], self.S["sd"][o, cc, sd].rearrange("c (a i) -> a c i", a=32),
                               reads=[self.ST["sd"]], writes=[us[sd][1]], sem_on=us[sd][1])
                        self.fft_s1(us[sd][0], us[sd][1], Aps[sd][0], Aps[sd][1], 32, self.W1, 128)
                    for piece in range(8):
                        mab, mabt = mabs[kst % 2]; ko, kot = kos[kst % 2]; kst += 1
                        fw.dma(fw.sp, mab[:], I["MABs"][piece], writes=[mabt], sem_on=mabt)
                        for g in range(2):
                            for ab in range(2):
                                Ap, Apt = Aps[ab]
                                ps, pt = self.next_ps()
                                for k in range(4):
                                    fl = g * 4 + k; flo = piece * 8 + fl
                                    for rip in range(2):
                                        off = ((fl * 2 + ab) * 2 + rip) * 128
                                        fw.op(fw.pe, lambda: nc.tensor.matmul(ps[:, k * 128:(k + 1) * 128], lhsT=mab[:, off:off + 128],
                                                                              rhs=Ap[:, :, rip * 64 + flo],
                                                                              start=(rip == 0), stop=(rip == 1)),
                                              [mabt] + Apt, [pt], signal=(k == 3 and rip == 1))
                                eng = self.rot([fw.act, fw.dve])
                                self.copy(eng, ko[:, ab, g * 4:(g + 1) * 4, :].rearrange("p f c -> p (f c)"), ps[:, :], [pt], [kot])
                        fw.dma(fw.sp, self.S["kab"][o, cc][:, :, piece * 1024:(piece + 1) * 1024].rearrange("a p n -> p a n"),
                               ko[:].rearrange("p a f c -> p a (f c)"), reads=[kot], writes=[self.ST["kab"]], sem_on=kot,
                               accumulate_w=True, defer=True)


_CONSTS = None


def kernel(**inputs):
    global _CONSTS
    if _CONSTS is None:
        _CONSTS = make_consts()
    x = np.ascontiguousarray(np.asarray(inputs["x"], dtype=np.float32))
    c = np.ascontiguousarray(np.asarray(inputs["c"], dtype=np.float32))
    weights = {n: np.ascontiguousarray(np.asarray(inputs[n], dtype=np.float32)) for n in WEIGHT_NAMES}
    n_cores = 8
    nc = Builder().build()
    in_maps = []
    for i in range(n_cores):
        m = {"x": x[i * NSEQ:(i + 1) * NSEQ], "c": c[i * NSEQ:(i + 1) * NSEQ]}
        m.update(weights)
        m.update(_CONSTS)
        in_maps.append(m)
    res = run_bass_kernel_spmd(nc, in_maps, core_ids=list(range(n_cores)))
    out = np.concatenate([np.asarray(r["out"], dtype=np.float32) for r in res.results], axis=0)
    return out
```
